# Optimizing a Trainium2 kernel written in Bass

```python
import jax, jax.numpy as jnp
from jax import lax
import numpy as np

D_MODEL = 1024
BATCH = 8
SEQ = 4096
DEPTH = 2

CHUNK = 64
LN_EPS = 1e-5
GMLP_HEADS = 4
GMLP_WIDTH = D_MODEL
GMLP_HEAD_DIM = GMLP_WIDTH // GMLP_HEADS
GMLP_BLOCK = 128
POOL_WINDOWS = (2, 4, 8, 16)
POOL_GROUPS = len(POOL_WINDOWS)
POOL_WIDTH = D_MODEL
POOL_GROUP_DIM = POOL_WIDTH // POOL_GROUPS
EVEN_IN = 3 * GMLP_WIDTH + 2 * POOL_WIDTH
EVEN_MIX = GMLP_WIDTH + POOL_WIDTH
MLA_HEADS = 16
MLA_NOPE = 128
MLA_ROPE = 64
MLA_V = 128
MLA_Q_RANK = 256
MLA_KV_RANK = 128
MLA_WIDTH = MLA_HEADS * MLA_V
ODD_IN = MLA_Q_RANK + MLA_KV_RANK + MLA_ROPE + MLA_WIDTH
ROPE_THETA = 10000.0
Q_BLOCK = 128
ATTN_SCALE = (MLA_NOPE + MLA_ROPE) ** -0.5
DEEPNORM_ALPHA = (2.0 * DEPTH) ** 0.25
DEEPNORM_BETA = (8.0 * DEPTH) ** -0.25
N_EVEN = (DEPTH + 1) // 2
N_ODD = DEPTH // 2

kernel_name = "hybrid_gmlp_pool_mla_deepnorm_adaln"


def layer_norm(x, g, b):
    xf = x.astype(jnp.float32)
    mu = jnp.mean(xf, axis=-1, keepdims=True)
    var = jnp.mean(jnp.square(xf - mu), axis=-1, keepdims=True)
    return ((xf - mu) * lax.rsqrt(var + LN_EPS) * g + b).astype(x.dtype)


def rms_norm(x, g):
    xf = x.astype(jnp.float32)
    ms = jnp.mean(jnp.square(xf), axis=-1, keepdims=True)
    return (xf * lax.rsqrt(ms + LN_EPS) * g).astype(x.dtype)


def rope_cos_sin(positions):
    inv = 1.0 / (ROPE_THETA ** (jnp.arange(0, MLA_ROPE, 2, dtype=jnp.float32) / MLA_ROPE))
    ang = positions.astype(jnp.float32)[..., None] * inv
    return jnp.cos(ang), jnp.sin(ang)


def apply_rope(x, cos, sin):
    half = x.shape[-1] // 2
    x1 = x[..., :half].astype(jnp.float32)
    x2 = x[..., half:].astype(jnp.float32)
    return jnp.concatenate([x1 * cos - x2 * sin, x2 * cos + x1 * sin], axis=-1).astype(x.dtype)


def gmlp_spatial_unit(u, v, norm_g, norm_b, ws, bs):
    B, S, _ = u.shape
    nb = S // GMLP_BLOCK
    v = layer_norm(v.reshape(B, S, GMLP_HEADS, GMLP_HEAD_DIM), norm_g, norm_b)
    v = v.reshape(B, nb, GMLP_BLOCK, GMLP_HEADS, GMLP_HEAD_DIM)
    pos_chunk = jnp.arange(GMLP_BLOCK) // CHUNK
    mask = pos_chunk[None, :] <= pos_chunk[:, None]
    w = jnp.where(mask[None], ws, jnp.zeros_like(ws))
    sv = jnp.einsum('hts,bnshd->bnthd', w, v) + bs.T[:, :, None]
    return u * sv.reshape(B, S, GMLP_WIDTH)


def multiscale_pool(xb, pool_w, pool_b, pool_scale):
    B, S, _ = xb.shape
    xg = xb.reshape(B, S, POOL_GROUPS, POOL_GROUP_DIM).astype(jnp.float32)
    cs = jnp.cumsum(xg, axis=1)
    t = jnp.arange(S)
    means = []
    for g, win in enumerate(POOL_WINDOWS):
        csg = cs[:, :, g]
        lagged = jnp.concatenate([jnp.zeros((B, win, POOL_GROUP_DIM), csg.dtype), csg[:, :S - win]], axis=1)
        cnt = jnp.minimum(t + 1, win).astype(jnp.float32)
        means.append((csg - lagged) / cnt[None, :, None])
    pooled = jnp.stack(means, axis=2) - xg
    y = jnp.einsum('bsgd,gde->bsge', pooled.astype(xb.dtype), pool_w).reshape(B, S, POOL_WIDTH)
    return (y + pool_b) * pool_scale


def even_mixer(h, w_in, gmlp_norm_g, gmlp_norm_b, gmlp_ws, gmlp_bs, pool_w, pool_b, pool_scale, w_out):
    proj = h @ w_in
    u, v, z_a, x_b, z_b = jnp.split(proj, [GMLP_WIDTH, 2 * GMLP_WIDTH, 3 * GMLP_WIDTH,
                                           3 * GMLP_WIDTH + POOL_WIDTH], axis=-1)
    a = gmlp_spatial_unit(u, v, gmlp_norm_g, gmlp_norm_b, gmlp_ws, gmlp_bs) * jax.nn.silu(z_a)
    b = multiscale_pool(x_b, pool_w, pool_b, pool_scale) * jax.nn.silu(z_b)
    return jnp.concatenate([a, b], axis=-1) @ w_out


def mla_mixer(h, positions, w_in, q_norm_g, kv_norm_g, w_uq, w_uk, w_uv, w_out):
    B, S, _ = h.shape
    proj = h @ w_in
    q_c, kv_c, k_r, z = jnp.split(proj, [MLA_Q_RANK, MLA_Q_RANK + MLA_KV_RANK,
                                         MLA_Q_RANK + MLA_KV_RANK + MLA_ROPE], axis=-1)
    q_c = rms_norm(q_c, q_norm_g)
    kv_c = rms_norm(kv_c, kv_norm_g)
    q = jnp.einsum('bsr,rhd->bshd', q_c, w_uq)
    q_nope, q_rope = q[..., :MLA_NOPE], q[..., MLA_NOPE:]
    cos, sin = rope_cos_sin(positions)
    q_rope = apply_rope(q_rope, cos[:, :, None, :], sin[:, :, None, :])
    k_rope = apply_rope(k_r, cos, sin)
    q_lat = jnp.einsum('bshd,rhd->bshr', q_nope, w_uk)
    nb = S // Q_BLOCK
    q_lat_b = q_lat.reshape(B, nb, Q_BLOCK, MLA_HEADS, MLA_KV_RANK).transpose(1, 0, 2, 3, 4)
    q_rope_b = q_rope.reshape(B, nb, Q_BLOCK, MLA_HEADS, MLA_ROPE).transpose(1, 0, 2, 3, 4)
    key_chunk = jnp.arange(S) // CHUNK

    def attend_block(args):
        ql, qr, i = args
        s = jnp.einsum('bqhr,bkr->bhqk', ql, kv_c) + jnp.einsum('bqhd,bkd->bhqk', qr, k_rope)
        s = s.astype(jnp.float32) * ATTN_SCALE
        q_chunk = (i * Q_BLOCK + jnp.arange(Q_BLOCK)) // CHUNK
        mask = key_chunk[None, :] <= q_chunk[:, None]
        p = jax.nn.softmax(jnp.where(mask, s, -jnp.inf), axis=-1).astype(kv_c.dtype)
        return jnp.einsum('bhqk,bkr->bqhr', p, kv_c)

    o_lat = lax.map(attend_block, (q_lat_b, q_rope_b, jnp.arange(nb)))
    o_lat = o_lat.transpose(1, 0, 2, 3, 4).reshape(B, S, MLA_HEADS, MLA_KV_RANK)
    o = jnp.einsum('bshr,rhd->bshd', o_lat, w_uv).reshape(B, S, MLA_WIDTH)
    return (o * jax.nn.silu(z)) @ w_out


def setup_inputs(seed: int = 0) -> dict:
    key = jax.random.key(seed)
    ks = jax.random.split(key, 24)
    f32 = jnp.float32

    def nrm(k, shape, s):
        return s * jax.random.normal(k, shape, f32)

    x = nrm(ks[0], (BATCH, SEQ, D_MODEL), 1.0)
    c = nrm(ks[1], (BATCH, D_MODEL), 1.0)
    offs = jax.random.randint(ks[2], (BATCH, 1), 0, 4096, dtype=jnp.int32)
    positions = offs + jnp.arange(SEQ, dtype=jnp.int32)[None, :]
    ada_w = nrm(ks[3], (DEPTH, D_MODEL, 3 * D_MODEL), 0.1 * D_MODEL ** -0.5)
    ada_b = nrm(ks[4], (DEPTH, 3 * D_MODEL), 0.01)
    ln_g = 1.0 + nrm(ks[5], (DEPTH, D_MODEL), 0.02)
    ln_b = nrm(ks[6], (DEPTH, D_MODEL), 0.02)
    e_w_in = nrm(ks[7], (N_EVEN, D_MODEL, EVEN_IN), D_MODEL ** -0.5)
    gmlp_norm_g = 1.0 + nrm(ks[8], (N_EVEN, GMLP_HEAD_DIM), 0.02)
    gmlp_norm_b = nrm(ks[9], (N_EVEN, GMLP_HEAD_DIM), 0.02)
    gmlp_ws = nrm(ks[10], (N_EVEN, GMLP_HEADS, GMLP_BLOCK, GMLP_BLOCK), 0.5 * GMLP_BLOCK ** -0.5)
    gmlp_bs = 1.0 + nrm(ks[11], (N_EVEN, GMLP_HEADS, GMLP_BLOCK), 0.02)
    pool_w = nrm(ks[12], (N_EVEN, POOL_GROUPS, POOL_GROUP_DIM, POOL_GROUP_DIM), POOL_GROUP_DIM ** -0.5)
    pool_b = nrm(ks[13], (N_EVEN, POOL_WIDTH), 0.01)
    pool_scale = 1.0 + nrm(ks[14], (N_EVEN, POOL_WIDTH), 0.1)
    e_w_out = nrm(ks[15], (N_EVEN, EVEN_MIX, D_MODEL), DEEPNORM_BETA * EVEN_MIX ** -0.5)
    o_w_in = nrm(ks[16], (N_ODD, D_MODEL, ODD_IN), D_MODEL ** -0.5)
    mla_q_norm_g = 1.0 + nrm(ks[17], (N_ODD, MLA_Q_RANK), 0.02)
    mla_kv_norm_g = 1.0 + nrm(ks[18], (N_ODD, MLA_KV_RANK), 0.02)
    mla_w_uq = nrm(ks[19], (N_ODD, MLA_Q_RANK, MLA_HEADS, MLA_NOPE + MLA_ROPE), MLA_Q_RANK ** -0.5)
    mla_w_uk = nrm(ks[20], (N_ODD, MLA_KV_RANK, MLA_HEADS, MLA_NOPE), MLA_KV_RANK ** -0.5)
    mla_w_uv = nrm(ks[21], (N_ODD, MLA_KV_RANK, MLA_HEADS, MLA_V), MLA_KV_RANK ** -0.5)
    o_w_out = nrm(ks[22], (N_ODD, MLA_WIDTH, D_MODEL), DEEPNORM_BETA * MLA_WIDTH ** -0.5)
    return {"x": x, "c": c, "positions": positions, "ada_w": ada_w, "ada_b": ada_b,
            "ln_g": ln_g, "ln_b": ln_b, "e_w_in": e_w_in, "gmlp_norm_g": gmlp_norm_g,
            "gmlp_norm_b": gmlp_norm_b, "gmlp_ws": gmlp_ws, "gmlp_bs": gmlp_bs,
            "pool_w": pool_w, "pool_b": pool_b, "pool_scale": pool_scale, "e_w_out": e_w_out,
            "o_w_in": o_w_in, "mla_q_norm_g": mla_q_norm_g, "mla_kv_norm_g": mla_kv_norm_g,
            "mla_w_uq": mla_w_uq, "mla_w_uk": mla_w_uk, "mla_w_uv": mla_w_uv, "o_w_out": o_w_out}


def reference(x, c, positions, ada_w, ada_b, ln_g, ln_b, e_w_in, gmlp_norm_g, gmlp_norm_b,
              gmlp_ws, gmlp_bs, pool_w, pool_b, pool_scale, e_w_out, o_w_in, mla_q_norm_g,
              mla_kv_norm_g, mla_w_uq, mla_w_uk, mla_w_uv, o_w_out):
    cond = jax.nn.silu(c)
    for l in range(DEPTH):
        mod = cond @ ada_w[l] + ada_b[l]
        shift, scale, gate = jnp.split(mod, 3, axis=-1)
        h = x * (1.0 + scale[:, None, :]) + shift[:, None, :]
        if l % 2 == 0:
            e = l // 2
            y = even_mixer(h, e_w_in[e], gmlp_norm_g[e], gmlp_norm_b[e], gmlp_ws[e], gmlp_bs[e],
                           pool_w[e], pool_b[e], pool_scale[e], e_w_out[e])
        else:
            o = l // 2
            y = mla_mixer(h, positions, o_w_in[o], mla_q_norm_g[o], mla_kv_norm_g[o],
                          mla_w_uq[o], mla_w_uk[o], mla_w_uv[o], o_w_out[o])
        x = layer_norm(DEEPNORM_ALPHA * x + (1.0 + gate[:, None, :]) * y, ln_g[l], ln_b[l])
    return x
```

```python
import numpy as np
import concourse.bass as bass
import concourse.mybir as mybir
from concourse.bass_utils import run_bass_kernel_spmd

F32 = mybir.dt.float32
BF16 = mybir.dt.bfloat16
I32 = mybir.dt.int32
AF = mybir.ActivationFunctionType
ALU = mybir.AluOpType

D = 1024
SEQ = 4096
T = 512
NCORES = 8
ALPHA = float((2.0 * 2) ** 0.25)
LN_EPS = 1e-5
ATTN_SCALE = float(192 ** -0.5)
POOL_WINDOWS = (2, 4, 8, 16)
TWO_PI = float(2 * np.pi)
PI = float(np.pi)
C1 = 6.28125
C2 = TWO_PI - C1

PP_C, PP_ADAB, PP_GNG, PP_GNB, PP_PB, PP_PS, PP_QNG, PP_KVG, PP_INV, PP_PH, PP_N = 0, 8, 40, 42, 44, 52, 60, 62, 63, 64, 65


class Prog:
    ENG = ("pe", "act", "dve", "pool", "sp")

    def __init__(self, nc):
        self.nc = nc
        self.e = {"pe": nc.tensor, "act": nc.scalar, "dve": nc.vector, "pool": nc.gpsimd, "sp": nc.sync}
        self.ops = []

    def op(self, eng, fn, r=(), w=()):
        self.ops.append(dict(eng=eng, fn=fn, r=tuple(r), w=tuple(w), dma=None))

    def dma(self, key, out, in_, r=(), w=(), eng="sp", **kw):
        self.ops.append(dict(eng=eng, fn=lambda E: E.dma_start(out=out, in_=in_, **kw),
                             r=tuple(r), w=tuple(w), dma=key))

    def mm(self, out, lhsT, rhs, start, stop, r, w):
        self.op("pe", lambda E: E.matmul(out, lhsT, rhs, start=start, stop=stop), r, w)

    def tr(self, out, in_, ident, r, w):
        self.op("pe", lambda E: E.transpose(out, in_, ident), r, w)

    def act(self, out, in_, func, r, w, bias=None, scale=None):
        kw = {}
        if bias is not None:
            kw["bias"] = bias
        if scale is not None:
            kw["scale"] = scale
        self.op("act", lambda E: E.activation(out, in_, func, **kw), r, w)

    def tt(self, eng, out, a, b, op, r, w):
        self.op(eng, lambda E: E.tensor_tensor(out, a, b, op), r, w)

    def ts(self, eng, out, a, s1, s2, op0, op1, r, w):
        if s2 is None:
            self.op(eng, lambda E: E.tensor_scalar(out, a, s1, None, op0), r, w)
        else:
            self.op(eng, lambda E: E.tensor_scalar(out, a, s1, s2, op0, op1), r, w)

    def stt(self, eng, out, a, s, b, op0, op1, r, w):
        self.op(eng, lambda E: E.scalar_tensor_tensor(out, a, s, b, op0, op1), r, w)

    def cp(self, eng, out, a, r, w):
        if eng == "act":
            self.op("act", lambda E: E.copy(out, a), r, w)
        else:
            self.op(eng, lambda E: E.tensor_copy(out, a), r, w)

    def ms(self, eng, out, val, w):
        self.op(eng, lambda E: E.memset(out, val), (), w)

    def emit(self):
        nc = self.nc
        ops = self.ops
        last_w, readers, last_dma = {}, {}, {}
        for i, o in enumerate(ops):
            deps = set()
            for res in o["r"]:
                if res in last_w:
                    deps.add(last_w[res])
            for res in o["w"]:
                if res in last_w:
                    deps.add(last_w[res])
                deps.update(readers.get(res, ()))
            if o["dma"] is not None and o["dma"] in last_dma:
                deps.add(last_dma[o["dma"]])
            deps.discard(i)
            o["deps"] = deps
            for res in o["r"]:
                readers.setdefault(res, []).append(i)
            for res in o["w"]:
                last_w[res] = i
                readers[res] = []
            if o["dma"] is not None:
                last_dma[o["dma"]] = i

        def skip(p, o):
            return p["dma"] is None and o["dma"] is None and p["eng"] == "pe" and o["eng"] == "pe"

        for o in ops:
            o["need_inc"] = o["dma"] is not None
        for o in ops:
            latest = {}
            for j in o["deps"]:
                p = ops[j]
                if skip(p, o) or p["dma"] is not None:
                    continue
                latest[p["eng"]] = max(latest.get(p["eng"], -1), j)
            for j in latest.values():
                ops[j]["need_inc"] = True
        sem = {e: nc.alloc_semaphore("s_" + e) for e in self.ENG}
        cnt = {e: 0 for e in self.ENG}
        dsem, dcnt = {}, {}
        for o in ops:
            if o["dma"] is not None:
                k = o["dma"]
                if k not in dsem:
                    dsem[k] = nc.alloc_semaphore("d_%d" % len(dsem))
                    dcnt[k] = 0
                dcnt[k] += 16
                o["sem"], o["val"] = ("d", k), dcnt[k]
            elif o["need_inc"]:
                cnt[o["eng"]] += 1
                o["sem"], o["val"] = ("e", o["eng"]), cnt[o["eng"]]
        known = {e: {} for e in self.ENG}
        nwait = 0
        for o in ops:
            E = self.e[o["eng"]]
            need = {}
            latest = {}
            for j in o["deps"]:
                p = ops[j]
                if skip(p, o):
                    continue
                if p["dma"] is not None:
                    need[p["sem"]] = max(need.get(p["sem"], 0), p["val"])
                else:
                    latest[p["eng"]] = max(latest.get(p["eng"], -1), j)
            for j in latest.values():
                p = ops[j]
                need[p["sem"]] = max(need.get(p["sem"], 0), p["val"])
            kn = known[o["eng"]]
            for s, v in need.items():
                if kn.get(s, 0) >= v:
                    continue
                kn[s] = v
                E.wait_ge(dsem[s[1]] if s[0] == "d" else sem[s[1]], v)
                nwait += 1
            ins = o["fn"](E)
            if o["dma"] is not None:
                ins.then_inc(dsem[o["dma"]], 16)
            elif o["need_inc"]:
                ins.then_inc(sem[o["eng"]], 1)
        for k, h in dsem.items():
            nc.sync.wait_ge(h, dcnt[k])
        self.stats = dict(n_ops=len(ops), n_wait=nwait, cnt=dict(cnt), n_dma_keys=len(dsem))
        return self.stats


def build_program(NT=8, debug_x1=False):
    S = NT * T
    nc = bass.Bass("TRN2", target_bir_lowering=False)
    dt = lambda name, shape, dtype=F32, kind="ExternalInput": nc.dram_tensor(name, list(shape), dtype, kind=kind).ap()
    x_d = dt("x", [S, D])
    pos_d = dt("pos", [1, S], I32)
    pp_d = dt("pp", [128, PP_N])
    rows_d = dt("rows", [1, 256 + 512])
    adaw_d = dt("ada_w", [2, D, 3 * D])
    adabg_d = dt("ada_bg", [2, D])
    lng_d = dt("ln_g", [2, D])
    lnb_d = dt("ln_b", [2, D])
    ewin_d = dt("e_w_in", [D, 5 * D])
    ws_d = dt("gmlp_ws", [4, 128, 128])
    poolw_d = dt("pool_w", [4, 256, 256])
    ewout_d = dt("e_w_out", [2 * D, D])
    owin_d = dt("o_w_in", [D, 2496])
    wuq_d = dt("w_uq", [256, 16 * 192])
    wuk_d = dt("w_uk", [128, 16 * 128])
    wuv_d = dt("w_uv", [128, 16 * 128])
    owout_d = dt("o_w_out", [2 * D, D])
    out_d = dt("out", [S, D], F32, "ExternalOutput")
    x1_d = dt("x1dbg", [S, D], F32, "ExternalOutput") if debug_x1 else None

    P = Prog(nc)
    sb = lambda name, shape, dtype=F32: nc.alloc_sbuf_tensor("sb_" + name, list(shape), dtype)

    xa = sb("xa", [128, 4, D])
    xb = sb("xb", [128, 4, D])
    gate1 = sb("gate1", [128, 2, D])
    lng = sb("lng", [128, 2, D])
    lnb = sb("lnb", [128, 2, D])
    ident = sb("ident", [128, 128])
    onesf = sb("onesf", [128, 128])
    onesb = sb("onesb", [128, 128], BF16)
    identb = sb("identb", [128, 128], BF16)
    xbf = sb("xbf", [128, D], BF16)
    pp = sb("pp", [128, PP_N])
    cond = sb("cond", [128, 8])
    modT = sb("modT", [128, 2, 16])
    hT = sb("hT", [128, 8, T], BF16)
    mixT = sb("mixT", [128, 16, T], BF16)
    NSLOT = 4
    wsl = [sb("wsl%d" % i, [128, 8, 512], BF16) for i in range(NSLOT)]
    wukT = sb("wukT", [128, 16, 128], BF16)
    wuv = sb("wuv", [128, 16, 128], BF16)
    poolw = sb("poolw", [128, 4, 2, 256], BF16)
    WsT = sb("WsT", [128, 4, 128], BF16)
    Cmat = sb("Cmat", [128, 4, 2, 128])
    rcw = sb("rcw", [128, 4, 16])
    halo = sb("halo", [128, 8, 16])
    vn = sb("vn", [128, 4, D], BF16)
    wq = [vn[:, 2 * i:2 * i + 2, :].rearrange("p a (b n) -> p a b n", b=4) for i in range(2)]
    WQ = [[("vn", j, h) for j in (2 * i, 2 * i + 1) for h in range(4)] for i in range(2)]
    NSCR = 8
    scr = [sb("scr%d" % i, [128, 528]) for i in range(NSCR)]
    pl = sb("pl", [128, 2, 2, T], BF16)
    szb = sb("szb", [128, 2, T])
    small = sb("small", [128, 128])
    qcn = sb("qcn", [128, 2, T], BF16)
    kvT = sb("kvT", [128, S], BF16)
    kvtok = sb("kvtok", [128, S // 128, 128], BF16)
    krT = sb("krT", [128, S], BF16)
    cs = sb("cs", [128, T])
    posi = sb("posi", [128, T], I32)
    qn = sb("qn", [128, T], BF16)
    qlat = [sb("qlat%d" % i, [128, T], BF16) for i in range(2)]
    qrope = [sb("qrope%d" % i, [128, T], BF16) for i in range(2)]
    olat = [sb("olat%d" % i, [128, T], BF16) for i in range(2)]
    NPT = 4
    pT = [sb("pT%d" % i, [128, T], BF16) for i in range(NPT)]
    ps = [nc.alloc_psum_tensor("ps%d" % i, [128, 512], F32) for i in range(8)]
    PS = lambda i: ("ps", i)
    state = dict(bank=0, scr=0, slot=0, pt=0, sb=0)

    def nbank(lo=0, hi=8):
        b = state["bank"]
        b = lo + ((b - lo + 1) % (hi - lo)) if lo <= b < hi else lo
        state["bank"] = b
        return b

    def nscr():
        state["scr"] = (state["scr"] + 1) % NSCR
        return state["scr"]

    SC = lambda i: ("scr", i)

    def kview(w2d, c0, ncols, k0=0, nk=8):
        return w2d[k0 * 128:(k0 + nk) * 128, c0:c0 + ncols].rearrange("(kc p) n -> p kc n", p=128)

    wlist = []
    for t in range(NT):
        for g in (2, 3, 0, 4, 1, 5, 6, 8, 7, 9):
            wlist.append(("ewin", g, kview(ewin_d, g * 512, 512), 512))
        for hf in range(2):
            for kh in range(2):
                wlist.append(("ewout", (hf, kh), kview(ewout_d, hf * 512, 512, kh * 8), 512))
        wlist.append(("owinA", 0, kview(owin_d, 0, 448), 448))
        for g in range(4):
            wlist.append(("owinZ", g, kview(owin_d, 448 + g * 512, 512), 512))
        for hf in range(2):
            for kh in range(2):
                wlist.append(("owout", (hf, kh), kview(owout_d, hf * 512, 512, kh * 8), 512))
    wstate = dict(issued=0, used=0, released=0)

    NG = len(wlist) // NT
    wscr = nc.dram_tensor("wscr", [NG, 128, 8 * 512], BF16, kind="Internal").ap()

    def w_pump():
        while wstate["issued"] < len(wlist) and wstate["issued"] - NSLOT < wstate["released"]:
            i = wstate["issued"]
            kind, g, src, ncols = wlist[i]
            s_ = i % NSLOT
            gid = i % NG
            scr_v = wscr[gid].rearrange("p (k n) -> p k n", k=8)[:, :, 0:ncols]
            if i < NG:
                P.dma(("wp", s_), wsl[s_][:, :, 0:ncols], src, w=[("wsl", s_)], eng="pool")
                if NT > 1:
                    P.dma(("wst", s_), scr_v, wsl[s_][:, :, 0:ncols], r=[("wsl", s_)], w=[("wscr", gid)])
            else:
                P.dma(("w", s_), wsl[s_][:, :, 0:ncols], scr_v, r=[("wscr", gid)], w=[("wsl", s_)])
            wstate["issued"] += 1

    def w_next(kind):
        i = wstate["used"]
        assert wlist[i][0] == kind, (wlist[i][0], kind)
        w_pump()
        assert wstate["issued"] > i
        wstate["used"] += 1
        return wsl[i % NSLOT], ("wsl", i % NSLOT)

    def w_release(n=1):
        wstate["released"] += n
        w_pump()

    P.dma("c0", pp[:], pp_d, w=["pp"])
    for l in range(2):
        P.dma(("cg", l), gate1[:, l, :], adabg_d[l:l + 1, :].partition_broadcast(128), w=[("gate1", l)])
        P.dma(("cl", l), lng[:, l, :], lng_d[l:l + 1, :].partition_broadcast(128), w=[("lng", l)])
        P.dma(("cb", l), lnb[:, l, :], lnb_d[l:l + 1, :].partition_broadcast(128), w=[("lnb", l)])
    P.op("pool", lambda E: E.iota(ident[:], [[1, 128]], base=0, channel_multiplier=-1,
                                    allow_small_or_imprecise_dtypes=True), w=["ident"])
    P.op("pool", lambda E: E.tensor_single_scalar(ident[:], ident[:], 0.0, ALU.is_equal), r=["ident"], w=["ident"])
    P.cp("pool", identb[:], ident[:], ["ident"], ["identb"])
    P.ms("pool", onesf[:], 1.0, ["onesf"])
    P.ms("pool", onesb[:], 1.0, ["onesb"])
    P.ms("pool", halo[:], 0.0, [("halo", c) for c in range(8)])
    P.op("pool", lambda E: E.iota(rcw[:, 0, :], [[1, 16]], base=1, channel_multiplier=0,
                                    allow_small_or_imprecise_dtypes=True), w=["rcw"])
    for g in (3, 2, 1):
        P.cp("pool", rcw[:, g, :], rcw[:, 0, :], ["rcw"], ["rcw"])
    for g, win in enumerate(POOL_WINDOWS):
        P.ts("dve", rcw[:, g, :], rcw[:, g, :], float(win), None, ALU.min, None, ["rcw"], ["rcw"])
    P.op("dve", lambda E: E.reciprocal(rcw[:], rcw[:]), ["rcw"], ["rcw"])
    P.act(cond[:], pp[:, PP_C:PP_C + 8], AF.Silu, ["pp"], ["cond"])

    for kc in range(8):
        P.cp("dve", scr[kc][:, 0:128], cond[:, kc:kc + 1].to_broadcast([128, 128]), ["cond"], [SC(kc)])
    stg = [xa[:].rearrange("p j (a n) -> p (j a) n", a=2), xb[:].rearrange("p j (a n) -> p (j a) n", a=2)]
    STG = [[("xa", j) for j in range(4)], [("xb", j) for j in range(4)]]
    sti = 0
    for l in range(2):
        bmod = nbank()
        for blk in range(4):
            s_ = sti % 2
            sti += 1
            P.dma(("stg", s_), stg[s_], kview(adaw_d[l], blk * 512, 512), w=STG[s_])
            for fc in range(4):
                col = blk * 4 + fc
                for kc in range(8):
                    P.mm(ps[bmod][:, col:col + 1], stg[s_][:, kc, fc * 128:(fc + 1) * 128], cond[:, kc:kc + 1],
                         kc == 0, kc == 7, STG[s_] + ["cond"], [PS(bmod)])
        P.tt("dve", modT[:, l, :], ps[bmod][:, 0:16], pp[:, PP_ADAB + 16 * l:PP_ADAB + 16 * l + 16], ALU.add,
             [PS(bmod), "pp"], [("modT", l)])
        P.ts("dve", modT[:, l, 8:16], modT[:, l, 8:16], 1.0, None, ALU.add, None, [("modT", l)], [("modT", l)])
        for hf in range(2):
            s_ = sti % 2
            sti += 1
            bg = nbank()
            P.dma(("stg", s_), stg[s_], kview(adaw_d[l], 2048 + hf * 512, 512), w=STG[s_])
            for kc in range(8):
                P.mm(ps[bg][:], scr[kc][:, 0:128], stg[s_][:, kc, :], kc == 0, kc == 7, STG[s_] + [SC(kc)], [PS(bg)])
            P.stt("dve", gate1[:, l, hf * 512:(hf + 1) * 512], ps[bg][:], 1.0, gate1[:, l, hf * 512:(hf + 1) * 512],
                  ALU.add, ALU.add, [PS(bg), ("gate1", l)], [("gate1", l)])

    wstg = stg[0]
    P.dma(("stg", 0), wstg[:, 0:4, 0:128], ws_d.rearrange("h t s -> t h s"), w=STG[0])
    wsf = stg[1]
    rows = scr[0][0:1, 0:256], scr[1][0:1, 0:512]
    P.dma("c1", rows[0], rows_d[0:1, 0:256], w=[SC(0)])
    P.dma("c2", rows[1], rows_d[0:1, 256:768], w=[SC(1)])
    rsrow = scr[2][0:1, 0:512]
    brs = nbank()
    for h in range(4):
        b_ = nbank()
        P.tr(ps[b_][:, 0:128], wstg[:, h, 0:128], ident[:], STG[0] + ["ident"], [PS(b_)])
        P.cp("dve", wsf[:, h, 0:128], ps[b_][:, 0:128], [PS(b_)], STG[1])
        P.ms("pool", wsf[64:128, h, 0:64], 0.0, STG[1])
        P.cp("act", WsT[:, h, :], wsf[:, h, 0:128], STG[1], ["WsT"])
        P.mm(ps[brs][0:1, h * 128:(h + 1) * 128], onesf[:, 0:1], wsf[:, h, 0:128], True, True,
             STG[1] + ["onesf"], [PS(brs)])
    P.cp("dve", rsrow, ps[brs][0:1, :], [PS(brs)], [SC(2)])
    for h in range(4):
        for dc in range(2):
            b_ = nbank()
            P.mm(ps[b_][:, 0:128], rows[0][0:1, dc * 128:(dc + 1) * 128], rsrow[0:1, h * 128:(h + 1) * 128], True, False,
                 [SC(0), SC(2)], [PS(b_)])
            P.mm(ps[b_][:, 0:128], onesf[0:1, :], rows[1][0:1, h * 128:(h + 1) * 128], False, True,
                 [SC(1), "onesf"], [PS(b_)])
            P.cp("dve", Cmat[:, h, dc, :], ps[b_][:, 0:128], [PS(b_)], ["Cmat"])
    P.dma("cw0", wuv[:], wuv_d.rearrange("r (h d) -> r h d", h=16), w=["wuv"], eng="pool")
    P.dma("cw1", poolw[:], poolw_d.rearrange("g (dc p) e -> p g dc e", p=128), w=["poolw"], eng="pool")
    P.dma(("stg", 1), stg[1][:, 0:4, :], wuk_d.rearrange("r (a n) -> r a n", a=4), w=STG[1])
    for h in range(16):
        b_ = nbank()
        P.tr(ps[b_][:, 0:128], stg[1][:, h // 4, (h % 4) * 128:(h % 4 + 1) * 128], ident[:], STG[1] + ["ident"], [PS(b_)])
        P.cp("dve" if h % 2 else "act", wukT[:, h, :], ps[b_][:, 0:128], [PS(b_)], ["wukT"])

    def make_hT(src, l, src_res):
        order = [(state["bank"] + 1 + i) % 8 for i in range(8)]
        state["bank"] = order[-1]
        for j in range(4):
            P.cp("dve" if j % 2 == 0 else "pool", xbf[:], src[:, j, :], [(src_res, j)], ["xbf"])
            for c in range(8):
                bk = order[c]
                P.mm(ps[bk][:, j * 128:(j + 1) * 128], xbf[:, c * 128:(c + 1) * 128], identb[:], True, True,
                     ["xbf", "identb"], [PS(bk)])
        for c in range(8):
            bk = order[c]
            if c % 2 == 0:
                P.act(hT[:, c, :], ps[bk][:], AF.Identity, [PS(bk), ("modT", l)], [("hT", c)],
                      bias=modT[:, l, c:c + 1], scale=modT[:, l, 8 + c:9 + c])
            else:
                P.ts("dve", hT[:, c, :], ps[bk][:], modT[:, l, 8 + c:9 + c], modT[:, l, c:c + 1], ALU.mult, ALU.add,
                     [PS(bk), ("modT", l)], [("hT", c)])

    HT_ALL = [("hT", c) for c in range(8)]
    MIX_ALL = [("mixT", c) for c in range(16)]

    def out_proj_and_norm(kind, l, res_src, res_name, t):
        for j in range(4):
            P.act(xb[:, j, :], res_src[:, j, :], AF.Identity, [(res_name, j)], [("xb", j)], scale=ALPHA)
        wg = [w_next(kind) for _ in range(4)]
        for j in range(4):
            for hf in range(2):
                b_ = nbank()
                for kc in range(16):
                    wt, rr = wg[2 * hf + (kc // 8)]
                    P.mm(ps[b_][:], mixT[:, kc, j * 128:(j + 1) * 128], wt[:, kc % 8, :], kc == 0, kc == 15,
                         [("mixT", kc), rr], [PS(b_)])
                s_ = nscr()
                P.tt("dve", scr[s_][:, 0:512], ps[b_][:], gate1[:, l, hf * 512:(hf + 1) * 512], ALU.mult,
                     [PS(b_), ("gate1", l)], [SC(s_)])
                P.tt("pool", xb[:, j, hf * 512:(hf + 1) * 512], xb[:, j, hf * 512:(hf + 1) * 512],
                     scr[s_][:, 0:512], ALU.add, [SC(s_), ("xb", j)], [("xb", j)])
            c0 = 64 + 16 * j
            P.op("dve", lambda E, j=j, c0=c0: E.bn_stats(small[:, c0:c0 + 6], xb[:, j, 0:512]), [("xb", j)], [("st", j)])
            P.op("dve", lambda E, j=j, c0=c0: E.bn_stats(small[:, c0 + 6:c0 + 12], xb[:, j, 512:1024]), [("xb", j)], [("st", j)])
            P.op("dve", lambda E, c0=c0: E.bn_aggr(small[:, c0 + 12:c0 + 14], small[:, c0:c0 + 12]), [("st", j)], [("mv", j)])
            P.act(small[:, c0 + 14:c0 + 15], small[:, c0 + 13:c0 + 14], AF.Ln, [("mv", j)], [("rstd", j)], bias=LN_EPS)
            P.act(small[:, c0 + 14:c0 + 15], small[:, c0 + 14:c0 + 15], AF.Exp, [("rstd", j)], [("rstd", j)], scale=-0.5)
            P.stt("dve", small[:, c0 + 15:c0 + 16], small[:, c0 + 12:c0 + 13], -1.0, small[:, c0 + 14:c0 + 15], ALU.mult, ALU.mult,
                  [("mv", j), ("rstd", j)], [("nb", j)])
            P.act(xb[:, j, :], xb[:, j, :], AF.Identity, [("xb", j), ("rstd", j), ("nb", j)], [("xb", j)],
                  bias=small[:, c0 + 15:c0 + 16], scale=small[:, c0 + 14:c0 + 15])
            P.tt("pool", xb[:, j, :], xb[:, j, :], lng[:, l, :], ALU.mult, [("xb", j), ("lng", l)], [("xb", j)])
            P.tt("dve", xb[:, j, :], xb[:, j, :], lnb[:, l, :], ALU.add, [("xb", j), ("lnb", l)], [("xb", j)])
        w_release(4)

    def layer0(t):
        make_hT(xa, 0, "xa")
        for hf in range(2):
            wt, wr = w_next("ewin")
            for j in range(4):
                b_ = nbank()
                for kc in range(8):
                    P.mm(ps[b_][:], hT[:, kc, j * 128:(j + 1) * 128], wt[:, kc, :], kc == 0, kc == 7,
                         [("hT", kc), wr], [PS(b_)])
                st = small[:, 16:28].rearrange("p (a b) -> p a b", a=2)
                mv = small[:, 28:32].rearrange("p (a b) -> p a b", a=2)
                for i in range(2):
                    P.op("dve", lambda E, i=i, b_=b_: E.bn_stats(st[:, i, :], ps[b_][:, i * 256:(i + 1) * 256]),
                         [PS(b_)], ["vst"])
                    P.op("dve", lambda E, i=i: E.bn_aggr(mv[:, i, :], st[:, i, :]), ["vst"], ["vmv"])
                rs_ = small[:, 32:34]
                nb_ = small[:, 34:36]
                P.act(rs_, mv[:, :, 1], AF.Ln, ["vmv"], ["vrs"], bias=LN_EPS)
                P.act(rs_, rs_, AF.Exp, ["vrs"], ["vrs"], scale=-0.5)
                P.stt("dve", nb_, mv[:, :, 0], -1.0, rs_, ALU.mult, ALU.mult, ["vmv", "vrs"], ["vnb"])
                for i in range(2):
                    h = hf * 2 + i
                    P.act(vn[:, j, h * 256:(h + 1) * 256], ps[b_][:, i * 256:(i + 1) * 256], AF.Identity,
                          [PS(b_), "vrs", "vnb"], [("vn", j, h)], bias=small[:, 34 + i:35 + i], scale=small[:, 32 + i:33 + i])
            w_release()
        for hp in range(2):
            wu, ru = w_next("ewin")
            wz, rz = w_next("ewin")
            for hh in range(2):
                h = hp * 2 + hh
                for dc in range(2):
                    cl = hh * 2 + dc
                    bu, bz, bs_ = nbank(), nbank(), nbank()
                    for kc in range(8):
                        P.mm(ps[bz][:], wz[:, kc, cl * 128:(cl + 1) * 128], hT[:, kc, :], kc == 0, kc == 7,
                             [("hT", kc), rz], [PS(bz)])
                    for j in range(4):
                        P.mm(ps[bs_][:, j * 128:(j + 1) * 128], vn[:, j, h * 256 + dc * 128:h * 256 + (dc + 1) * 128],
                             WsT[:, h, :], True, True, [("vn", j, h), "WsT"], [PS(bs_)])
                    for kc in range(8):
                        P.mm(ps[bu][:], wu[:, kc, cl * 128:(cl + 1) * 128], hT[:, kc, :], kc == 0, kc == 7,
                             [("hT", kc), ru], [PS(bu)])
                    s1, s2 = nscr(), nscr()
                    P.act(scr[s1][:, 0:512], ps[bz][:], AF.Silu, [PS(bz)], [SC(s1)])
                    P.stt("dve", scr[s2][:, 0:512].rearrange("p (j t) -> p j t", j=4),
                          ps[bs_][:].rearrange("p (j t) -> p j t", j=4), pp[:, PP_GNG + dc:PP_GNG + dc + 1],
                          Cmat[:, h, dc, :].unsqueeze(1).to_broadcast([128, 4, 128]), ALU.mult, ALU.add,
                          [PS(bs_), "pp", "Cmat"], [SC(s2)])
                    P.tt("pool", scr[s2][:, 0:512], scr[s2][:, 0:512], scr[s1][:, 0:512], ALU.mult,
                         [SC(s1), SC(s2)], [SC(s2)])
                    P.tt("dve", mixT[:, 2 * h + dc, :], ps[bu][:], scr[s2][:, 0:512], ALU.mult,
                         [PS(bu), SC(s2)], [("mixT", 2 * h + dc)])
            w_release(2)
        pw = {}

        def pool_x(g):
            wx, rx = pw["x%d" % (g // 2)]
            gg = g % 2
            win = POOL_WINDOWS[g]
            nstep = int(np.log2(win))
            for dc in range(2):
                cl = gg * 2 + dc
                ch = 2 * g + dc
                bx = nbank()
                for kc in range(8):
                    P.mm(ps[bx][:], wx[:, kc, cl * 128:(cl + 1) * 128], hT[:, kc, :], kc == 0, kc == 7,
                         [("hT", kc), rx], [PS(bx)])
                sx = nscr()
                P.cp("pool", scr[sx][:, 0:16], halo[:, ch, :], [("halo", ch)], [SC(sx)])
                P.cp("act", scr[sx][:, 16:528], ps[bx][:], [PS(bx)], [SC(sx)])
                P.cp("pool", halo[:, ch, :], scr[sx][:, 512:528], [SC(sx)], [("halo", ch)])
                cur = sx
                for k in range(nstep):
                    sh = 1 << k
                    lo = (1 << (k + 1)) - 1
                    nx = nscr()
                    P.tt("pool", scr[nx][:, lo:528], scr[cur][:, lo:528], scr[cur][:, lo - sh:528 - sh], ALU.add,
                         [SC(cur)], [SC(nx)])
                    cur = nx
                P.stt("dve", pl[:, g % 2, dc, :], scr[cur][:, 16:528], 1.0 / win, scr[sx][:, 16:528], ALU.mult, ALU.subtract,
                      [SC(cur), SC(sx)], [("pl", g % 2, dc)])
                if t == 0:
                    s3 = nscr()
                    P.tt("dve", scr[s3][:, 0:16], scr[cur][:, 16:32], rcw[:, g, :], ALU.mult, [SC(cur), "rcw"], [SC(s3)])
                    P.tt("dve", pl[:, g % 2, dc, 0:16], scr[s3][:, 0:16], scr[sx][:, 16:32], ALU.subtract,
                         [SC(s3), SC(sx)], [("pl", g % 2, dc)])

        def pool_zy(g):
            wz, rz = pw["z%d" % (g // 2)]
            gg = g % 2
            for ec in range(2):
                cl = gg * 2 + ec
                ch = 2 * g + ec
                by, bz = nbank(), nbank()
                for kc in range(8):
                    P.mm(ps[bz][:], wz[:, kc, cl * 128:(cl + 1) * 128], hT[:, kc, :], kc == 0, kc == 7,
                         [("hT", kc), rz], [PS(bz)])
                P.act(szb[:, ec, :], ps[bz][:], AF.Silu, [PS(bz)], [("szb", ec)])
                for dc in range(2):
                    P.mm(ps[by][:], poolw[:, g, dc, ec * 128:(ec + 1) * 128], pl[:, g % 2, dc, :], dc == 0, dc == 1,
                         [("pl", g % 2, dc), "poolw"], [PS(by)])
                s2 = nscr()
                P.ts("dve", scr[s2][:, 0:512], ps[by][:], pp[:, PP_PB + ch:PP_PB + ch + 1], pp[:, PP_PS + ch:PP_PS + ch + 1],
                     ALU.add, ALU.mult, [PS(by), "pp"], [SC(s2)])
                P.tt("pool", mixT[:, 8 + ch, :], scr[s2][:, 0:512], szb[:, ec, :], ALU.mult,
                     [("szb", ec), SC(s2)], [("mixT", 8 + ch)])

        pw["x0"] = w_next("ewin")
        pool_x(0)
        pool_x(1)
        w_release()
        pw["z0"] = w_next("ewin")
        pool_zy(0)
        pw["x1"] = w_next("ewin")
        pool_x(2)
        pool_zy(1)
        w_release()
        pool_x(3)
        w_release()
        pw["z1"] = w_next("ewin")
        pool_zy(2)
        pool_zy(3)
        w_release()
        out_proj_and_norm("ewout", 0, xa, "xa", t)
        if debug_x1:
            P.dma("x1o", x1_d[t * T:(t + 1) * T, :].rearrange("(j p) d -> p j d", p=128), xb[:],
                  r=[("xb", j) for j in range(4)])

    def rope_apply(bank, dst, scale):
        s1, s2 = nscr(), nscr()
        P.stt("dve", scr[s1][0:64, 0:512], ps[bank][0:64, :], scale, cs[0:64, :], ALU.mult, ALU.mult,
              [PS(bank), "cs"], [SC(s1)])
        P.stt("dve", scr[s2][0:64, 0:512], ps[bank][64:128, :], scale, cs[64:128, :], ALU.mult, ALU.mult,
              [PS(bank), "cs"], [SC(s2)])
        return s1, s2

    def load_x(t):
        P.dma("xin", xa[:], x_d[t * T:(t + 1) * T, :].rearrange("(j p) d -> p j d", p=128),
              w=[("xa", j) for j in range(4)])

    def layer1(t):
        if t + 1 < NT:
            load_x(t + 1)
        make_hT(xb, 1, "xb")

        def load_wq(g):
            wqi = g % 2
            for cc in range(2):
                P.dma(("wq", wqi, cc), wq[wqi][:, cc, :, 0:192],
                      wuq_d[cc * 128:(cc + 1) * 128, g * 768:(g + 1) * 768].rearrange("p (a n) -> p a n", a=4),
                      w=WQ[wqi], eng="pool")
            P.ts("pool", wq[wqi][:, :, :, 192:224], wq[wqi][:, :, :, 160:192], -1.0, None, ALU.mult, None,
                 WQ[wqi], WQ[wqi])
            P.cp("pool", wq[wqi][:, :, :, 224:256], wq[wqi][:, :, :, 128:160], WQ[wqi], WQ[wqi])

        load_wq(0)
        p_ = "posi"
        posi_ap = posi[:]
        P.dma("pos", posi_ap, pos_d[0:1, t * T:(t + 1) * T].partition_broadcast(128), w=["posi"])
        a_, k_, m_ = nscr(), nscr(), nscr()
        ang, kf, mk = scr[a_][:, 0:512], scr[k_][:, 0:512], scr[m_][:, 0:512]
        ki = posi_ap
        P.cp("dve", ang, posi_ap, ["posi"], [SC(a_)])
        P.ts("dve", ang, ang, pp[:, PP_INV:PP_INV + 1], None, ALU.mult, None, [SC(a_), "pp"], [SC(a_)])

        def wrap():
            P.ts("dve", mk, ang, PI, -TWO_PI, ALU.is_gt, ALU.mult, [SC(a_)], [SC(m_)])
            P.tt("dve", ang, ang, mk, ALU.add, [SC(a_), SC(m_)], [SC(a_)])
            P.ts("dve", mk, ang, -PI, TWO_PI, ALU.is_lt, ALU.mult, [SC(a_)], [SC(m_)])
            P.tt("dve", ang, ang, mk, ALU.add, [SC(a_), SC(m_)], [SC(a_)])

        P.ts("dve", kf, ang, 1.0 / TWO_PI, None, ALU.mult, None, [SC(a_)], [SC(k_)])
        P.cp("dve", ki, kf, [SC(k_), SC(a_)], ["posi"])
        P.cp("dve", kf, ki, ["posi"], [SC(k_)])
        P.stt("dve", ang, kf, -C1, ang, ALU.mult, ALU.add, [SC(k_), SC(a_)], [SC(a_)])
        P.stt("dve", ang, kf, -C2, ang, ALU.mult, ALU.add, [SC(k_), SC(a_)], [SC(a_)])
        wrap()
        P.ts("dve", ang, ang, pp[:, PP_PH:PP_PH + 1], None, ALU.add, None, [SC(a_), "pp"], [SC(a_)])
        wrap()
        P.act(cs[:], ang, AF.Sin, [SC(a_)], ["cs"])

        MB = 7
        HOOKS = 15

        def q_nope(h):
            wqi, hq = (h // 4) % 2, h % 4
            if hq == 0 and h + 4 < 16:
                load_wq(h // 4 + 1)
            for cc in range(2):
                P.mm(ps[MB][:], wq[wqi][:, cc, hq, 0:128], qcn[:, cc, :], cc == 0, cc == 1,
                     WQ[wqi] + [("qcn", cc)], [PS(MB)])
            P.cp("act", qn[:], ps[MB][:], [PS(MB)], ["qn"])

        def q_rope(h):
            wqi, hq, qi = (h // 4) % 2, h % 4, h % 2
            for cc in range(2):
                P.mm(ps[MB][:], wq[wqi][:, cc, hq, 128:256], qcn[:, cc, :], cc == 0, cc == 1,
                     WQ[wqi] + [("qcn", cc)], [PS(MB)])
            s1, s2 = rope_apply(MB, None, ATTN_SCALE)
            P.tt("dve", qrope[qi][0:64, :], scr[s1][0:64, 0:512], scr[s2][0:64, 0:512], ALU.add, [SC(s1), SC(s2)], [("qrope", qi)])
            P.tt("pool", qrope[qi][64:128, :], scr[s1][0:64, 0:512], scr[s2][0:64, 0:512], ALU.add, [SC(s1), SC(s2)], [("qrope", qi)])

        def q_lat(h):
            qi = h % 2
            P.mm(ps[MB][:], wukT[:, h, :], qn[:], True, True, ["wukT", "qn"], [PS(MB)])
            P.act(qlat[qi][:], ps[MB][:], AF.Identity, [PS(MB)], [("qlat", qi)], scale=ATTN_SCALE)

        def o_norm(h):
            qi = h % 2
            bo, bm = 3 + qi, 5 + qi
            rs_ = nscr()
            P.op("dve", lambda E, rs_=rs_, bm=bm: E.reciprocal(scr[rs_][:, 0:512], ps[bm][:]), [PS(bm)], [SC(rs_)])
            P.tt("dve", olat[qi][:], ps[bo][:], scr[rs_][:, 0:512], ALU.mult, [PS(bo), SC(rs_)], [("olat", qi)])

        def o_proj(h):
            qi = h % 2
            P.mm(ps[MB][:], wuv[:, h, :], olat[qi][:], True, True, ["wuv", ("olat", qi)], [PS(MB)])
            P.tt("dve", mixT[:, h, :], ps[MB][:], mixT[:, h, :], ALU.mult, [PS(MB), ("mixT", h)], [("mixT", h)])

        def zgroup(g):
            wz, rz = w_next("owinZ")
            for cl in range(4):
                h = g * 4 + cl
                b_ = 6 + (h % 2)
                for kc in range(8):
                    P.mm(ps[b_][:], wz[:, kc, cl * 128:(cl + 1) * 128], hT[:, kc, :], kc == 0, kc == 7,
                         [("hT", kc), rz], [PS(b_)])
                P.act(mixT[:, h, :], ps[b_][:], AF.Silu, [PS(b_)], [("mixT", h)])
            w_release()

        wa, ra = w_next("owinA")
        P.ts("pool", wa[:, :, 448:480], wa[:, :, 416:448], -1.0, None, ALU.mult, None, [ra], [ra])
        P.cp("pool", wa[:, :, 480:512], wa[:, :, 384:416], [ra], [ra])
        bq = [0, 1]
        bkv, bkr = 2, 3
        for cc in range(2):
            for kc in range(8):
                P.mm(ps[bq[cc]][:], wa[:, kc, cc * 128:(cc + 1) * 128], hT[:, kc, :], kc == 0, kc == 7,
                     [("hT", kc), ra], [PS(bq[cc])])
        for kc in range(8):
            P.mm(ps[bkv][:], wa[:, kc, 256:384], hT[:, kc, :], kc == 0, kc == 7, [("hT", kc), ra], [PS(bkv)])
        for kc in range(8):
            P.mm(ps[bkr][:], wa[:, kc, 384:512], hT[:, kc, :], kc == 0, kc == 7, [("hT", kc), ra], [PS(bkr)])
        w_release()
        for cc in range(2):
            P.act(pT[cc][:], ps[bq[cc]][:], AF.Square, [PS(bq[cc])], [("pT", cc)])
        P.act(pT[2][:], ps[bkv][:], AF.Square, [PS(bkv)], [("pT", 2)])
        zgroup(0)
        bsq, bsk = 4, 5
        for cc in range(2):
            P.mm(ps[bsq][:], onesb[:], pT[cc][:], cc == 0, cc == 1, [("pT", cc), "onesb"], [PS(bsq)])
        P.mm(ps[bsk][:], onesb[:], pT[2][:], True, True, [("pT", 2), "onesb"], [PS(bsk)])
        rq, rk = nscr(), nscr()
        P.act(scr[rq][:, 0:512], ps[bsq][:], AF.Ln, [PS(bsq)], [SC(rq)], bias=LN_EPS, scale=1.0 / 256)
        P.act(scr[rq][:, 0:512], scr[rq][:, 0:512], AF.Exp, [SC(rq)], [SC(rq)], scale=-0.5)
        P.act(scr[rk][:, 0:512], ps[bsk][:], AF.Ln, [PS(bsk)], [SC(rk)], bias=LN_EPS, scale=1.0 / 128)
        P.act(scr[rk][:, 0:512], scr[rk][:, 0:512], AF.Exp, [SC(rk)], [SC(rk)], scale=-0.5)
        for cc in range(2):
            P.stt("dve", qcn[:, cc, :], ps[bq[cc]][:], pp[:, PP_QNG + cc:PP_QNG + cc + 1], scr[rq][:, 0:512],
                  ALU.mult, ALU.mult, [PS(bq[cc]), "pp", SC(rq)], [("qcn", cc)])
        zgroup(1)
        kvf = nscr()
        P.stt("dve", scr[kvf][:, 0:512], ps[bkv][:], pp[:, PP_KVG:PP_KVG + 1], scr[rk][:, 0:512],
              ALU.mult, ALU.mult, [PS(bkv), "pp", SC(rk)], [SC(kvf)])
        P.cp("pool", kvT[:, t * T:(t + 1) * T], scr[kvf][:, 0:512], [SC(kvf)], [("kvT", t)])
        bt = 4
        for j in range(4):
            P.mm(ps[bt][:, j * 128:(j + 1) * 128], kvT[:, t * T + j * 128:t * T + (j + 1) * 128], identb[:], True, True,
                 [("kvT", t), "identb"], [PS(bt)])
        P.cp("act", kvtok[:, 4 * t:4 * t + 4, :], ps[bt][:].rearrange("p (j r) -> p j r", j=4), [PS(bt)], [("kvtok", t)])
        s1, s2 = rope_apply(bkr, None, 1.0)
        P.tt("dve", krT[0:64, t * T:(t + 1) * T], scr[s1][0:64, 0:512], scr[s2][0:64, 0:512], ALU.add,
             [SC(s1), SC(s2)], [("krT", t)])
        P.tt("pool", krT[64:128, t * T:(t + 1) * T], scr[s1][0:64, 0:512], scr[s2][0:64, 0:512], ALU.add,
             [SC(s1), SC(s2)], [("krT", t)])
        zgroup(2)
        q_nope(0)
        q_rope(0)
        q_lat(0)
        zgroup(3)
        for h in range(16):
            qi = h % 2
            bo, bm = 3 + qi, 5 + qi
            kbs = [(4 * t + kk, kk * 128, True) for kk in range(4)] + [(kb, 0, False) for kb in range(4 * t)]
            nk = len(kbs)
            hooks = {}
            pre = []
            if h >= 1:
                (hooks.__setitem__(min(nk - 1, 5), lambda h=h: o_proj(h - 1)) if HOOKS & 8 else pre.append(lambda h=h: o_proj(h - 1)))
            if h + 1 < 16:
                (hooks.__setitem__(0, lambda h=h: q_nope(h + 1)) if HOOKS & 1 else pre.append(lambda h=h: q_nope(h + 1)))
                (hooks.__setitem__(1, lambda h=h: q_rope(h + 1)) if HOOKS & 2 else pre.append(lambda h=h: q_rope(h + 1)))
                (hooks.__setitem__(2 if nk < 8 else 3, lambda h=h: q_lat(h + 1)) if HOOKS & 4 else pre.append(lambda h=h: q_lat(h + 1)))
            for f_ in pre:
                f_()
            pts = {}
            sbk = {}
            npair = nk // 2
            for p_ in range(npair + 1):
                if p_ < npair:
                    blk = [2 * p_, 2 * p_ + 1]
                    for n_ in blk:
                        kb, lo, diag = kbs[n_]
                        sbk[n_] = state["sb"] = (state["sb"] + 1) % 3
                        bs_ = sbk[n_]
                        pts[n_] = state["pt"] = (state["pt"] + 1) % NPT
                        P.mm(ps[bs_][:, lo:512], kvT[:, kb * 128:(kb + 1) * 128], qlat[qi][:, lo:512], True, False,
                             [("kvT", kb // 4), ("qlat", qi)], [PS(bs_)])
                    for i_, n_ in enumerate(blk):
                        kb, lo, diag = kbs[n_]
                        bs_ = sbk[n_]
                        r0 = 64 * i_
                        P.mm(ps[bs_][:, lo:512], krT[r0:r0 + 64, kb * 128:(kb + 1) * 128], qrope[qi][r0:r0 + 64, lo:512],
                             False, True, [("krT", kb // 4), ("qrope", qi)], [PS(bs_)])
                    for n_ in blk:
                        kb, lo, diag = kbs[n_]
                        bs_ = sbk[n_]
                        pi_ = pts[n_]
                        P.act(pT[pi_][:, lo:512], ps[bs_][:, lo:512], AF.Exp, [PS(bs_)], [("pT", pi_)])
                        if diag:
                            P.ms("pool", pT[pi_][64:128, lo:lo + 64], 0.0, [("pT", pi_)])
                if p_ >= 1:
                    for m_ in (2 * p_ - 2, 2 * p_ - 1):
                        kb, lo, diag = kbs[m_]
                        pi_ = pts[m_]
                        tk = kb // 4
                        P.mm(ps[bo][:, lo:512], kvtok[:, kb, :], pT[pi_][:, lo:512], m_ == 0, m_ == nk - 1,
                             [("kvtok", tk), ("pT", pi_)], [PS(bo)])
                        P.mm(ps[bm][:, lo:512], onesb[:], pT[pi_][:, lo:512], m_ == 0, m_ == nk - 1,
                             ["onesb", ("pT", pi_)], [PS(bm)])
                        if m_ in hooks:
                            hooks[m_]()
                        if m_ + 0.5 in hooks:
                            hooks[m_ + 0.5]()
            o_norm(h)
        o_proj(15)
        out_proj_and_norm("owout", 1, xb, "xb", t)
        P.dma("xout", out_d[t * T:(t + 1) * T, :].rearrange("(j p) d -> p j d", p=128), xb[:],
              r=[("xb", j) for j in range(4)])

    load_x(0)
    for t in range(NT):
        layer0(t)
        layer1(t)
    stats = P.emit()
    return nc, stats


def _host_inputs(inp, b, NT=8):
    S = NT * T
    f = lambda a: np.ascontiguousarray(a, dtype=np.float32)
    fm = lambda v, n: np.asarray(v, np.float32).reshape(n, 128).T
    pp = np.zeros((128, PP_N), np.float32)
    pp[:, PP_C:PP_C + 8] = fm(inp["c"][b], 8)
    for l in range(2):
        pp[:, PP_ADAB + 16 * l:PP_ADAB + 16 * l + 16] = fm(inp["ada_b"][l, :2048], 16)
    pp[:, PP_GNG:PP_GNG + 2] = fm(inp["gmlp_norm_g"][0], 2)
    pp[:, PP_GNB:PP_GNB + 2] = fm(inp["gmlp_norm_b"][0], 2)
    pp[:, PP_PB:PP_PB + 8] = fm(inp["pool_b"][0], 8)
    pp[:, PP_PS:PP_PS + 8] = fm(inp["pool_scale"][0], 8)
    pp[:, PP_QNG:PP_QNG + 2] = fm(inp["mla_q_norm_g"][0], 2)
    pp[:, PP_KVG:PP_KVG + 1] = fm(inp["mla_kv_norm_g"][0], 1)
    inv = (1.0 / (np.float32(10000.0) ** (np.arange(0, 64, 2, dtype=np.float32) / np.float32(64)))).astype(np.float32)
    pp[:, PP_INV] = np.tile(inv, 4)
    pp[:, PP_PH] = np.where(np.arange(128) < 64, np.float32(np.pi / 2), np.float32(0.0))
    rows = np.concatenate([np.asarray(inp["gmlp_norm_b"][0], np.float32).reshape(-1),
                           np.asarray(inp["gmlp_bs"][0], np.float32).reshape(-1)])[None, :]
    return {
        "x": f(inp["x"][b, :S]),
        "pos": np.ascontiguousarray(inp["positions"][b:b + 1, :S], dtype=np.int32),
        "pp": pp, "rows": f(rows),
        "ada_w": f(inp["ada_w"]), "ada_bg": f(inp["ada_b"][:, 2048:]),
        "ln_g": f(inp["ln_g"]), "ln_b": f(inp["ln_b"]),
        "e_w_in": f(inp["e_w_in"][0]), "gmlp_ws": f(inp["gmlp_ws"][0]), "pool_w": f(inp["pool_w"][0]),
        "e_w_out": f(inp["e_w_out"][0]), "o_w_in": f(inp["o_w_in"][0]),
        "w_uq": f(np.asarray(inp["mla_w_uq"][0]).reshape(256, 16 * 192)),
        "w_uk": f(np.asarray(inp["mla_w_uk"][0]).reshape(128, 16 * 128)),
        "w_uv": f(np.asarray(inp["mla_w_uv"][0]).reshape(128, 16 * 128)),
        "o_w_out": f(inp["o_w_out"][0]),
    }


_CACHE = {}


def kernel(**inputs):
    inp = {k: np.asarray(v) for k, v in inputs.items()}
    if "nc" not in _CACHE:
        _CACHE["nc"] = build_program(8)[0]
    nc = _CACHE["nc"]
    in_maps = [_host_inputs(inp, b) for b in range(NCORES)]
    res = run_bass_kernel_spmd(nc, in_maps, core_ids=list(range(NCORES)))
    return np.stack([np.asarray(r["out"], dtype=np.float32) for r in res.results], axis=0)
```

```python
import numpy as np
import concourse.bass as bass
import concourse.mybir as mybir
from concourse.bass_utils import run_bass_kernel_spmd

F32 = mybir.dt.float32
BF16 = mybir.dt.bfloat16
I32 = mybir.dt.int32
AF = mybir.ActivationFunctionType
ALU = mybir.AluOpType

D = 1024
SEQ = 4096
T = 512
NCORES = 8
ALPHA = float((2.0 * 2) ** 0.25)
LN_EPS = 1e-5
ATTN_SCALE = float(192 ** -0.5)
POOL_WINDOWS = (2, 4, 8, 16)
TWO_PI = float(2 * np.pi)
PI = float(np.pi)
C1 = 6.28125
C2 = TWO_PI - C1

PP_C, PP_ADAB, PP_GNG, PP_GNB, PP_PB, PP_PS, PP_QNG, PP_KVG, PP_INV, PP_PH, PP_N = 0, 8, 40, 42, 44, 52, 60, 62, 63, 64, 65


class Prog:
    ENG = ("pe", "act", "dve", "pool", "sp")

    def __init__(self, nc):
        self.nc = nc
        self.e = {"pe": nc.tensor, "act": nc.scalar, "dve": nc.vector, "pool": nc.gpsimd, "sp": nc.sync}
        self.ops = []

    def op(self, eng, fn, r=(), w=()):
        self.ops.append(dict(eng=eng, fn=fn, r=tuple(r), w=tuple(w), dma=None))

    def dma(self, key, out, in_, r=(), w=(), eng="sp", **kw):
        self.ops.append(dict(eng=eng, fn=lambda E: E.dma_start(out=out, in_=in_, **kw),
                             r=tuple(r), w=tuple(w), dma=key))

    def mm(self, out, lhsT, rhs, start, stop, r, w):
        self.op("pe", lambda E: E.matmul(out, lhsT, rhs, start=start, stop=stop), r, w)

    def tr(self, out, in_, ident, r, w):
        self.op("pe", lambda E: E.transpose(out, in_, ident), r, w)

    def act(self, out, in_, func, r, w, bias=None, scale=None):
        kw = {}
        if bias is not None:
            kw["bias"] = bias
        if scale is not None:
            kw["scale"] = scale
        self.op("act", lambda E: E.activation(out, in_, func, **kw), r, w)

    def tt(self, eng, out, a, b, op, r, w):
        self.op(eng, lambda E: E.tensor_tensor(out, a, b, op), r, w)

    def ts(self, eng, out, a, s1, s2, op0, op1, r, w):
        if s2 is None:
            self.op(eng, lambda E: E.tensor_scalar(out, a, s1, None, op0), r, w)
        else:
            self.op(eng, lambda E: E.tensor_scalar(out, a, s1, s2, op0, op1), r, w)

    def stt(self, eng, out, a, s, b, op0, op1, r, w):
        self.op(eng, lambda E: E.scalar_tensor_tensor(out, a, s, b, op0, op1), r, w)

    def cp(self, eng, out, a, r, w):
        if eng == "act":
            self.op("act", lambda E: E.copy(out, a), r, w)
        else:
            self.op(eng, lambda E: E.tensor_copy(out, a), r, w)

    def ms(self, eng, out, val, w):
        self.op(eng, lambda E: E.memset(out, val), (), w)

    def emit(self):
        nc = self.nc
        ops = self.ops
        last_w, readers, last_dma = {}, {}, {}
        for i, o in enumerate(ops):
            deps = set()
            for res in o["r"]:
                if res in last_w:
                    deps.add(last_w[res])
            for res in o["w"]:
                if res in last_w:
                    deps.add(last_w[res])
                deps.update(readers.get(res, ()))
            if o["dma"] is not None and o["dma"] in last_dma:
                deps.add(last_dma[o["dma"]])
            deps.discard(i)
            o["deps"] = deps
            for res in o["r"]:
                readers.setdefault(res, []).append(i)
            for res in o["w"]:
                last_w[res] = i
                readers[res] = []
            if o["dma"] is not None:
                last_dma[o["dma"]] = i

        def skip(p, o):
            return p["dma"] is None and o["dma"] is None and p["eng"] == "pe" and o["eng"] == "pe"

        for o in ops:
            o["need_inc"] = o["dma"] is not None
        for o in ops:
            latest = {}
            for j in o["deps"]:
                p = ops[j]
                if skip(p, o) or p["dma"] is not None:
                    continue
                latest[p["eng"]] = max(latest.get(p["eng"], -1), j)
            for j in latest.values():
                ops[j]["need_inc"] = True
        sem = {e: nc.alloc_semaphore("s_" + e) for e in self.ENG}
        cnt = {e: 0 for e in self.ENG}
        dsem, dcnt = {}, {}
        for o in ops:
            if o["dma"] is not None:
                k = o["dma"]
                if k not in dsem:
                    dsem[k] = nc.alloc_semaphore("d_%d" % len(dsem))
                    dcnt[k] = 0
                dcnt[k] += 16
                o["sem"], o["val"] = ("d", k), dcnt[k]
            elif o["need_inc"]:
                cnt[o["eng"]] += 1
                o["sem"], o["val"] = ("e", o["eng"]), cnt[o["eng"]]
        known = {e: {} for e in self.ENG}
        nwait = 0
        for o in ops:
            E = self.e[o["eng"]]
            need = {}
            latest = {}
            for j in o["deps"]:
                p = ops[j]
                if skip(p, o):
                    continue
                if p["dma"] is not None:
                    need[p["sem"]] = max(need.get(p["sem"], 0), p["val"])
                else:
                    latest[p["eng"]] = max(latest.get(p["eng"], -1), j)
            for j in latest.values():
                p = ops[j]
                need[p["sem"]] = max(need.get(p["sem"], 0), p["val"])
            kn = known[o["eng"]]
            for s, v in need.items():
                if kn.get(s, 0) >= v:
                    continue
                kn[s] = v
                E.wait_ge(dsem[s[1]] if s[0] == "d" else sem[s[1]], v)
                nwait += 1
            ins = o["fn"](E)
            if o["dma"] is not None:
                ins.then_inc(dsem[o["dma"]], 16)
            elif o["need_inc"]:
                ins.then_inc(sem[o["eng"]], 1)
        for k, h in dsem.items():
            nc.sync.wait_ge(h, dcnt[k])
        self.stats = dict(n_ops=len(ops), n_wait=nwait, cnt=dict(cnt), n_dma_keys=len(dsem))
        return self.stats


def build_program(NT=8, debug_x1=False):
    S = NT * T
    nc = bass.Bass("TRN2", target_bir_lowering=False)
    dt = lambda name, shape, dtype=F32, kind="ExternalInput": nc.dram_tensor(name, list(shape), dtype, kind=kind).ap()
    x_d = dt("x", [S, D])
    pos_d = dt("pos", [1, S], I32)
    pp_d = dt("pp", [128, PP_N])
    rows_d = dt("rows", [1, 256 + 512])
    adaw_d = dt("ada_w", [2, D, 3 * D])
    adabg_d = dt("ada_bg", [2, D])
    lng_d = dt("ln_g", [2, D])
    lnb_d = dt("ln_b", [2, D])
    ewin_d = dt("e_w_in", [D, 5 * D])
    ws_d = dt("gmlp_ws", [4, 128, 128])
    poolw_d = dt("pool_w", [4, 256, 256])
    ewout_d = dt("e_w_out", [2 * D, D])
    owin_d = dt("o_w_in", [D, 2496])
    wuq_d = dt("w_uq", [256, 16 * 192])
    wuk_d = dt("w_uk", [128, 16 * 128])
    wuv_d = dt("w_uv", [128, 16 * 128])
    owout_d = dt("o_w_out", [2 * D, D])
    out_d = dt("out", [S, D], F32, "ExternalOutput")
    x1_d = dt("x1dbg", [S, D], F32, "ExternalOutput") if debug_x1 else None

    P = Prog(nc)
    sb = lambda name, shape, dtype=F32: nc.alloc_sbuf_tensor("sb_" + name, list(shape), dtype)

    xa = sb("xa", [128, 4, D])
    xb = sb("xb", [128, 4, D])
    gate1 = sb("gate1", [128, 2, D])
    lng = sb("lng", [128, 2, D])
    lnb = sb("lnb", [128, 2, D])
    ident = sb("ident", [128, 128])
    onesf = sb("onesf", [128, 128])
    onesb = sb("onesb", [128, 128], BF16)
    pp = sb("pp", [128, PP_N])
    cond = sb("cond", [128, 8])
    modT = sb("modT", [128, 2, 16])
    hT = sb("hT", [128, 8, T], BF16)
    mixT = sb("mixT", [128, 16, T], BF16)
    NSLOT = 4
    wsl = [sb("wsl%d" % i, [128, 8, 512], BF16) for i in range(NSLOT)]
    wukT = sb("wukT", [128, 16, 128], BF16)
    wuv = sb("wuv", [128, 16, 128], BF16)
    poolw = sb("poolw", [128, 4, 2, 256], BF16)
    WsT = sb("WsT", [128, 4, 128], BF16)
    Cmat = sb("Cmat", [128, 4, 2, 128])
    rcw = sb("rcw", [128, 4, 16])
    halo = sb("halo", [128, 8, 16])
    vn = sb("vn", [128, 4, D], BF16)
    wq = [vn[:, 2 * i:2 * i + 2, :].rearrange("p a (b n) -> p a b n", b=4) for i in range(2)]
    WQ = [[("vn", j, h) for j in (2 * i, 2 * i + 1) for h in range(4)] for i in range(2)]
    NSCR = 8
    scr = [sb("scr%d" % i, [128, 528]) for i in range(NSCR)]
    pl = sb("pl", [128, 2, 2, T], BF16)
    szb = sb("szb", [128, 2, T])
    small = sb("small", [128, 128])
    qcn = sb("qcn", [128, 2, T], BF16)
    kvT = sb("kvT", [128, S], BF16)
    kvtok = sb("kvtok", [128, S // 128, 128], BF16)
    krT = sb("krT", [128, S], BF16)
    cs = sb("cs", [128, T])
    posi = sb("posi", [128, T], I32)
    qn = sb("qn", [128, T], BF16)
    qlat = [sb("qlat%d" % i, [128, T], BF16) for i in range(2)]
    qrope = [sb("qrope%d" % i, [128, T], BF16) for i in range(2)]
    olat = [sb("olat%d" % i, [128, T], BF16) for i in range(2)]
    NPT = 6
    pT = [sb("pT%d" % i, [128, T], BF16) for i in range(NPT)]
    ps = [nc.alloc_psum_tensor("ps%d" % i, [128, 512], F32) for i in range(8)]
    PS = lambda i: ("ps", i)
    state = dict(bank=0, scr=0, slot=0, pt=0, sb=0)

    def nbank(lo=0, hi=8):
        b = state["bank"]
        b = lo + ((b - lo + 1) % (hi - lo)) if lo <= b < hi else lo
        state["bank"] = b
        return b

    def nscr():
        state["scr"] = (state["scr"] + 1) % NSCR
        return state["scr"]

    SC = lambda i: ("scr", i)

    def kview(w2d, c0, ncols, k0=0, nk=8):
        return w2d[k0 * 128:(k0 + nk) * 128, c0:c0 + ncols].rearrange("(kc p) n -> p kc n", p=128)

    wlist = []
    for t in range(NT):
        for g in (2, 3, 0, 4, 1, 5, 6, 8, 7, 9):
            wlist.append(("ewin", g, kview(ewin_d, g * 512, 512), 512))
        for hf in range(2):
            for kh in range(2):
                wlist.append(("ewout", (hf, kh), kview(ewout_d, hf * 512, 512, kh * 8), 512))
        wlist.append(("owinA", 0, kview(owin_d, 0, 448), 448))
        for g in range(4):
            wlist.append(("owinZ", g, kview(owin_d, 448 + g * 512, 512), 512))
        for hf in range(2):
            for kh in range(2):
                wlist.append(("owout", (hf, kh), kview(owout_d, hf * 512, 512, kh * 8), 512))
    wstate = dict(issued=0, used=0, released=0)

    NG = len(wlist) // NT
    wscr = nc.dram_tensor("wscr", [NG, 128, 8 * 512], BF16, kind="Internal").ap()

    def w_pump():
        while wstate["issued"] < len(wlist) and wstate["issued"] - NSLOT < wstate["released"]:
            i = wstate["issued"]
            kind, g, src, ncols = wlist[i]
            s_ = i % NSLOT
            gid = i % NG
            scr_v = wscr[gid].rearrange("p (k n) -> p k n", k=8)[:, :, 0:ncols]
            if i < NG:
                P.dma(("wp", s_), wsl[s_][:, :, 0:ncols], src, w=[("wsl", s_)], eng="pool")
                if NT > 1:
                    P.dma(("wst", s_), scr_v, wsl[s_][:, :, 0:ncols], r=[("wsl", s_)], w=[("wscr", gid)])
            else:
                P.dma(("w", s_), wsl[s_][:, :, 0:ncols], scr_v, r=[("wscr", gid)], w=[("wsl", s_)])
            wstate["issued"] += 1

    def w_next(kind):
        i = wstate["used"]
        assert wlist[i][0] == kind, (wlist[i][0], kind)
        w_pump()
        assert wstate["issued"] > i
        wstate["used"] += 1
        return wsl[i % NSLOT], ("wsl", i % NSLOT)

    def w_release(n=1):
        wstate["released"] += n
        w_pump()

    P.dma("c0", pp[:], pp_d, w=["pp"])
    for l in range(2):
        P.dma(("cg", l), gate1[:, l, :], adabg_d[l:l + 1, :].partition_broadcast(128), w=[("gate1", l)])
        P.dma(("cl", l), lng[:, l, :], lng_d[l:l + 1, :].partition_broadcast(128), w=[("lng", l)])
        P.dma(("cb", l), lnb[:, l, :], lnb_d[l:l + 1, :].partition_broadcast(128), w=[("lnb", l)])
    P.op("pool", lambda E: E.iota(ident[:], [[1, 128]], base=0, channel_multiplier=-1,
                                    allow_small_or_imprecise_dtypes=True), w=["ident"])
    P.op("pool", lambda E: E.tensor_single_scalar(ident[:], ident[:], 0.0, ALU.is_equal), r=["ident"], w=["ident"])
    P.ms("pool", onesf[:], 1.0, ["onesf"])
    P.ms("pool", onesb[:], 1.0, ["onesb"])
    P.ms("pool", halo[:], 0.0, [("halo", c) for c in range(8)])
    P.op("pool", lambda E: E.iota(rcw[:, 0, :], [[1, 16]], base=1, channel_multiplier=0,
                                    allow_small_or_imprecise_dtypes=True), w=["rcw"])
    for g in (3, 2, 1):
        P.cp("pool", rcw[:, g, :], rcw[:, 0, :], ["rcw"], ["rcw"])
    for g, win in enumerate(POOL_WINDOWS):
        P.ts("dve", rcw[:, g, :], rcw[:, g, :], float(win), None, ALU.min, None, ["rcw"], ["rcw"])
    P.op("dve", lambda E: E.reciprocal(rcw[:], rcw[:]), ["rcw"], ["rcw"])
    P.act(cond[:], pp[:, PP_C:PP_C + 8], AF.Silu, ["pp"], ["cond"])

    for kc in range(8):
        P.cp("dve", scr[kc][:, 0:128], cond[:, kc:kc + 1].to_broadcast([128, 128]), ["cond"], [SC(kc)])
    stg = [xa[:].rearrange("p j (a n) -> p (j a) n", a=2), xb[:].rearrange("p j (a n) -> p (j a) n", a=2)]
    STG = [[("xa", j) for j in range(4)], [("xb", j) for j in range(4)]]
    sti = 0
    for l in range(2):
        bmod = nbank()
        for blk in range(4):
            s_ = sti % 2
            sti += 1
            P.dma(("stg", s_), stg[s_], kview(adaw_d[l], blk * 512, 512), w=STG[s_])
            for fc in range(4):
                col = blk * 4 + fc
                for kc in range(8):
                    P.mm(ps[bmod][:, col:col + 1], stg[s_][:, kc, fc * 128:(fc + 1) * 128], cond[:, kc:kc + 1],
                         kc == 0, kc == 7, STG[s_] + ["cond"], [PS(bmod)])
        P.tt("dve", modT[:, l, :], ps[bmod][:, 0:16], pp[:, PP_ADAB + 16 * l:PP_ADAB + 16 * l + 16], ALU.add,
             [PS(bmod), "pp"], [("modT", l)])
        P.ts("dve", modT[:, l, 8:16], modT[:, l, 8:16], 1.0, None, ALU.add, None, [("modT", l)], [("modT", l)])
        for hf in range(2):
            s_ = sti % 2
            sti += 1
            bg = nbank()
            P.dma(("stg", s_), stg[s_], kview(adaw_d[l], 2048 + hf * 512, 512), w=STG[s_])
            for kc in range(8):
                P.mm(ps[bg][:], scr[kc][:, 0:128], stg[s_][:, kc, :], kc == 0, kc == 7, STG[s_] + [SC(kc)], [PS(bg)])
            P.stt("dve", gate1[:, l, hf * 512:(hf + 1) * 512], ps[bg][:], 1.0, gate1[:, l, hf * 512:(hf + 1) * 512],
                  ALU.add, ALU.add, [PS(bg), ("gate1", l)], [("gate1", l)])

    wstg = stg[0]
    P.dma(("stg", 0), wstg[:, 0:4, 0:128], ws_d.rearrange("h t s -> t h s"), w=STG[0])
    wsf = stg[1]
    rows = scr[0][0:1, 0:256], scr[1][0:1, 0:512]
    P.dma("c1", rows[0], rows_d[0:1, 0:256], w=[SC(0)])
    P.dma("c2", rows[1], rows_d[0:1, 256:768], w=[SC(1)])
    rsrow = scr[2][0:1, 0:512]
    brs = nbank()
    for h in range(4):
        b_ = nbank()
        P.tr(ps[b_][:, 0:128], wstg[:, h, 0:128], ident[:], STG[0] + ["ident"], [PS(b_)])
        P.cp("dve", wsf[:, h, 0:128], ps[b_][:, 0:128], [PS(b_)], STG[1])
        P.ms("pool", wsf[64:128, h, 0:64], 0.0, STG[1])
        P.cp("act", WsT[:, h, :], wsf[:, h, 0:128], STG[1], ["WsT"])
        P.mm(ps[brs][0:1, h * 128:(h + 1) * 128], onesf[:, 0:1], wsf[:, h, 0:128], True, True,
             STG[1] + ["onesf"], [PS(brs)])
    P.cp("dve", rsrow, ps[brs][0:1, :], [PS(brs)], [SC(2)])
    for h in range(4):
        for dc in range(2):
            b_ = nbank()
            P.mm(ps[b_][:, 0:128], rows[0][0:1, dc * 128:(dc + 1) * 128], rsrow[0:1, h * 128:(h + 1) * 128], True, False,
                 [SC(0), SC(2)], [PS(b_)])
            P.mm(ps[b_][:, 0:128], onesf[0:1, :], rows[1][0:1, h * 128:(h + 1) * 128], False, True,
                 [SC(1), "onesf"], [PS(b_)])
            P.cp("dve", Cmat[:, h, dc, :], ps[b_][:, 0:128], [PS(b_)], ["Cmat"])
    P.dma("cw0", wuv[:], wuv_d.rearrange("r (h d) -> r h d", h=16), w=["wuv"], eng="pool")
    P.dma("cw1", poolw[:], poolw_d.rearrange("g (dc p) e -> p g dc e", p=128), w=["poolw"], eng="pool")
    P.dma(("stg", 1), stg[1][:, 0:4, :], wuk_d.rearrange("r (a n) -> r a n", a=4), w=STG[1])
    for h in range(16):
        b_ = nbank()
        P.tr(ps[b_][:, 0:128], stg[1][:, h // 4, (h % 4) * 128:(h % 4 + 1) * 128], ident[:], STG[1] + ["ident"], [PS(b_)])
        P.cp("dve" if h % 2 else "act", wukT[:, h, :], ps[b_][:, 0:128], [PS(b_)], ["wukT"])

    def make_hT(src, l, src_res):
        order = [(state["bank"] + 1 + i) % 8 for i in range(8)]
        state["bank"] = order[-1]
        for c in range(8):
            bk = order[c]
            for j in range(3):
                P.tr(ps[bk][:, j * 128:(j + 1) * 128], src[:, j, c * 128:(c + 1) * 128], ident[:],
                     [(src_res, j), "ident"], [PS(bk)])
        for c in range(8):
            bk = order[c]
            P.tr(ps[bk][:, 384:512], src[:, 3, c * 128:(c + 1) * 128], ident[:], [(src_res, 3), "ident"], [PS(bk)])
            if c % 2 == 0:
                P.act(hT[:, c, :], ps[bk][:], AF.Identity, [PS(bk), ("modT", l)], [("hT", c)],
                      bias=modT[:, l, c:c + 1], scale=modT[:, l, 8 + c:9 + c])
            else:
                P.ts("dve", hT[:, c, :], ps[bk][:], modT[:, l, 8 + c:9 + c], modT[:, l, c:c + 1], ALU.mult, ALU.add,
                     [PS(bk), ("modT", l)], [("hT", c)])

    HT_ALL = [("hT", c) for c in range(8)]
    MIX_ALL = [("mixT", c) for c in range(16)]

    def out_proj_and_norm(kind, l, res_src, res_name, t):
        for j in range(4):
            P.act(xb[:, j, :], res_src[:, j, :], AF.Identity, [(res_name, j)], [("xb", j)], scale=ALPHA)
        wg = [w_next(kind) for _ in range(4)]
        for j in range(4):
            for hf in range(2):
                b_ = nbank()
                for kc in range(16):
                    wt, rr = wg[2 * hf + (kc // 8)]
                    P.mm(ps[b_][:], mixT[:, kc, j * 128:(j + 1) * 128], wt[:, kc % 8, :], kc == 0, kc == 15,
                         [("mixT", kc), rr], [PS(b_)])
                s_ = nscr()
                P.tt("dve", scr[s_][:, 0:512], ps[b_][:], gate1[:, l, hf * 512:(hf + 1) * 512], ALU.mult,
                     [PS(b_), ("gate1", l)], [SC(s_)])
                P.tt("pool", xb[:, j, hf * 512:(hf + 1) * 512], xb[:, j, hf * 512:(hf + 1) * 512],
                     scr[s_][:, 0:512], ALU.add, [SC(s_), ("xb", j)], [("xb", j)])
            c0 = 64 + 16 * j
            P.op("dve", lambda E, j=j, c0=c0: E.bn_stats(small[:, c0:c0 + 6], xb[:, j, 0:512]), [("xb", j)], [("st", j)])
            P.op("dve", lambda E, j=j, c0=c0: E.bn_stats(small[:, c0 + 6:c0 + 12], xb[:, j, 512:1024]), [("xb", j)], [("st", j)])
            P.op("dve", lambda E, c0=c0: E.bn_aggr(small[:, c0 + 12:c0 + 14], small[:, c0:c0 + 12]), [("st", j)], [("mv", j)])
            P.act(small[:, c0 + 14:c0 + 15], small[:, c0 + 13:c0 + 14], AF.Ln, [("mv", j)], [("rstd", j)], bias=LN_EPS)
            P.act(small[:, c0 + 14:c0 + 15], small[:, c0 + 14:c0 + 15], AF.Exp, [("rstd", j)], [("rstd", j)], scale=-0.5)
            P.stt("dve", small[:, c0 + 15:c0 + 16], small[:, c0 + 12:c0 + 13], -1.0, small[:, c0 + 14:c0 + 15], ALU.mult, ALU.mult,
                  [("mv", j), ("rstd", j)], [("nb", j)])
            P.act(xb[:, j, :], xb[:, j, :], AF.Identity, [("xb", j), ("rstd", j), ("nb", j)], [("xb", j)],
                  bias=small[:, c0 + 15:c0 + 16], scale=small[:, c0 + 14:c0 + 15])
            P.tt("pool", xb[:, j, :], xb[:, j, :], lng[:, l, :], ALU.mult, [("xb", j), ("lng", l)], [("xb", j)])
            P.tt("dve", xb[:, j, :], xb[:, j, :], lnb[:, l, :], ALU.add, [("xb", j), ("lnb", l)], [("xb", j)])
        w_release(4)

    def layer0(t):
        make_hT(xa, 0, "xa")
        for hf in range(2):
            wt, wr = w_next("ewin")
            for j in range(4):
                b_ = nbank()
                for kc in range(8):
                    P.mm(ps[b_][:], hT[:, kc, j * 128:(j + 1) * 128], wt[:, kc, :], kc == 0, kc == 7,
                         [("hT", kc), wr], [PS(b_)])
                st = small[:, 16:28].rearrange("p (a b) -> p a b", a=2)
                mv = small[:, 28:32].rearrange("p (a b) -> p a b", a=2)
                for i in range(2):
                    P.op("dve", lambda E, i=i, b_=b_: E.bn_stats(st[:, i, :], ps[b_][:, i * 256:(i + 1) * 256]),
                         [PS(b_)], ["vst"])
                    P.op("dve", lambda E, i=i: E.bn_aggr(mv[:, i, :], st[:, i, :]), ["vst"], ["vmv"])
                rs_ = small[:, 32:34]
                nb_ = small[:, 34:36]
                P.act(rs_, mv[:, :, 1], AF.Ln, ["vmv"], ["vrs"], bias=LN_EPS)
                P.act(rs_, rs_, AF.Exp, ["vrs"], ["vrs"], scale=-0.5)
                P.stt("dve", nb_, mv[:, :, 0], -1.0, rs_, ALU.mult, ALU.mult, ["vmv", "vrs"], ["vnb"])
                for i in range(2):
                    h = hf * 2 + i
                    P.act(vn[:, j, h * 256:(h + 1) * 256], ps[b_][:, i * 256:(i + 1) * 256], AF.Identity,
                          [PS(b_), "vrs", "vnb"], [("vn", j, h)], bias=small[:, 34 + i:35 + i], scale=small[:, 32 + i:33 + i])
            w_release()
        for hp in range(2):
            wu, ru = w_next("ewin")
            wz, rz = w_next("ewin")
            for hh in range(2):
                h = hp * 2 + hh
                for dc in range(2):
                    cl = hh * 2 + dc
                    bu, bz, bs_ = nbank(), nbank(), nbank()
                    for kc in range(8):
                        P.mm(ps[bz][:], wz[:, kc, cl * 128:(cl + 1) * 128], hT[:, kc, :], kc == 0, kc == 7,
                             [("hT", kc), rz], [PS(bz)])
                    for j in range(4):
                        P.mm(ps[bs_][:, j * 128:(j + 1) * 128], vn[:, j, h * 256 + dc * 128:h * 256 + (dc + 1) * 128],
                             WsT[:, h, :], True, True, [("vn", j, h), "WsT"], [PS(bs_)])
                    for kc in range(8):
                        P.mm(ps[bu][:], wu[:, kc, cl * 128:(cl + 1) * 128], hT[:, kc, :], kc == 0, kc == 7,
                             [("hT", kc), ru], [PS(bu)])
                    s1, s2 = nscr(), nscr()
                    P.act(scr[s1][:, 0:512], ps[bz][:], AF.Silu, [PS(bz)], [SC(s1)])
                    P.stt("dve", scr[s2][:, 0:512].rearrange("p (j t) -> p j t", j=4),
                          ps[bs_][:].rearrange("p (j t) -> p j t", j=4), pp[:, PP_GNG + dc:PP_GNG + dc + 1],
                          Cmat[:, h, dc, :].unsqueeze(1).to_broadcast([128, 4, 128]), ALU.mult, ALU.add,
                          [PS(bs_), "pp", "Cmat"], [SC(s2)])
                    P.tt("pool", scr[s2][:, 0:512], scr[s2][:, 0:512], scr[s1][:, 0:512], ALU.mult,
                         [SC(s1), SC(s2)], [SC(s2)])
                    P.tt("dve", mixT[:, 2 * h + dc, :], ps[bu][:], scr[s2][:, 0:512], ALU.mult,
                         [PS(bu), SC(s2)], [("mixT", 2 * h + dc)])
            w_release(2)
        pw = {}

        def pool_x(g):
            wx, rx = pw["x%d" % (g // 2)]
            gg = g % 2
            win = POOL_WINDOWS[g]
            nstep = int(np.log2(win))
            for dc in range(2):
                cl = gg * 2 + dc
                ch = 2 * g + dc
                bx = nbank()
                for kc in range(8):
                    P.mm(ps[bx][:], wx[:, kc, cl * 128:(cl + 1) * 128], hT[:, kc, :], kc == 0, kc == 7,
                         [("hT", kc), rx], [PS(bx)])
                sx = nscr()
                P.cp("pool", scr[sx][:, 0:16], halo[:, ch, :], [("halo", ch)], [SC(sx)])
                P.cp("act", scr[sx][:, 16:528], ps[bx][:], [PS(bx)], [SC(sx)])
                P.cp("pool", halo[:, ch, :], scr[sx][:, 512:528], [SC(sx)], [("halo", ch)])
                cur = sx
                for k in range(nstep):
                    sh = 1 << k
                    lo = (1 << (k + 1)) - 1
                    nx = nscr()
                    P.tt("pool", scr[nx][:, lo:528], scr[cur][:, lo:528], scr[cur][:, lo - sh:528 - sh], ALU.add,
                         [SC(cur)], [SC(nx)])
                    cur = nx
                P.stt("dve", pl[:, g % 2, dc, :], scr[cur][:, 16:528], 1.0 / win, scr[sx][:, 16:528], ALU.mult, ALU.subtract,
                      [SC(cur), SC(sx)], [("pl", g % 2, dc)])
                if t == 0:
                    s3 = nscr()
                    P.tt("dve", scr[s3][:, 0:16], scr[cur][:, 16:32], rcw[:, g, :], ALU.mult, [SC(cur), "rcw"], [SC(s3)])
                    P.tt("dve", pl[:, g % 2, dc, 0:16], scr[s3][:, 0:16], scr[sx][:, 16:32], ALU.subtract,
                         [SC(s3), SC(sx)], [("pl", g % 2, dc)])

        def pool_zy(g):
            wz, rz = pw["z%d" % (g // 2)]
            gg = g % 2
            for ec in range(2):
                cl = gg * 2 + ec
                ch = 2 * g + ec
                by, bz = nbank(), nbank()
                for kc in range(8):
                    P.mm(ps[bz][:], wz[:, kc, cl * 128:(cl + 1) * 128], hT[:, kc, :], kc == 0, kc == 7,
                         [("hT", kc), rz], [PS(bz)])
                P.act(szb[:, ec, :], ps[bz][:], AF.Silu, [PS(bz)], [("szb", ec)])
                for dc in range(2):
                    P.mm(ps[by][:], poolw[:, g, dc, ec * 128:(ec + 1) * 128], pl[:, g % 2, dc, :], dc == 0, dc == 1,
                         [("pl", g % 2, dc), "poolw"], [PS(by)])
                s2 = nscr()
                P.ts("dve", scr[s2][:, 0:512], ps[by][:], pp[:, PP_PB + ch:PP_PB + ch + 1], pp[:, PP_PS + ch:PP_PS + ch + 1],
                     ALU.add, ALU.mult, [PS(by), "pp"], [SC(s2)])
                P.tt("pool", mixT[:, 8 + ch, :], scr[s2][:, 0:512], szb[:, ec, :], ALU.mult,
                     [("szb", ec), SC(s2)], [("mixT", 8 + ch)])

        pw["x0"] = w_next("ewin")
        pool_x(0)
        pool_x(1)
        w_release()
        pw["z0"] = w_next("ewin")
        pool_zy(0)
        pw["x1"] = w_next("ewin")
        pool_x(2)
        pool_zy(1)
        w_release()
        pool_x(3)
        w_release()
        pw["z1"] = w_next("ewin")
        pool_zy(2)
        pool_zy(3)
        w_release()
        out_proj_and_norm("ewout", 0, xa, "xa", t)
        if debug_x1:
            P.dma("x1o", x1_d[t * T:(t + 1) * T, :].rearrange("(j p) d -> p j d", p=128), xb[:],
                  r=[("xb", j) for j in range(4)])

    def rope_apply(bank, dst, scale):
        s1, s2 = nscr(), nscr()
        P.stt("dve", scr[s1][0:64, 0:512], ps[bank][0:64, :], scale, cs[0:64, :], ALU.mult, ALU.mult,
              [PS(bank), "cs"], [SC(s1)])
        P.stt("dve", scr[s2][0:64, 0:512], ps[bank][64:128, :], scale, cs[64:128, :], ALU.mult, ALU.mult,
              [PS(bank), "cs"], [SC(s2)])
        return s1, s2

    def load_x(t):
        P.dma("xin", xa[:], x_d[t * T:(t + 1) * T, :].rearrange("(j p) d -> p j d", p=128),
              w=[("xa", j) for j in range(4)])

    def layer1(t):
        if t + 1 < NT:
            load_x(t + 1)
        make_hT(xb, 1, "xb")

        def load_wq(g):
            wqi = g % 2
            for cc in range(2):
                P.dma(("wq", wqi, cc), wq[wqi][:, cc, :, 0:192],
                      wuq_d[cc * 128:(cc + 1) * 128, g * 768:(g + 1) * 768].rearrange("p (a n) -> p a n", a=4),
                      w=WQ[wqi], eng="pool")
            P.ts("pool", wq[wqi][:, :, :, 192:224], wq[wqi][:, :, :, 160:192], -1.0, None, ALU.mult, None,
                 WQ[wqi], WQ[wqi])
            P.cp("pool", wq[wqi][:, :, :, 224:256], wq[wqi][:, :, :, 128:160], WQ[wqi], WQ[wqi])

        load_wq(0)
        p_ = "posi"
        posi_ap = posi[:]
        P.dma("pos", posi_ap, pos_d[0:1, t * T:(t + 1) * T].partition_broadcast(128), w=["posi"])
        a_, k_, m_ = nscr(), nscr(), nscr()
        ang, kf, mk = scr[a_][:, 0:512], scr[k_][:, 0:512], scr[m_][:, 0:512]
        ki = posi_ap
        P.cp("dve", ang, posi_ap, ["posi"], [SC(a_)])
        P.ts("dve", ang, ang, pp[:, PP_INV:PP_INV + 1], None, ALU.mult, None, [SC(a_), "pp"], [SC(a_)])

        def wrap():
            P.ts("dve", mk, ang, PI, -TWO_PI, ALU.is_gt, ALU.mult, [SC(a_)], [SC(m_)])
            P.tt("dve", ang, ang, mk, ALU.add, [SC(a_), SC(m_)], [SC(a_)])
            P.ts("dve", mk, ang, -PI, TWO_PI, ALU.is_lt, ALU.mult, [SC(a_)], [SC(m_)])
            P.tt("dve", ang, ang, mk, ALU.add, [SC(a_), SC(m_)], [SC(a_)])

        P.ts("dve", kf, ang, 1.0 / TWO_PI, None, ALU.mult, None, [SC(a_)], [SC(k_)])
        P.cp("dve", ki, kf, [SC(k_), SC(a_)], ["posi"])
        P.cp("dve", kf, ki, ["posi"], [SC(k_)])
        P.stt("dve", ang, kf, -C1, ang, ALU.mult, ALU.add, [SC(k_), SC(a_)], [SC(a_)])
        P.stt("dve", ang, kf, -C2, ang, ALU.mult, ALU.add, [SC(k_), SC(a_)], [SC(a_)])
        wrap()
        P.ts("dve", ang, ang, pp[:, PP_PH:PP_PH + 1], None, ALU.add, None, [SC(a_), "pp"], [SC(a_)])
        wrap()
        P.act(cs[:], ang, AF.Sin, [SC(a_)], ["cs"])

        MB = 7
        HOOKS = 15

        def q_nope(h):
            wqi, hq = (h // 4) % 2, h % 4
            if hq == 0 and h + 4 < 16:
                load_wq(h // 4 + 1)
            for cc in range(2):
                P.mm(ps[MB][:], wq[wqi][:, cc, hq, 0:128], qcn[:, cc, :], cc == 0, cc == 1,
                     WQ[wqi] + [("qcn", cc)], [PS(MB)])
            P.cp("act", qn[:], ps[MB][:], [PS(MB)], ["qn"])

        def q_rope(h):
            wqi, hq, qi = (h // 4) % 2, h % 4, h % 2
            for cc in range(2):
                P.mm(ps[MB][:], wq[wqi][:, cc, hq, 128:256], qcn[:, cc, :], cc == 0, cc == 1,
                     WQ[wqi] + [("qcn", cc)], [PS(MB)])
            s1, s2 = rope_apply(MB, None, ATTN_SCALE)
            P.tt("dve", qrope[qi][0:64, :], scr[s1][0:64, 0:512], scr[s2][0:64, 0:512], ALU.add, [SC(s1), SC(s2)], [("qrope", qi)])
            P.tt("pool", qrope[qi][64:128, :], scr[s1][0:64, 0:512], scr[s2][0:64, 0:512], ALU.add, [SC(s1), SC(s2)], [("qrope", qi)])

        def q_lat(h):
            qi = h % 2
            P.mm(ps[MB][:], wukT[:, h, :], qn[:], True, True, ["wukT", "qn"], [PS(MB)])
            P.act(qlat[qi][:], ps[MB][:], AF.Identity, [PS(MB)], [("qlat", qi)], scale=ATTN_SCALE)

        def o_norm(h):
            qi = h % 2
            bo, bm = 3 + qi, 5 + qi
            rs_ = nscr()
            P.op("dve", lambda E, rs_=rs_, bm=bm: E.reciprocal(scr[rs_][:, 0:512], ps[bm][:]), [PS(bm)], [SC(rs_)])
            P.tt("dve", olat[qi][:], ps[bo][:], scr[rs_][:, 0:512], ALU.mult, [PS(bo), SC(rs_)], [("olat", qi)])

        def o_proj(h):
            qi = h % 2
            P.mm(ps[MB][:], wuv[:, h, :], olat[qi][:], True, True, ["wuv", ("olat", qi)], [PS(MB)])
            P.tt("dve", mixT[:, h, :], ps[MB][:], mixT[:, h, :], ALU.mult, [PS(MB), ("mixT", h)], [("mixT", h)])

        def zgroup(g):
            wz, rz = w_next("owinZ")
            for cl in range(4):
                h = g * 4 + cl
                b_ = 6 + (h % 2)
                for kc in range(8):
                    P.mm(ps[b_][:], wz[:, kc, cl * 128:(cl + 1) * 128], hT[:, kc, :], kc == 0, kc == 7,
                         [("hT", kc), rz], [PS(b_)])
                P.act(mixT[:, h, :], ps[b_][:], AF.Silu, [PS(b_)], [("mixT", h)])
            w_release()

        wa, ra = w_next("owinA")
        P.ts("pool", wa[:, :, 448:480], wa[:, :, 416:448], -1.0, None, ALU.mult, None, [ra], [ra])
        P.cp("pool", wa[:, :, 480:512], wa[:, :, 384:416], [ra], [ra])
        bq = [0, 1]
        bkv, bkr = 2, 3
        for cc in range(2):
            for kc in range(8):
                P.mm(ps[bq[cc]][:], wa[:, kc, cc * 128:(cc + 1) * 128], hT[:, kc, :], kc == 0, kc == 7,
                     [("hT", kc), ra], [PS(bq[cc])])
        for kc in range(8):
            P.mm(ps[bkv][:], wa[:, kc, 256:384], hT[:, kc, :], kc == 0, kc == 7, [("hT", kc), ra], [PS(bkv)])
        for kc in range(8):
            P.mm(ps[bkr][:], wa[:, kc, 384:512], hT[:, kc, :], kc == 0, kc == 7, [("hT", kc), ra], [PS(bkr)])
        w_release()
        sq = [nscr(), nscr(), nscr()]
        for cc in range(2):
            P.act(scr[sq[cc]][:, 0:512], ps[bq[cc]][:], AF.Square, [PS(bq[cc])], [SC(sq[cc])])
        P.act(scr[sq[2]][:, 0:512], ps[bkv][:], AF.Square, [PS(bkv)], [SC(sq[2])])
        zgroup(0)
        bsq, bsk = 4, 5
        for cc in range(2):
            P.mm(ps[bsq][:], onesf[:], scr[sq[cc]][:, 0:512], cc == 0, cc == 1, [SC(sq[cc]), "onesf"], [PS(bsq)])
        P.mm(ps[bsk][:], onesf[:], scr[sq[2]][:, 0:512], True, True, [SC(sq[2]), "onesf"], [PS(bsk)])
        rq, rk = nscr(), nscr()
        P.act(scr[rq][:, 0:512], ps[bsq][:], AF.Ln, [PS(bsq)], [SC(rq)], bias=LN_EPS, scale=1.0 / 256)
        P.act(scr[rq][:, 0:512], scr[rq][:, 0:512], AF.Exp, [SC(rq)], [SC(rq)], scale=-0.5)
        P.act(scr[rk][:, 0:512], ps[bsk][:], AF.Ln, [PS(bsk)], [SC(rk)], bias=LN_EPS, scale=1.0 / 128)
        P.act(scr[rk][:, 0:512], scr[rk][:, 0:512], AF.Exp, [SC(rk)], [SC(rk)], scale=-0.5)
        for cc in range(2):
            P.stt("dve", qcn[:, cc, :], ps[bq[cc]][:], pp[:, PP_QNG + cc:PP_QNG + cc + 1], scr[rq][:, 0:512],
                  ALU.mult, ALU.mult, [PS(bq[cc]), "pp", SC(rq)], [("qcn", cc)])
        zgroup(1)
        kvf = nscr()
        P.stt("dve", scr[kvf][:, 0:512], ps[bkv][:], pp[:, PP_KVG:PP_KVG + 1], scr[rk][:, 0:512],
              ALU.mult, ALU.mult, [PS(bkv), "pp", SC(rk)], [SC(kvf)])
        P.cp("pool", kvT[:, t * T:(t + 1) * T], scr[kvf][:, 0:512], [SC(kvf)], [("kvT", t)])
        bt = 4
        for j in range(4):
            P.tr(ps[bt][:, j * 128:(j + 1) * 128], scr[kvf][:, j * 128:(j + 1) * 128], ident[:], [SC(kvf), "ident"], [PS(bt)])
        P.cp("act", kvtok[:, 4 * t:4 * t + 4, :], ps[bt][:].rearrange("p (j r) -> p j r", j=4), [PS(bt)], [("kvtok", t)])
        s1, s2 = rope_apply(bkr, None, 1.0)
        P.tt("dve", krT[0:64, t * T:(t + 1) * T], scr[s1][0:64, 0:512], scr[s2][0:64, 0:512], ALU.add,
             [SC(s1), SC(s2)], [("krT", t)])
        P.tt("pool", krT[64:128, t * T:(t + 1) * T], scr[s1][0:64, 0:512], scr[s2][0:64, 0:512], ALU.add,
             [SC(s1), SC(s2)], [("krT", t)])
        zgroup(2)
        q_nope(0)
        q_rope(0)
        q_lat(0)
        zgroup(3)
        for h in range(16):
            qi = h % 2
            bo, bm = 3 + qi, 5 + qi
            kbs = [(4 * t + kk, kk * 128, True) for kk in range(4)] + [(kb, 0, False) for kb in range(4 * t)]
            nk = len(kbs)
            hooks = {}
            pre = []
            if h >= 1:
                (hooks.__setitem__(min(nk - 1, 5), lambda h=h: o_proj(h - 1)) if HOOKS & 8 else pre.append(lambda h=h: o_proj(h - 1)))
            if h + 1 < 16:
                (hooks.__setitem__(0, lambda h=h: q_nope(h + 1)) if HOOKS & 1 else pre.append(lambda h=h: q_nope(h + 1)))
                (hooks.__setitem__(1, lambda h=h: q_rope(h + 1)) if HOOKS & 2 else pre.append(lambda h=h: q_rope(h + 1)))
                (hooks.__setitem__(2 if nk < 8 else 3, lambda h=h: q_lat(h + 1)) if HOOKS & 4 else pre.append(lambda h=h: q_lat(h + 1)))
            for f_ in pre:
                f_()
            pts = {}
            sbk = {}
            npair = nk // 2
            for p_ in range(npair + 1):
                if p_ < npair:
                    blk = [2 * p_, 2 * p_ + 1]
                    for n_ in blk:
                        kb, lo, diag = kbs[n_]
                        sbk[n_] = state["sb"] = (state["sb"] + 1) % 3
                        bs_ = sbk[n_]
                        pts[n_] = state["pt"] = (state["pt"] + 1) % NPT
                        P.mm(ps[bs_][:, lo:512], kvT[:, kb * 128:(kb + 1) * 128], qlat[qi][:, lo:512], True, False,
                             [("kvT", kb // 4), ("qlat", qi)], [PS(bs_)])
                    for i_, n_ in enumerate(blk):
                        kb, lo, diag = kbs[n_]
                        bs_ = sbk[n_]
                        r0 = 64 * i_
                        P.mm(ps[bs_][:, lo:512], krT[r0:r0 + 64, kb * 128:(kb + 1) * 128], qrope[qi][r0:r0 + 64, lo:512],
                             False, True, [("krT", kb // 4), ("qrope", qi)], [PS(bs_)])
                    for n_ in blk:
                        kb, lo, diag = kbs[n_]
                        bs_ = sbk[n_]
                        pi_ = pts[n_]
                        P.act(pT[pi_][:, lo:512], ps[bs_][:, lo:512], AF.Exp, [PS(bs_)], [("pT", pi_)])
                        if diag:
                            P.ms("pool", pT[pi_][64:128, lo:lo + 64], 0.0, [("pT", pi_)])
                if p_ >= 1:
                    for m_ in (2 * p_ - 2, 2 * p_ - 1):
                        kb, lo, diag = kbs[m_]
                        pi_ = pts[m_]
                        tk = kb // 4
                        P.mm(ps[bo][:, lo:512], kvtok[:, kb, :], pT[pi_][:, lo:512], m_ == 0, m_ == nk - 1,
                             [("kvtok", tk), ("pT", pi_)], [PS(bo)])
                        P.mm(ps[bm][:, lo:512], onesb[:], pT[pi_][:, lo:512], m_ == 0, m_ == nk - 1,
                             ["onesb", ("pT", pi_)], [PS(bm)])
                        if m_ in hooks:
                            hooks[m_]()
                        if m_ + 0.5 in hooks:
                            hooks[m_ + 0.5]()
            o_norm(h)
        o_proj(15)
        out_proj_and_norm("owout", 1, xb, "xb", t)
        P.dma("xout", out_d[t * T:(t + 1) * T, :].rearrange("(j p) d -> p j d", p=128), xb[:],
              r=[("xb", j) for j in range(4)])

    load_x(0)
    for t in range(NT):
        layer0(t)
        layer1(t)
    stats = P.emit()
    return nc, stats


def _host_inputs(inp, b, NT=8):
    S = NT * T
    f = lambda a: np.ascontiguousarray(a, dtype=np.float32)
    fm = lambda v, n: np.asarray(v, np.float32).reshape(n, 128).T
    pp = np.zeros((128, PP_N), np.float32)
    pp[:, PP_C:PP_C + 8] = fm(inp["c"][b], 8)
    for l in range(2):
        pp[:, PP_ADAB + 16 * l:PP_ADAB + 16 * l + 16] = fm(inp["ada_b"][l, :2048], 16)
    pp[:, PP_GNG:PP_GNG + 2] = fm(inp["gmlp_norm_g"][0], 2)
    pp[:, PP_GNB:PP_GNB + 2] = fm(inp["gmlp_norm_b"][0], 2)
    pp[:, PP_PB:PP_PB + 8] = fm(inp["pool_b"][0], 8)
    pp[:, PP_PS:PP_PS + 8] = fm(inp["pool_scale"][0], 8)
    pp[:, PP_QNG:PP_QNG + 2] = fm(inp["mla_q_norm_g"][0], 2)
    pp[:, PP_KVG:PP_KVG + 1] = fm(inp["mla_kv_norm_g"][0], 1)
    inv = (1.0 / (np.float32(10000.0) ** (np.arange(0, 64, 2, dtype=np.float32) / np.float32(64)))).astype(np.float32)
    pp[:, PP_INV] = np.tile(inv, 4)
    pp[:, PP_PH] = np.where(np.arange(128) < 64, np.float32(np.pi / 2), np.float32(0.0))
    rows = np.concatenate([np.asarray(inp["gmlp_norm_b"][0], np.float32).reshape(-1),
                           np.asarray(inp["gmlp_bs"][0], np.float32).reshape(-1)])[None, :]
    return {
        "x": f(inp["x"][b, :S]),
        "pos": np.ascontiguousarray(inp["positions"][b:b + 1, :S], dtype=np.int32),
        "pp": pp, "rows": f(rows),
        "ada_w": f(inp["ada_w"]), "ada_bg": f(inp["ada_b"][:, 2048:]),
        "ln_g": f(inp["ln_g"]), "ln_b": f(inp["ln_b"]),
        "e_w_in": f(inp["e_w_in"][0]), "gmlp_ws": f(inp["gmlp_ws"][0]), "pool_w": f(inp["pool_w"][0]),
        "e_w_out": f(inp["e_w_out"][0]), "o_w_in": f(inp["o_w_in"][0]),
        "w_uq": f(np.asarray(inp["mla_w_uq"][0]).reshape(256, 16 * 192)),
        "w_uk": f(np.asarray(inp["mla_w_uk"][0]).reshape(128, 16 * 128)),
        "w_uv": f(np.asarray(inp["mla_w_uv"][0]).reshape(128, 16 * 128)),
        "o_w_out": f(inp["o_w_out"][0]),
    }


_CACHE = {}


def kernel(**inputs):
    inp = {k: np.asarray(v) for k, v in inputs.items()}
    if "nc" not in _CACHE:
        _CACHE["nc"] = build_program(8)[0]
    nc = _CACHE["nc"]
    in_maps = [_host_inputs(inp, b) for b in range(NCORES)]
    res = run_bass_kernel_spmd(nc, in_maps, core_ids=list(range(NCORES)))
    return np.stack([np.asarray(r["out"], dtype=np.float32) for r in res.results], axis=0)
```

```python
import numpy as np
import concourse.bass as bass
import concourse.mybir as mybir
from concourse.bass_utils import run_bass_kernel_spmd

F32 = mybir.dt.float32
BF16 = mybir.dt.bfloat16
I32 = mybir.dt.int32
AF = mybir.ActivationFunctionType
ALU = mybir.AluOpType

D = 1024
SEQ = 4096
T = 512
NCORES = 8
ALPHA = float((2.0 * 2) ** 0.25)
LN_EPS = 1e-5
ATTN_SCALE = float(192 ** -0.5)
POOL_WINDOWS = (2, 4, 8, 16)
TWO_PI = float(2 * np.pi)
PI = float(np.pi)
C1 = 6.28125
C2 = TWO_PI - C1

PP_C, PP_ADAB, PP_GNG, PP_GNB, PP_PB, PP_PS, PP_QNG, PP_KVG, PP_INV, PP_PH, PP_N = 0, 8, 40, 42, 44, 52, 60, 62, 63, 64, 65


class Prog:
    ENG = ("pe", "act", "dve", "pool", "sp")

    def __init__(self, nc):
        self.nc = nc
        self.e = {"pe": nc.tensor, "act": nc.scalar, "dve": nc.vector, "pool": nc.gpsimd, "sp": nc.sync}
        self.ops = []

    def op(self, eng, fn, r=(), w=()):
        self.ops.append(dict(eng=eng, fn=fn, r=tuple(r), w=tuple(w), dma=None))

    def dma(self, key, out, in_, r=(), w=(), eng="sp", **kw):
        self.ops.append(dict(eng=eng, fn=lambda E: E.dma_start(out=out, in_=in_, **kw),
                             r=tuple(r), w=tuple(w), dma=key))

    def mm(self, out, lhsT, rhs, start, stop, r, w):
        self.op("pe", lambda E: E.matmul(out, lhsT, rhs, start=start, stop=stop), r, w)

    def tr(self, out, in_, ident, r, w):
        self.op("pe", lambda E: E.transpose(out, in_, ident), r, w)

    def act(self, out, in_, func, r, w, bias=None, scale=None):
        kw = {}
        if bias is not None:
            kw["bias"] = bias
        if scale is not None:
            kw["scale"] = scale
        self.op("act", lambda E: E.activation(out, in_, func, **kw), r, w)

    def tt(self, eng, out, a, b, op, r, w):
        self.op(eng, lambda E: E.tensor_tensor(out, a, b, op), r, w)

    def ts(self, eng, out, a, s1, s2, op0, op1, r, w):
        if s2 is None:
            self.op(eng, lambda E: E.tensor_scalar(out, a, s1, None, op0), r, w)
        else:
            self.op(eng, lambda E: E.tensor_scalar(out, a, s1, s2, op0, op1), r, w)

    def stt(self, eng, out, a, s, b, op0, op1, r, w):
        self.op(eng, lambda E: E.scalar_tensor_tensor(out, a, s, b, op0, op1), r, w)

    def cp(self, eng, out, a, r, w):
        if eng == "act":
            self.op("act", lambda E: E.copy(out, a), r, w)
        else:
            self.op(eng, lambda E: E.tensor_copy(out, a), r, w)

    def ms(self, eng, out, val, w):
        self.op(eng, lambda E: E.memset(out, val), (), w)

    def emit(self):
        nc = self.nc
        ops = self.ops
        last_w, readers, last_dma = {}, {}, {}
        for i, o in enumerate(ops):
            deps = set()
            for res in o["r"]:
                if res in last_w:
                    deps.add(last_w[res])
            for res in o["w"]:
                if res in last_w:
                    deps.add(last_w[res])
                deps.update(readers.get(res, ()))
            if o["dma"] is not None and o["dma"] in last_dma:
                deps.add(last_dma[o["dma"]])
            deps.discard(i)
            o["deps"] = deps
            for res in o["r"]:
                readers.setdefault(res, []).append(i)
            for res in o["w"]:
                last_w[res] = i
                readers[res] = []
            if o["dma"] is not None:
                last_dma[o["dma"]] = i

        def skip(p, o):
            return p["dma"] is None and o["dma"] is None and p["eng"] == "pe" and o["eng"] == "pe"

        for o in ops:
            o["need_inc"] = o["dma"] is not None
        for o in ops:
            latest = {}
            for j in o["deps"]:
                p = ops[j]
                if skip(p, o) or p["dma"] is not None:
                    continue
                latest[p["eng"]] = max(latest.get(p["eng"], -1), j)
            for j in latest.values():
                ops[j]["need_inc"] = True
        sem = {e: nc.alloc_semaphore("s_" + e) for e in self.ENG}
        cnt = {e: 0 for e in self.ENG}
        dsem, dcnt = {}, {}
        for o in ops:
            if o["dma"] is not None:
                k = o["dma"]
                if k not in dsem:
                    dsem[k] = nc.alloc_semaphore("d_%d" % len(dsem))
                    dcnt[k] = 0
                dcnt[k] += 16
                o["sem"], o["val"] = ("d", k), dcnt[k]
            elif o["need_inc"]:
                cnt[o["eng"]] += 1
                o["sem"], o["val"] = ("e", o["eng"]), cnt[o["eng"]]
        known = {e: {} for e in self.ENG}
        nwait = 0
        for o in ops:
            E = self.e[o["eng"]]
            need = {}
            latest = {}
            for j in o["deps"]:
                p = ops[j]
                if skip(p, o):
                    continue
                if p["dma"] is not None:
                    need[p["sem"]] = max(need.get(p["sem"], 0), p["val"])
                else:
                    latest[p["eng"]] = max(latest.get(p["eng"], -1), j)
            for j in latest.values():
                p = ops[j]
                need[p["sem"]] = max(need.get(p["sem"], 0), p["val"])
            kn = known[o["eng"]]
            for s, v in need.items():
                if kn.get(s, 0) >= v:
                    continue
                kn[s] = v
                E.wait_ge(dsem[s[1]] if s[0] == "d" else sem[s[1]], v)
                nwait += 1
            ins = o["fn"](E)
            if o["dma"] is not None:
                ins.then_inc(dsem[o["dma"]], 16)
            elif o["need_inc"]:
                ins.then_inc(sem[o["eng"]], 1)
        for k, h in dsem.items():
            nc.sync.wait_ge(h, dcnt[k])
        self.stats = dict(n_ops=len(ops), n_wait=nwait, cnt=dict(cnt), n_dma_keys=len(dsem))
        return self.stats


def build_program(NT=8, debug_x1=False):
    S = NT * T
    nc = bass.Bass("TRN2", target_bir_lowering=False)
    dt = lambda name, shape, dtype=F32, kind="ExternalInput": nc.dram_tensor(name, list(shape), dtype, kind=kind).ap()
    x_d = dt("x", [S, D])
    pos_d = dt("pos", [1, S], I32)
    pp_d = dt("pp", [128, PP_N])
    rows_d = dt("rows", [1, 256 + 512])
    adaw_d = dt("ada_w", [2, D, 3 * D])
    adabg_d = dt("ada_bg", [2, D])
    lng_d = dt("ln_g", [2, D])
    lnb_d = dt("ln_b", [2, D])
    ewin_d = dt("e_w_in", [D, 5 * D])
    ws_d = dt("gmlp_ws", [4, 128, 128])
    poolw_d = dt("pool_w", [4, 256, 256])
    ewout_d = dt("e_w_out", [2 * D, D])
    owin_d = dt("o_w_in", [D, 2496])
    wuq_d = dt("w_uq", [256, 16 * 192])
    wuk_d = dt("w_uk", [128, 16 * 128])
    wuv_d = dt("w_uv", [128, 16 * 128])
    owout_d = dt("o_w_out", [2 * D, D])
    out_d = dt("out", [S, D], F32, "ExternalOutput")
    x1_d = dt("x1dbg", [S, D], F32, "ExternalOutput") if debug_x1 else None

    P = Prog(nc)
    sb = lambda name, shape, dtype=F32: nc.alloc_sbuf_tensor("sb_" + name, list(shape), dtype)

    xa = sb("xa", [128, 4, D])
    xb = sb("xb", [128, 4, D])
    gate1 = sb("gate1", [128, 2, D])
    lng = sb("lng", [128, 2, D])
    lnb = sb("lnb", [128, 2, D])
    ident = sb("ident", [128, 128])
    onesf = sb("onesf", [128, 128])
    onesb = sb("onesb", [128, 128], BF16)
    pp = sb("pp", [128, PP_N])
    cond = sb("cond", [128, 8])
    modT = sb("modT", [128, 2, 16])
    hT = sb("hT", [128, 8, T], BF16)
    mixT = sb("mixT", [128, 16, T], BF16)
    NSLOT = 4
    wsl = [sb("wsl%d" % i, [128, 8, 512], BF16) for i in range(NSLOT)]
    wukT = sb("wukT", [128, 16, 128], BF16)
    wuv = sb("wuv", [128, 16, 128], BF16)
    poolw = sb("poolw", [128, 4, 2, 256], BF16)
    WsT = sb("WsT", [128, 4, 128], BF16)
    Cmat = sb("Cmat", [128, 4, 2, 128])
    rcw = sb("rcw", [128, 4, 16])
    halo = sb("halo", [128, 8, 16])
    vn = sb("vn", [128, 4, D], BF16)
    wq = [vn[:, 2 * i:2 * i + 2, :].rearrange("p a (b n) -> p a b n", b=4) for i in range(2)]
    WQ = [[("vn", j, h) for j in (2 * i, 2 * i + 1) for h in range(4)] for i in range(2)]
    NSCR = 8
    scr = [sb("scr%d" % i, [128, 528]) for i in range(NSCR)]
    pl = sb("pl", [128, 2, 2, T], BF16)
    szb = sb("szb", [128, 2, T])
    small = sb("small", [128, 128])
    qcn = sb("qcn", [128, 2, T], BF16)
    kvT = sb("kvT", [128, S], BF16)
    kvtok = sb("kvtok", [128, S // 128, 128], BF16)
    krT = sb("krT", [128, S], BF16)
    cs = sb("cs", [128, T])
    posi = sb("posi", [128, T], I32)
    qn = sb("qn", [128, T], BF16)
    qlat = [sb("qlat%d" % i, [128, T], BF16) for i in range(2)]
    qrope = [sb("qrope%d" % i, [128, T], BF16) for i in range(2)]
    olat = [sb("olat%d" % i, [128, T], BF16) for i in range(2)]
    NPT = 4
    pT = [sb("pT%d" % i, [128, T], BF16) for i in range(NPT)]
    ps = [nc.alloc_psum_tensor("ps%d" % i, [128, 512], F32) for i in range(8)]
    PS = lambda i: ("ps", i)
    state = dict(bank=0, scr=0, slot=0, pt=0, sb=0)

    def nbank(lo=0, hi=8):
        b = state["bank"]
        b = lo + ((b - lo + 1) % (hi - lo)) if lo <= b < hi else lo
        state["bank"] = b
        return b

    def nscr():
        state["scr"] = (state["scr"] + 1) % NSCR
        return state["scr"]

    SC = lambda i: ("scr", i)

    def kview(w2d, c0, ncols, k0=0, nk=8):
        return w2d[k0 * 128:(k0 + nk) * 128, c0:c0 + ncols].rearrange("(kc p) n -> p kc n", p=128)

    wlist = []
    for t in range(NT):
        for g in (2, 3, 0, 4, 1, 5, 6, 8, 7, 9):
            wlist.append(("ewin", g, kview(ewin_d, g * 512, 512), 512))
        for hf in range(2):
            for kh in range(2):
                wlist.append(("ewout", (hf, kh), kview(ewout_d, hf * 512, 512, kh * 8), 512))
        wlist.append(("owinA", 0, kview(owin_d, 0, 448), 448))
        for g in range(4):
            wlist.append(("owinZ", g, kview(owin_d, 448 + g * 512, 512), 512))
        for hf in range(2):
            for kh in range(2):
                wlist.append(("owout", (hf, kh), kview(owout_d, hf * 512, 512, kh * 8), 512))
    wstate = dict(issued=0, used=0, released=0)

    NG = len(wlist) // NT
    wscr = nc.dram_tensor("wscr", [NG, 128, 8 * 512], BF16, kind="Internal").ap()

    def w_pump():
        while wstate["issued"] < len(wlist) and wstate["issued"] - NSLOT < wstate["released"]:
            i = wstate["issued"]
            kind, g, src, ncols = wlist[i]
            s_ = i % NSLOT
            gid = i % NG
            scr_v = wscr[gid].rearrange("p (k n) -> p k n", k=8)[:, :, 0:ncols]
            if i < NG:
                P.dma(("wp", s_), wsl[s_][:, :, 0:ncols], src, w=[("wsl", s_)], eng="pool")
                if kind in ("ewout", "owout"):
                    l_ = 0 if kind == "ewout" else 1
                    hf_ = g[0]
                    P.tt("dve", wsl[s_][:], wsl[s_][:],
                         gate1[:, l_, hf_ * 512:(hf_ + 1) * 512].unsqueeze(1).to_broadcast([128, 8, 512]), ALU.mult,
                         [("wsl", s_), ("gate1", l_)], [("wsl", s_)])
                if NT > 1:
                    P.dma(("wst", s_), scr_v, wsl[s_][:, :, 0:ncols], r=[("wsl", s_)], w=[("wscr", gid)])
            else:
                P.dma(("w", s_), wsl[s_][:, :, 0:ncols], scr_v, r=[("wscr", gid)], w=[("wsl", s_)])
            wstate["issued"] += 1

    def w_next(kind):
        i = wstate["used"]
        assert wlist[i][0] == kind, (wlist[i][0], kind)
        w_pump()
        assert wstate["issued"] > i
        wstate["used"] += 1
        return wsl[i % NSLOT], ("wsl", i % NSLOT)

    def w_release(n=1):
        wstate["released"] += n
        w_pump()

    P.dma("c0", pp[:], pp_d, w=["pp"])
    for l in range(2):
        P.dma(("cg", l), gate1[:, l, :], adabg_d[l:l + 1, :].partition_broadcast(128), w=[("gate1", l)])
        P.dma(("cl", l), lng[:, l, :], lng_d[l:l + 1, :].partition_broadcast(128), w=[("lng", l)])
        P.dma(("cb", l), lnb[:, l, :], lnb_d[l:l + 1, :].partition_broadcast(128), w=[("lnb", l)])
    P.op("pool", lambda E: E.iota(ident[:], [[1, 128]], base=0, channel_multiplier=-1,
                                    allow_small_or_imprecise_dtypes=True), w=["ident"])
    P.op("pool", lambda E: E.tensor_single_scalar(ident[:], ident[:], 0.0, ALU.is_equal), r=["ident"], w=["ident"])
    P.ms("pool", onesf[:], 1.0, ["onesf"])
    P.ms("pool", onesb[:], 1.0, ["onesb"])
    P.ms("pool", halo[:], 0.0, [("halo", c) for c in range(8)])
    P.op("pool", lambda E: E.iota(rcw[:, 0, :], [[1, 16]], base=1, channel_multiplier=0,
                                    allow_small_or_imprecise_dtypes=True), w=["rcw"])
    for g in (3, 2, 1):
        P.cp("pool", rcw[:, g, :], rcw[:, 0, :], ["rcw"], ["rcw"])
    for g, win in enumerate(POOL_WINDOWS):
        P.ts("dve", rcw[:, g, :], rcw[:, g, :], float(win), None, ALU.min, None, ["rcw"], ["rcw"])
    P.op("dve", lambda E: E.reciprocal(rcw[:], rcw[:]), ["rcw"], ["rcw"])
    P.act(cond[:], pp[:, PP_C:PP_C + 8], AF.Silu, ["pp"], ["cond"])

    for kc in range(8):
        P.cp("dve", scr[kc][:, 0:128], cond[:, kc:kc + 1].to_broadcast([128, 128]), ["cond"], [SC(kc)])
    stg = [xa[:].rearrange("p j (a n) -> p (j a) n", a=2), xb[:].rearrange("p j (a n) -> p (j a) n", a=2)]
    STG = [[("xa", j) for j in range(4)], [("xb", j) for j in range(4)]]
    sti = 0
    for l in range(2):
        bmod = nbank()
        for blk in range(4):
            s_ = sti % 2
            sti += 1
            P.dma(("stg", s_), stg[s_], kview(adaw_d[l], blk * 512, 512), w=STG[s_])
            for fc in range(4):
                col = blk * 4 + fc
                for kc in range(8):
                    P.mm(ps[bmod][:, col:col + 1], stg[s_][:, kc, fc * 128:(fc + 1) * 128], cond[:, kc:kc + 1],
                         kc == 0, kc == 7, STG[s_] + ["cond"], [PS(bmod)])
        P.tt("dve", modT[:, l, :], ps[bmod][:, 0:16], pp[:, PP_ADAB + 16 * l:PP_ADAB + 16 * l + 16], ALU.add,
             [PS(bmod), "pp"], [("modT", l)])
        P.ts("dve", modT[:, l, 8:16], modT[:, l, 8:16], 1.0, None, ALU.add, None, [("modT", l)], [("modT", l)])
        for hf in range(2):
            s_ = sti % 2
            sti += 1
            bg = nbank()
            P.dma(("stg", s_), stg[s_], kview(adaw_d[l], 2048 + hf * 512, 512), w=STG[s_])
            for kc in range(8):
                P.mm(ps[bg][:], scr[kc][:, 0:128], stg[s_][:, kc, :], kc == 0, kc == 7, STG[s_] + [SC(kc)], [PS(bg)])
            P.stt("dve", gate1[:, l, hf * 512:(hf + 1) * 512], ps[bg][:], 1.0, gate1[:, l, hf * 512:(hf + 1) * 512],
                  ALU.add, ALU.add, [PS(bg), ("gate1", l)], [("gate1", l)])

    wstg = stg[0]
    P.dma(("stg", 0), wstg[:, 0:4, 0:128], ws_d.rearrange("h t s -> t h s"), w=STG[0])
    wsf = stg[1]
    rows = scr[0][0:1, 0:256], scr[1][0:1, 0:512]
    P.dma("c1", rows[0], rows_d[0:1, 0:256], w=[SC(0)])
    P.dma("c2", rows[1], rows_d[0:1, 256:768], w=[SC(1)])
    rsrow = scr[2][0:1, 0:512]
    brs = nbank()
    for h in range(4):
        b_ = nbank()
        P.tr(ps[b_][:, 0:128], wstg[:, h, 0:128], ident[:], STG[0] + ["ident"], [PS(b_)])
        P.cp("dve", wsf[:, h, 0:128], ps[b_][:, 0:128], [PS(b_)], STG[1])
        P.ms("pool", wsf[64:128, h, 0:64], 0.0, STG[1])
        P.cp("act", WsT[:, h, :], wsf[:, h, 0:128], STG[1], ["WsT"])
        P.mm(ps[brs][0:1, h * 128:(h + 1) * 128], onesf[:, 0:1], wsf[:, h, 0:128], True, True,
             STG[1] + ["onesf"], [PS(brs)])
    P.cp("dve", rsrow, ps[brs][0:1, :], [PS(brs)], [SC(2)])
    for h in range(4):
        for dc in range(2):
            b_ = nbank()
            P.mm(ps[b_][:, 0:128], rows[0][0:1, dc * 128:(dc + 1) * 128], rsrow[0:1, h * 128:(h + 1) * 128], True, False,
                 [SC(0), SC(2)], [PS(b_)])
            P.mm(ps[b_][:, 0:128], onesf[0:1, :], rows[1][0:1, h * 128:(h + 1) * 128], False, True,
                 [SC(1), "onesf"], [PS(b_)])
            P.cp("dve", Cmat[:, h, dc, :], ps[b_][:, 0:128], [PS(b_)], ["Cmat"])
    P.dma("cw0", wuv[:], wuv_d.rearrange("r (h d) -> r h d", h=16), w=["wuv"], eng="pool")
    P.dma("cw1", poolw[:], poolw_d.rearrange("g (dc p) e -> p g dc e", p=128), w=["poolw"], eng="pool")
    P.dma(("stg", 1), stg[1][:, 0:4, :], wuk_d.rearrange("r (a n) -> r a n", a=4), w=STG[1])
    for h in range(16):
        b_ = nbank()
        P.tr(ps[b_][:, 0:128], stg[1][:, h // 4, (h % 4) * 128:(h % 4 + 1) * 128], ident[:], STG[1] + ["ident"], [PS(b_)])
        P.cp("dve" if h % 2 else "act", wukT[:, h, :], ps[b_][:, 0:128], [PS(b_)], ["wukT"])

    def make_hT(src, l, src_res):
        order = [(state["bank"] + 1 + i) % 8 for i in range(8)]
        state["bank"] = order[-1]
        for c in range(8):
            bk = order[c]
            for j in range(3):
                P.tr(ps[bk][:, j * 128:(j + 1) * 128], src[:, j, c * 128:(c + 1) * 128], ident[:],
                     [(src_res, j), "ident"], [PS(bk)])
        for c in range(8):
            bk = order[c]
            P.tr(ps[bk][:, 384:512], src[:, 3, c * 128:(c + 1) * 128], ident[:], [(src_res, 3), "ident"], [PS(bk)])
            if c % 2 == 0:
                P.act(hT[:, c, :], ps[bk][:], AF.Identity, [PS(bk), ("modT", l)], [("hT", c)],
                      bias=modT[:, l, c:c + 1], scale=modT[:, l, 8 + c:9 + c])
            else:
                P.ts("dve", hT[:, c, :], ps[bk][:], modT[:, l, 8 + c:9 + c], modT[:, l, c:c + 1], ALU.mult, ALU.add,
                     [PS(bk), ("modT", l)], [("hT", c)])

    HT_ALL = [("hT", c) for c in range(8)]
    MIX_ALL = [("mixT", c) for c in range(16)]

    def out_proj_and_norm(kind, l, res_src, res_name, t):
        for j in range(4):
            P.act(xb[:, j, :], res_src[:, j, :], AF.Identity, [(res_name, j)], [("xb", j)], scale=ALPHA)
        wg = [w_next(kind) for _ in range(4)]
        for j in range(4):
            for hf in range(2):
                b_ = nbank()
                for kc in range(16):
                    wt, rr = wg[2 * hf + (kc // 8)]
                    P.mm(ps[b_][:], mixT[:, kc, j * 128:(j + 1) * 128], wt[:, kc % 8, :], kc == 0, kc == 15,
                         [("mixT", kc), rr], [PS(b_)])
                P.tt("dve", xb[:, j, hf * 512:(hf + 1) * 512], ps[b_][:], xb[:, j, hf * 512:(hf + 1) * 512], ALU.add,
                     [PS(b_), ("xb", j)], [("xb", j)])
            c0 = 64 + 16 * j
            P.op("dve", lambda E, j=j, c0=c0: E.bn_stats(small[:, c0:c0 + 6], xb[:, j, 0:512]), [("xb", j)], [("st", j)])
            P.op("dve", lambda E, j=j, c0=c0: E.bn_stats(small[:, c0 + 6:c0 + 12], xb[:, j, 512:1024]), [("xb", j)], [("st", j)])
            P.op("dve", lambda E, c0=c0: E.bn_aggr(small[:, c0 + 12:c0 + 14], small[:, c0:c0 + 12]), [("st", j)], [("mv", j)])
            P.act(small[:, c0 + 14:c0 + 15], small[:, c0 + 13:c0 + 14], AF.Ln, [("mv", j)], [("rstd", j)], bias=LN_EPS)
            P.act(small[:, c0 + 14:c0 + 15], small[:, c0 + 14:c0 + 15], AF.Exp, [("rstd", j)], [("rstd", j)], scale=-0.5)
            P.stt("dve", small[:, c0 + 15:c0 + 16], small[:, c0 + 12:c0 + 13], -1.0, small[:, c0 + 14:c0 + 15], ALU.mult, ALU.mult,
                  [("mv", j), ("rstd", j)], [("nb", j)])
            P.act(xb[:, j, :], xb[:, j, :], AF.Identity, [("xb", j), ("rstd", j), ("nb", j)], [("xb", j)],
                  bias=small[:, c0 + 15:c0 + 16], scale=small[:, c0 + 14:c0 + 15])
            P.tt("pool", xb[:, j, :], xb[:, j, :], lng[:, l, :], ALU.mult, [("xb", j), ("lng", l)], [("xb", j)])
            P.tt("dve", xb[:, j, :], xb[:, j, :], lnb[:, l, :], ALU.add, [("xb", j), ("lnb", l)], [("xb", j)])
        w_release(4)

    def layer0(t):
        make_hT(xa, 0, "xa")
        for hf in range(2):
            wt, wr = w_next("ewin")
            for j in range(4):
                b_ = nbank()
                for kc in range(8):
                    P.mm(ps[b_][:], hT[:, kc, j * 128:(j + 1) * 128], wt[:, kc, :], kc == 0, kc == 7,
                         [("hT", kc), wr], [PS(b_)])
                st = small[:, 16:28].rearrange("p (a b) -> p a b", a=2)
                mv = small[:, 28:32].rearrange("p (a b) -> p a b", a=2)
                for i in range(2):
                    P.op("dve", lambda E, i=i, b_=b_: E.bn_stats(st[:, i, :], ps[b_][:, i * 256:(i + 1) * 256]),
                         [PS(b_)], ["vst"])
                    P.op("dve", lambda E, i=i: E.bn_aggr(mv[:, i, :], st[:, i, :]), ["vst"], ["vmv"])
                rs_ = small[:, 32:34]
                nb_ = small[:, 34:36]
                P.act(rs_, mv[:, :, 1], AF.Ln, ["vmv"], ["vrs"], bias=LN_EPS)
                P.act(rs_, rs_, AF.Exp, ["vrs"], ["vrs"], scale=-0.5)
                P.stt("dve", nb_, mv[:, :, 0], -1.0, rs_, ALU.mult, ALU.mult, ["vmv", "vrs"], ["vnb"])
                for i in range(2):
                    h = hf * 2 + i
                    P.act(vn[:, j, h * 256:(h + 1) * 256], ps[b_][:, i * 256:(i + 1) * 256], AF.Identity,
                          [PS(b_), "vrs", "vnb"], [("vn", j, h)], bias=small[:, 34 + i:35 + i], scale=small[:, 32 + i:33 + i])
            w_release()
        for hp in range(2):
            wu, ru = w_next("ewin")
            wz, rz = w_next("ewin")
            for hh in range(2):
                h = hp * 2 + hh
                for dc in range(2):
                    cl = hh * 2 + dc
                    bu, bz, bs_ = nbank(), nbank(), nbank()
                    for kc in range(8):
                        P.mm(ps[bz][:], wz[:, kc, cl * 128:(cl + 1) * 128], hT[:, kc, :], kc == 0, kc == 7,
                             [("hT", kc), rz], [PS(bz)])
                    for j in range(4):
                        P.mm(ps[bs_][:, j * 128:(j + 1) * 128], vn[:, j, h * 256 + dc * 128:h * 256 + (dc + 1) * 128],
                             WsT[:, h, :], True, True, [("vn", j, h), "WsT"], [PS(bs_)])
                    for kc in range(8):
                        P.mm(ps[bu][:], wu[:, kc, cl * 128:(cl + 1) * 128], hT[:, kc, :], kc == 0, kc == 7,
                             [("hT", kc), ru], [PS(bu)])
                    s1, s2 = nscr(), nscr()
                    P.act(scr[s1][:, 0:512], ps[bz][:], AF.Silu, [PS(bz)], [SC(s1)])
                    P.stt("dve", scr[s2][:, 0:512].rearrange("p (j t) -> p j t", j=4),
                          ps[bs_][:].rearrange("p (j t) -> p j t", j=4), pp[:, PP_GNG + dc:PP_GNG + dc + 1],
                          Cmat[:, h, dc, :].unsqueeze(1).to_broadcast([128, 4, 128]), ALU.mult, ALU.add,
                          [PS(bs_), "pp", "Cmat"], [SC(s2)])
                    P.tt("pool", scr[s2][:, 0:512], scr[s2][:, 0:512], scr[s1][:, 0:512], ALU.mult,
                         [SC(s1), SC(s2)], [SC(s2)])
                    P.tt("dve", mixT[:, 2 * h + dc, :], ps[bu][:], scr[s2][:, 0:512], ALU.mult,
                         [PS(bu), SC(s2)], [("mixT", 2 * h + dc)])
            w_release(2)
        pw = {}

        def pool_x(g):
            wx, rx = pw["x%d" % (g // 2)]
            gg = g % 2
            win = POOL_WINDOWS[g]
            nstep = int(np.log2(win))
            for dc in range(2):
                cl = gg * 2 + dc
                ch = 2 * g + dc
                bx = nbank()
                for kc in range(8):
                    P.mm(ps[bx][:], wx[:, kc, cl * 128:(cl + 1) * 128], hT[:, kc, :], kc == 0, kc == 7,
                         [("hT", kc), rx], [PS(bx)])
                sx = nscr()
                P.cp("pool", scr[sx][:, 0:16], halo[:, ch, :], [("halo", ch)], [SC(sx)])
                P.cp("act", scr[sx][:, 16:528], ps[bx][:], [PS(bx)], [SC(sx)])
                P.cp("pool", halo[:, ch, :], scr[sx][:, 512:528], [SC(sx)], [("halo", ch)])
                cur = sx
                for k in range(nstep):
                    sh = 1 << k
                    lo = (1 << (k + 1)) - 1
                    nx = nscr()
                    P.tt("pool", scr[nx][:, lo:528], scr[cur][:, lo:528], scr[cur][:, lo - sh:528 - sh], ALU.add,
                         [SC(cur)], [SC(nx)])
                    cur = nx
                P.stt("dve", pl[:, g % 2, dc, :], scr[cur][:, 16:528], 1.0 / win, scr[sx][:, 16:528], ALU.mult, ALU.subtract,
                      [SC(cur), SC(sx)], [("pl", g % 2, dc)])
                if t == 0:
                    s3 = nscr()
                    P.tt("dve", scr[s3][:, 0:16], scr[cur][:, 16:32], rcw[:, g, :], ALU.mult, [SC(cur), "rcw"], [SC(s3)])
                    P.tt("dve", pl[:, g % 2, dc, 0:16], scr[s3][:, 0:16], scr[sx][:, 16:32], ALU.subtract,
                         [SC(s3), SC(sx)], [("pl", g % 2, dc)])

        def pool_zy(g):
            wz, rz = pw["z%d" % (g // 2)]
            gg = g % 2
            for ec in range(2):
                cl = gg * 2 + ec
                ch = 2 * g + ec
                by, bz = nbank(), nbank()
                for kc in range(8):
                    P.mm(ps[bz][:], wz[:, kc, cl * 128:(cl + 1) * 128], hT[:, kc, :], kc == 0, kc == 7,
                         [("hT", kc), rz], [PS(bz)])
                P.act(szb[:, ec, :], ps[bz][:], AF.Silu, [PS(bz)], [("szb", ec)])
                for dc in range(2):
                    P.mm(ps[by][:], poolw[:, g, dc, ec * 128:(ec + 1) * 128], pl[:, g % 2, dc, :], dc == 0, dc == 1,
                         [("pl", g % 2, dc), "poolw"], [PS(by)])
                s2 = nscr()
                P.ts("dve", scr[s2][:, 0:512], ps[by][:], pp[:, PP_PB + ch:PP_PB + ch + 1], pp[:, PP_PS + ch:PP_PS + ch + 1],
                     ALU.add, ALU.mult, [PS(by), "pp"], [SC(s2)])
                P.tt("pool", mixT[:, 8 + ch, :], scr[s2][:, 0:512], szb[:, ec, :], ALU.mult,
                     [("szb", ec), SC(s2)], [("mixT", 8 + ch)])

        pw["x0"] = w_next("ewin")
        pool_x(0)
        pool_x(1)
        w_release()
        pw["z0"] = w_next("ewin")
        pool_zy(0)
        pw["x1"] = w_next("ewin")
        pool_x(2)
        pool_zy(1)
        w_release()
        pool_x(3)
        w_release()
        pw["z1"] = w_next("ewin")
        pool_zy(2)
        pool_zy(3)
        w_release()
        out_proj_and_norm("ewout", 0, xa, "xa", t)
        if debug_x1:
            P.dma("x1o", x1_d[t * T:(t + 1) * T, :].rearrange("(j p) d -> p j d", p=128), xb[:],
                  r=[("xb", j) for j in range(4)])

    def rope_apply(bank, dst, scale):
        s1, s2 = nscr(), nscr()
        P.stt("dve", scr[s1][0:64, 0:512], ps[bank][0:64, :], scale, cs[0:64, :], ALU.mult, ALU.mult,
              [PS(bank), "cs"], [SC(s1)])
        P.stt("dve", scr[s2][0:64, 0:512], ps[bank][64:128, :], scale, cs[64:128, :], ALU.mult, ALU.mult,
              [PS(bank), "cs"], [SC(s2)])
        return s1, s2

    def load_x(t):
        P.dma("xin", xa[:], x_d[t * T:(t + 1) * T, :].rearrange("(j p) d -> p j d", p=128),
              w=[("xa", j) for j in range(4)])

    def layer1(t):
        if t + 1 < NT:
            load_x(t + 1)
        make_hT(xb, 1, "xb")

        def load_wq(g):
            wqi = g % 2
            for cc in range(2):
                P.dma(("wq", wqi, cc), wq[wqi][:, cc, :, 0:192],
                      wuq_d[cc * 128:(cc + 1) * 128, g * 768:(g + 1) * 768].rearrange("p (a n) -> p a n", a=4),
                      w=WQ[wqi], eng="pool")
            P.ts("pool", wq[wqi][:, :, :, 192:224], wq[wqi][:, :, :, 160:192], -1.0, None, ALU.mult, None,
                 WQ[wqi], WQ[wqi])
            P.cp("pool", wq[wqi][:, :, :, 224:256], wq[wqi][:, :, :, 128:160], WQ[wqi], WQ[wqi])

        load_wq(0)
        p_ = "posi"
        posi_ap = posi[:]
        P.dma("pos", posi_ap, pos_d[0:1, t * T:(t + 1) * T].partition_broadcast(128), w=["posi"])
        a_, k_, m_ = nscr(), nscr(), nscr()
        ang, kf, mk = scr[a_][:, 0:512], scr[k_][:, 0:512], scr[m_][:, 0:512]
        ki = posi_ap
        P.cp("dve", ang, posi_ap, ["posi"], [SC(a_)])
        P.ts("dve", ang, ang, pp[:, PP_INV:PP_INV + 1], None, ALU.mult, None, [SC(a_), "pp"], [SC(a_)])

        def wrap():
            P.ts("dve", mk, ang, PI, -TWO_PI, ALU.is_gt, ALU.mult, [SC(a_)], [SC(m_)])
            P.tt("dve", ang, ang, mk, ALU.add, [SC(a_), SC(m_)], [SC(a_)])
            P.ts("dve", mk, ang, -PI, TWO_PI, ALU.is_lt, ALU.mult, [SC(a_)], [SC(m_)])
            P.tt("dve", ang, ang, mk, ALU.add, [SC(a_), SC(m_)], [SC(a_)])

        P.ts("dve", kf, ang, 1.0 / TWO_PI, None, ALU.mult, None, [SC(a_)], [SC(k_)])
        P.cp("dve", ki, kf, [SC(k_), SC(a_)], ["posi"])
        P.cp("dve", kf, ki, ["posi"], [SC(k_)])
        P.stt("dve", ang, kf, -C1, ang, ALU.mult, ALU.add, [SC(k_), SC(a_)], [SC(a_)])
        P.stt("dve", ang, kf, -C2, ang, ALU.mult, ALU.add, [SC(k_), SC(a_)], [SC(a_)])
        wrap()
        P.ts("dve", ang, ang, pp[:, PP_PH:PP_PH + 1], None, ALU.add, None, [SC(a_), "pp"], [SC(a_)])
        wrap()
        P.act(cs[:], ang, AF.Sin, [SC(a_)], ["cs"])

        MB = 7
        HOOKS = 15

        def q_nope(h):
            wqi, hq = (h // 4) % 2, h % 4
            if hq == 0 and h + 4 < 16:
                load_wq(h // 4 + 1)
            for cc in range(2):
                P.mm(ps[MB][:], wq[wqi][:, cc, hq, 0:128], qcn[:, cc, :], cc == 0, cc == 1,
                     WQ[wqi] + [("qcn", cc)], [PS(MB)])
            P.cp("act", qn[:], ps[MB][:], [PS(MB)], ["qn"])

        def q_rope(h):
            wqi, hq, qi = (h // 4) % 2, h % 4, h % 2
            for cc in range(2):
                P.mm(ps[MB][:], wq[wqi][:, cc, hq, 128:256], qcn[:, cc, :], cc == 0, cc == 1,
                     WQ[wqi] + [("qcn", cc)], [PS(MB)])
            s1, s2 = rope_apply(MB, None, ATTN_SCALE)
            P.tt("dve", qrope[qi][0:64, :], scr[s1][0:64, 0:512], scr[s2][0:64, 0:512], ALU.add, [SC(s1), SC(s2)], [("qrope", qi)])
            P.tt("pool", qrope[qi][64:128, :], scr[s1][0:64, 0:512], scr[s2][0:64, 0:512], ALU.add, [SC(s1), SC(s2)], [("qrope", qi)])

        def q_lat(h):
            qi = h % 2
            P.mm(ps[MB][:], wukT[:, h, :], qn[:], True, True, ["wukT", "qn"], [PS(MB)])
            P.act(qlat[qi][:], ps[MB][:], AF.Identity, [PS(MB)], [("qlat", qi)], scale=ATTN_SCALE)

        def o_norm(h):
            qi = h % 2
            bo, bm = 3 + qi, 5 + qi
            rs_ = nscr()
            P.op("dve", lambda E, rs_=rs_, bm=bm: E.reciprocal(scr[rs_][:, 0:512], ps[bm][:]), [PS(bm)], [SC(rs_)])
            P.tt("dve", olat[qi][:], ps[bo][:], scr[rs_][:, 0:512], ALU.mult, [PS(bo), SC(rs_)], [("olat", qi)])

        def o_proj(h):
            qi = h % 2
            P.mm(ps[MB][:], wuv[:, h, :], olat[qi][:], True, True, ["wuv", ("olat", qi)], [PS(MB)])
            P.tt("dve", mixT[:, h, :], ps[MB][:], mixT[:, h, :], ALU.mult, [PS(MB), ("mixT", h)], [("mixT", h)])

        def zgroup(g):
            wz, rz = w_next("owinZ")
            for cl in range(4):
                h = g * 4 + cl
                b_ = 6 + (h % 2)
                for kc in range(8):
                    P.mm(ps[b_][:], wz[:, kc, cl * 128:(cl + 1) * 128], hT[:, kc, :], kc == 0, kc == 7,
                         [("hT", kc), rz], [PS(b_)])
                P.act(mixT[:, h, :], ps[b_][:], AF.Silu, [PS(b_)], [("mixT", h)])
            w_release()

        wa, ra = w_next("owinA")
        P.ts("pool", wa[:, :, 448:480], wa[:, :, 416:448], -1.0, None, ALU.mult, None, [ra], [ra])
        P.cp("pool", wa[:, :, 480:512], wa[:, :, 384:416], [ra], [ra])
        bq = [0, 1]
        bkv, bkr = 2, 3
        for cc in range(2):
            for kc in range(8):
                P.mm(ps[bq[cc]][:], wa[:, kc, cc * 128:(cc + 1) * 128], hT[:, kc, :], kc == 0, kc == 7,
                     [("hT", kc), ra], [PS(bq[cc])])
        for kc in range(8):
            P.mm(ps[bkv][:], wa[:, kc, 256:384], hT[:, kc, :], kc == 0, kc == 7, [("hT", kc), ra], [PS(bkv)])
        for kc in range(8):
            P.mm(ps[bkr][:], wa[:, kc, 384:512], hT[:, kc, :], kc == 0, kc == 7, [("hT", kc), ra], [PS(bkr)])
        w_release()
        sq = [nscr(), nscr(), nscr()]
        for cc in range(2):
            P.act(scr[sq[cc]][:, 0:512], ps[bq[cc]][:], AF.Square, [PS(bq[cc])], [SC(sq[cc])])
        P.act(scr[sq[2]][:, 0:512], ps[bkv][:], AF.Square, [PS(bkv)], [SC(sq[2])])
        zgroup(0)
        bsq, bsk = 4, 5
        for cc in range(2):
            P.mm(ps[bsq][:], onesf[:], scr[sq[cc]][:, 0:512], cc == 0, cc == 1, [SC(sq[cc]), "onesf"], [PS(bsq)])
        P.mm(ps[bsk][:], onesf[:], scr[sq[2]][:, 0:512], True, True, [SC(sq[2]), "onesf"], [PS(bsk)])
        rq, rk = nscr(), nscr()
        P.act(scr[rq][:, 0:512], ps[bsq][:], AF.Ln, [PS(bsq)], [SC(rq)], bias=LN_EPS, scale=1.0 / 256)
        P.act(scr[rq][:, 0:512], scr[rq][:, 0:512], AF.Exp, [SC(rq)], [SC(rq)], scale=-0.5)
        P.act(scr[rk][:, 0:512], ps[bsk][:], AF.Ln, [PS(bsk)], [SC(rk)], bias=LN_EPS, scale=1.0 / 128)
        P.act(scr[rk][:, 0:512], scr[rk][:, 0:512], AF.Exp, [SC(rk)], [SC(rk)], scale=-0.5)
        for cc in range(2):
            P.stt("dve", qcn[:, cc, :], ps[bq[cc]][:], pp[:, PP_QNG + cc:PP_QNG + cc + 1], scr[rq][:, 0:512],
                  ALU.mult, ALU.mult, [PS(bq[cc]), "pp", SC(rq)], [("qcn", cc)])
        zgroup(1)
        kvf = nscr()
        P.stt("dve", scr[kvf][:, 0:512], ps[bkv][:], pp[:, PP_KVG:PP_KVG + 1], scr[rk][:, 0:512],
              ALU.mult, ALU.mult, [PS(bkv), "pp", SC(rk)], [SC(kvf)])
        P.cp("pool", kvT[:, t * T:(t + 1) * T], scr[kvf][:, 0:512], [SC(kvf)], [("kvT", t)])
        bt = 4
        for j in range(4):
            P.tr(ps[bt][:, j * 128:(j + 1) * 128], scr[kvf][:, j * 128:(j + 1) * 128], ident[:], [SC(kvf), "ident"], [PS(bt)])
        P.cp("act", kvtok[:, 4 * t:4 * t + 4, :], ps[bt][:].rearrange("p (j r) -> p j r", j=4), [PS(bt)], [("kvtok", t)])
        s1, s2 = rope_apply(bkr, None, 1.0)
        P.tt("dve", krT[0:64, t * T:(t + 1) * T], scr[s1][0:64, 0:512], scr[s2][0:64, 0:512], ALU.add,
             [SC(s1), SC(s2)], [("krT", t)])
        P.tt("pool", krT[64:128, t * T:(t + 1) * T], scr[s1][0:64, 0:512], scr[s2][0:64, 0:512], ALU.add,
             [SC(s1), SC(s2)], [("krT", t)])
        zgroup(2)
        q_nope(0)
        q_rope(0)
        q_lat(0)
        zgroup(3)
        for h in range(16):
            qi = h % 2
            bo, bm = 3 + qi, 5 + qi
            kbs = [(4 * t + kk, kk * 128, True) for kk in range(4)] + [(kb, 0, False) for kb in range(4 * t)]
            nk = len(kbs)
            hooks = {}
            pre = []
            if h >= 1:
                (hooks.__setitem__(min(nk - 1, 5), lambda h=h: o_proj(h - 1)) if HOOKS & 8 else pre.append(lambda h=h: o_proj(h - 1)))
            if h + 1 < 16:
                (hooks.__setitem__(0, lambda h=h: q_nope(h + 1)) if HOOKS & 1 else pre.append(lambda h=h: q_nope(h + 1)))
                (hooks.__setitem__(1, lambda h=h: q_rope(h + 1)) if HOOKS & 2 else pre.append(lambda h=h: q_rope(h + 1)))
                (hooks.__setitem__(2 if nk < 8 else 3, lambda h=h: q_lat(h + 1)) if HOOKS & 4 else pre.append(lambda h=h: q_lat(h + 1)))
            for f_ in pre:
                f_()
            pts = {}
            sbk = {}
            npair = nk // 2
            for p_ in range(npair + 1):
                if p_ < npair:
                    blk = [2 * p_, 2 * p_ + 1]
                    for n_ in blk:
                        kb, lo, diag = kbs[n_]
                        sbk[n_] = state["sb"] = (state["sb"] + 1) % 3
                        bs_ = sbk[n_]
                        pts[n_] = state["pt"] = (state["pt"] + 1) % NPT
                        P.mm(ps[bs_][:, lo:512], kvT[:, kb * 128:(kb + 1) * 128], qlat[qi][:, lo:512], True, False,
                             [("kvT", kb // 4), ("qlat", qi)], [PS(bs_)])
                    for i_, n_ in enumerate(blk):
                        kb, lo, diag = kbs[n_]
                        bs_ = sbk[n_]
                        r0 = 64 * i_
                        P.mm(ps[bs_][:, lo:512], krT[r0:r0 + 64, kb * 128:(kb + 1) * 128], qrope[qi][r0:r0 + 64, lo:512],
                             False, True, [("krT", kb // 4), ("qrope", qi)], [PS(bs_)])
                    for n_ in blk:
                        kb, lo, diag = kbs[n_]
                        bs_ = sbk[n_]
                        pi_ = pts[n_]
                        P.act(pT[pi_][:, lo:512], ps[bs_][:, lo:512], AF.Exp, [PS(bs_)], [("pT", pi_)])
                        if diag:
                            P.ms("pool", pT[pi_][64:128, lo:lo + 64], 0.0, [("pT", pi_)])
                if p_ >= 1:
                    for m_ in (2 * p_ - 2, 2 * p_ - 1):
                        kb, lo, diag = kbs[m_]
                        pi_ = pts[m_]
                        tk = kb // 4
                        P.mm(ps[bo][:, lo:512], kvtok[:, kb, :], pT[pi_][:, lo:512], m_ == 0, m_ == nk - 1,
                             [("kvtok", tk), ("pT", pi_)], [PS(bo)])
                        P.mm(ps[bm][:, lo:512], onesb[:], pT[pi_][:, lo:512], m_ == 0, m_ == nk - 1,
                             ["onesb", ("pT", pi_)], [PS(bm)])
                        if m_ in hooks:
                            hooks[m_]()
                        if m_ + 0.5 in hooks:
                            hooks[m_ + 0.5]()
            o_norm(h)
        o_proj(15)
        out_proj_and_norm("owout", 1, xb, "xb", t)
        P.dma("xout", out_d[t * T:(t + 1) * T, :].rearrange("(j p) d -> p j d", p=128), xb[:],
              r=[("xb", j) for j in range(4)])

    load_x(0)
    for t in range(NT):
        layer0(t)
        layer1(t)
    stats = P.emit()
    return nc, stats


def _host_inputs(inp, b, NT=8):
    S = NT * T
    f = lambda a: np.ascontiguousarray(a, dtype=np.float32)
    fm = lambda v, n: np.asarray(v, np.float32).reshape(n, 128).T
    pp = np.zeros((128, PP_N), np.float32)
    pp[:, PP_C:PP_C + 8] = fm(inp["c"][b], 8)
    for l in range(2):
        pp[:, PP_ADAB + 16 * l:PP_ADAB + 16 * l + 16] = fm(inp["ada_b"][l, :2048], 16)
    pp[:, PP_GNG:PP_GNG + 2] = fm(inp["gmlp_norm_g"][0], 2)
    pp[:, PP_GNB:PP_GNB + 2] = fm(inp["gmlp_norm_b"][0], 2)
    pp[:, PP_PB:PP_PB + 8] = fm(inp["pool_b"][0], 8)
    pp[:, PP_PS:PP_PS + 8] = fm(inp["pool_scale"][0], 8)
    pp[:, PP_QNG:PP_QNG + 2] = fm(inp["mla_q_norm_g"][0], 2)
    pp[:, PP_KVG:PP_KVG + 1] = fm(inp["mla_kv_norm_g"][0], 1)
    inv = (1.0 / (np.float32(10000.0) ** (np.arange(0, 64, 2, dtype=np.float32) / np.float32(64)))).astype(np.float32)
    pp[:, PP_INV] = np.tile(inv, 4)
    pp[:, PP_PH] = np.where(np.arange(128) < 64, np.float32(np.pi / 2), np.float32(0.0))
    rows = np.concatenate([np.asarray(inp["gmlp_norm_b"][0], np.float32).reshape(-1),
                           np.asarray(inp["gmlp_bs"][0], np.float32).reshape(-1)])[None, :]
    return {
        "x": f(inp["x"][b, :S]),
        "pos": np.ascontiguousarray(inp["positions"][b:b + 1, :S], dtype=np.int32),
        "pp": pp, "rows": f(rows),
        "ada_w": f(inp["ada_w"]), "ada_bg": f(inp["ada_b"][:, 2048:]),
        "ln_g": f(inp["ln_g"]), "ln_b": f(inp["ln_b"]),
        "e_w_in": f(inp["e_w_in"][0]), "gmlp_ws": f(inp["gmlp_ws"][0]), "pool_w": f(inp["pool_w"][0]),
        "e_w_out": f(inp["e_w_out"][0]), "o_w_in": f(inp["o_w_in"][0]),
        "w_uq": f(np.asarray(inp["mla_w_uq"][0]).reshape(256, 16 * 192)),
        "w_uk": f(np.asarray(inp["mla_w_uk"][0]).reshape(128, 16 * 128)),
        "w_uv": f(np.asarray(inp["mla_w_uv"][0]).reshape(128, 16 * 128)),
        "o_w_out": f(inp["o_w_out"][0]),
    }


_CACHE = {}


def kernel(**inputs):
    inp = {k: np.asarray(v) for k, v in inputs.items()}
    if "nc" not in _CACHE:
        _CACHE["nc"] = build_program(8)[0]
    nc = _CACHE["nc"]
    in_maps = [_host_inputs(inp, b) for b in range(NCORES)]
    res = run_bass_kernel_spmd(nc, in_maps, core_ids=list(range(NCORES)))
    return np.stack([np.asarray(r["out"], dtype=np.float32) for r in res.results], axis=0)
```

```python
import numpy as np
import concourse.bass as bass
import concourse.mybir as mybir
from concourse.bass_utils import run_bass_kernel_spmd

F32 = mybir.dt.float32
BF16 = mybir.dt.bfloat16
I32 = mybir.dt.int32
AF = mybir.ActivationFunctionType
ALU = mybir.AluOpType

D = 1024
SEQ = 4096
T = 512
NCORES = 8
ALPHA = float((2.0 * 2) ** 0.25)
LN_EPS = 1e-5
ATTN_SCALE = float(192 ** -0.5)
POOL_WINDOWS = (2, 4, 8, 16)
TWO_PI = float(2 * np.pi)
PI = float(np.pi)
C1 = 6.28125
C2 = TWO_PI - C1

PP_C, PP_ADAB, PP_GNG, PP_GNB, PP_PB, PP_PS, PP_QNG, PP_KVG, PP_INV, PP_PH, PP_N = 0, 8, 40, 42, 44, 52, 60, 62, 63, 64, 65


class Prog:
    ENG = ("pe", "act", "dve", "pool", "sp")

    def __init__(self, nc):
        self.nc = nc
        self.e = {"pe": nc.tensor, "act": nc.scalar, "dve": nc.vector, "pool": nc.gpsimd, "sp": nc.sync}
        self.ops = []

    def op(self, eng, fn, r=(), w=()):
        self.ops.append(dict(eng=eng, fn=fn, r=tuple(r), w=tuple(w), dma=None))

    def dma(self, key, out, in_, r=(), w=(), eng="sp", **kw):
        self.ops.append(dict(eng=eng, fn=lambda E: E.dma_start(out=out, in_=in_, **kw),
                             r=tuple(r), w=tuple(w), dma=key))

    def mm(self, out, lhsT, rhs, start, stop, r, w):
        self.op("pe", lambda E: E.matmul(out, lhsT, rhs, start=start, stop=stop), r, w)

    def tr(self, out, in_, ident, r, w):
        self.op("pe", lambda E: E.transpose(out, in_, ident), r, w)

    def act(self, out, in_, func, r, w, bias=None, scale=None):
        kw = {}
        if bias is not None:
            kw["bias"] = bias
        if scale is not None:
            kw["scale"] = scale
        self.op("act", lambda E: E.activation(out, in_, func, **kw), r, w)

    def tt(self, eng, out, a, b, op, r, w):
        self.op(eng, lambda E: E.tensor_tensor(out, a, b, op), r, w)

    def ts(self, eng, out, a, s1, s2, op0, op1, r, w):
        if s2 is None:
            self.op(eng, lambda E: E.tensor_scalar(out, a, s1, None, op0), r, w)
        else:
            self.op(eng, lambda E: E.tensor_scalar(out, a, s1, s2, op0, op1), r, w)

    def stt(self, eng, out, a, s, b, op0, op1, r, w):
        self.op(eng, lambda E: E.scalar_tensor_tensor(out, a, s, b, op0, op1), r, w)

    def cp(self, eng, out, a, r, w):
        if eng == "act":
            self.op("act", lambda E: E.copy(out, a), r, w)
        else:
            self.op(eng, lambda E: E.tensor_copy(out, a), r, w)

    def ms(self, eng, out, val, w):
        self.op(eng, lambda E: E.memset(out, val), (), w)

    def emit(self):
        nc = self.nc
        ops = self.ops
        last_w, readers, last_dma = {}, {}, {}
        for i, o in enumerate(ops):
            deps = set()
            for res in o["r"]:
                if res in last_w:
                    deps.add(last_w[res])
            for res in o["w"]:
                if res in last_w:
                    deps.add(last_w[res])
                deps.update(readers.get(res, ()))
            if o["dma"] is not None and o["dma"] in last_dma:
                deps.add(last_dma[o["dma"]])
            deps.discard(i)
            o["deps"] = deps
            for res in o["r"]:
                readers.setdefault(res, []).append(i)
            for res in o["w"]:
                last_w[res] = i
                readers[res] = []
            if o["dma"] is not None:
                last_dma[o["dma"]] = i

        def skip(p, o):
            return p["dma"] is None and o["dma"] is None and p["eng"] == "pe" and o["eng"] == "pe"

        for o in ops:
            o["need_inc"] = o["dma"] is not None
        for o in ops:
            latest = {}
            for j in o["deps"]:
                p = ops[j]
                if skip(p, o) or p["dma"] is not None:
                    continue
                latest[p["eng"]] = max(latest.get(p["eng"], -1), j)
            for j in latest.values():
                ops[j]["need_inc"] = True
        sem = {e: nc.alloc_semaphore("s_" + e) for e in self.ENG}
        cnt = {e: 0 for e in self.ENG}
        dsem, dcnt = {}, {}
        for o in ops:
            if o["dma"] is not None:
                k = o["dma"]
                if k not in dsem:
                    dsem[k] = nc.alloc_semaphore("d_%d" % len(dsem))
                    dcnt[k] = 0
                dcnt[k] += 16
                o["sem"], o["val"] = ("d", k), dcnt[k]
            elif o["need_inc"]:
                cnt[o["eng"]] += 1
                o["sem"], o["val"] = ("e", o["eng"]), cnt[o["eng"]]
        known = {e: {} for e in self.ENG}
        nwait = 0
        for o in ops:
            E = self.e[o["eng"]]
            need = {}
            latest = {}
            for j in o["deps"]:
                p = ops[j]
                if skip(p, o):
                    continue
                if p["dma"] is not None:
                    need[p["sem"]] = max(need.get(p["sem"], 0), p["val"])
                else:
                    latest[p["eng"]] = max(latest.get(p["eng"], -1), j)
            for j in latest.values():
                p = ops[j]
                need[p["sem"]] = max(need.get(p["sem"], 0), p["val"])
            kn = known[o["eng"]]
            for s, v in need.items():
                if kn.get(s, 0) >= v:
                    continue
                kn[s] = v
                E.wait_ge(dsem[s[1]] if s[0] == "d" else sem[s[1]], v)
                nwait += 1
            ins = o["fn"](E)
            if o["dma"] is not None:
                ins.then_inc(dsem[o["dma"]], 16)
            elif o["need_inc"]:
                ins.then_inc(sem[o["eng"]], 1)
        for k, h in dsem.items():
            nc.sync.wait_ge(h, dcnt[k])
        self.stats = dict(n_ops=len(ops), n_wait=nwait, cnt=dict(cnt), n_dma_keys=len(dsem))
        return self.stats


def build_program(NT=8, debug_x1=False):
    S = NT * T
    nc = bass.Bass("TRN2", target_bir_lowering=False)
    dt = lambda name, shape, dtype=F32, kind="ExternalInput": nc.dram_tensor(name, list(shape), dtype, kind=kind).ap()
    x_d = dt("x", [S, D])
    pos_d = dt("pos", [1, S], I32)
    pp_d = dt("pp", [128, PP_N])
    rows_d = dt("rows", [1, 256 + 512])
    adaw_d = dt("ada_w", [2, D, 3 * D])
    adabg_d = dt("ada_bg", [2, D])
    lng_d = dt("ln_g", [2, D])
    lnb_d = dt("ln_b", [2, D])
    ewin_d = dt("e_w_in", [D, 5 * D])
    ws_d = dt("gmlp_ws", [4, 128, 128])
    poolw_d = dt("pool_w", [4, 256, 256])
    ewout_d = dt("e_w_out", [2 * D, D])
    owin_d = dt("o_w_in", [D, 2496])
    wuq_d = dt("w_uq", [256, 16 * 192])
    wuk_d = dt("w_uk", [128, 16 * 128])
    wuv_d = dt("w_uv", [128, 16 * 128])
    owout_d = dt("o_w_out", [2 * D, D])
    out_d = dt("out", [S, D], F32, "ExternalOutput")
    x1_d = dt("x1dbg", [S, D], F32, "ExternalOutput") if debug_x1 else None

    P = Prog(nc)
    sb = lambda name, shape, dtype=F32: nc.alloc_sbuf_tensor("sb_" + name, list(shape), dtype)

    xa = sb("xa", [128, 4, D])
    xb = sb("xb", [128, 4, D])
    gate1 = sb("gate1", [128, 2, D])
    lng = sb("lng", [128, 2, D])
    lnb = sb("lnb", [128, 2, D])
    ident = sb("ident", [128, 128])
    onesf = sb("onesf", [128, 128])
    onesb = sb("onesb", [128, 128], BF16)
    pp = sb("pp", [128, PP_N])
    cond = sb("cond", [128, 8])
    modT = sb("modT", [128, 2, 16])
    hT = sb("hT", [128, 8, T], BF16)
    mixT = sb("mixT", [128, 16, T], BF16)
    NSLOT = 4
    wsl = [sb("wsl%d" % i, [128, 8, 512], BF16) for i in range(NSLOT)]
    wukT = sb("wukT", [128, 16, 128], BF16)
    wuv = sb("wuv", [128, 16, 128], BF16)
    poolw = sb("poolw", [128, 4, 2, 256], BF16)
    WsT = sb("WsT", [128, 4, 128], BF16)
    Cmat = sb("Cmat", [128, 4, 2, 128])
    rcw = sb("rcw", [128, 4, 16])
    halo = sb("halo", [128, 8, 16])
    vn = sb("vn", [128, 4, D], BF16)
    wq = [vn[:, 2 * i:2 * i + 2, :].rearrange("p a (b n) -> p a b n", b=4) for i in range(2)]
    WQ = [[("vn", j, h) for j in (2 * i, 2 * i + 1) for h in range(4)] for i in range(2)]
    NSCR = 8
    scr = [sb("scr%d" % i, [128, 528]) for i in range(NSCR)]
    pl = sb("pl", [128, 2, 2, T], BF16)
    szb = sb("szb", [128, 2, T])
    small = sb("small", [128, 128])
    qcn = sb("qcn", [128, 2, T], BF16)
    kvT = sb("kvT", [128, S], BF16)
    kvtok = sb("kvtok", [128, S // 128, 128], BF16)
    krT = sb("krT", [128, S], BF16)
    cs = sb("cs", [128, T])
    posi = sb("posi", [128, T], I32)
    qn = sb("qn", [128, T], BF16)
    qlat = [sb("qlat%d" % i, [128, T], BF16) for i in range(2)]
    qrope = [sb("qrope%d" % i, [128, T], BF16) for i in range(2)]
    olat = [sb("olat%d" % i, [128, T], BF16) for i in range(2)]
    NPT = 4
    pT = [sb("pT%d" % i, [128, T], BF16) for i in range(NPT)]
    ps = [nc.alloc_psum_tensor("ps%d" % i, [128, 512], F32) for i in range(8)]
    PS = lambda i: ("ps", i)
    state = dict(bank=0, scr=0, slot=0, pt=0, sb=0)

    def nbank(lo=0, hi=8):
        b = state["bank"]
        b = lo + ((b - lo + 1) % (hi - lo)) if lo <= b < hi else lo
        state["bank"] = b
        return b

    def nscr():
        state["scr"] = (state["scr"] + 1) % NSCR
        return state["scr"]

    SC = lambda i: ("scr", i)

    def kview(w2d, c0, ncols, k0=0, nk=8):
        return w2d[k0 * 128:(k0 + nk) * 128, c0:c0 + ncols].rearrange("(kc p) n -> p kc n", p=128)

    wlist = []
    for t in range(NT):
        for g in (2, 3, 0, 4, 1, 5, 6, 8, 7, 9):
            wlist.append(("ewin", g, kview(ewin_d, g * 512, 512), 512))
        for hf in range(2):
            for kh in range(2):
                wlist.append(("ewout", (hf, kh), kview(ewout_d, hf * 512, 512, kh * 8), 512))
        wlist.append(("owinA", 0, kview(owin_d, 0, 448), 448))
        for g in range(4):
            wlist.append(("owinZ", g, kview(owin_d, 448 + g * 512, 512), 512))
        for hf in range(2):
            for kh in range(2):
                wlist.append(("owout", (hf, kh), kview(owout_d, hf * 512, 512, kh * 8), 512))
    wstate = dict(issued=0, used=0, released=0)

    NG = len(wlist) // NT
    wscr = nc.dram_tensor("wscr", [NG, 128, 8 * 512], BF16, kind="Internal").ap()

    def w_pump():
        while wstate["issued"] < len(wlist) and wstate["issued"] - NSLOT < wstate["released"]:
            i = wstate["issued"]
            kind, g, src, ncols = wlist[i]
            s_ = i % NSLOT
            gid = i % NG
            scr_v = wscr[gid].rearrange("p (k n) -> p k n", k=8)[:, :, 0:ncols]
            if i < NG:
                P.dma(("wp", s_), wsl[s_][:, :, 0:ncols], src, w=[("wsl", s_)], eng="pool")
                if kind in ("ewout", "owout"):
                    l_ = 0 if kind == "ewout" else 1
                    hf_ = g[0]
                    P.tt("dve", wsl[s_][:], wsl[s_][:],
                         gate1[:, l_, hf_ * 512:(hf_ + 1) * 512].unsqueeze(1).to_broadcast([128, 8, 512]), ALU.mult,
                         [("wsl", s_), ("gate1", l_)], [("wsl", s_)])
                if NT > 1:
                    P.dma(("wst", s_), scr_v, wsl[s_][:, :, 0:ncols], r=[("wsl", s_)], w=[("wscr", gid)])
            else:
                P.dma(("w", s_), wsl[s_][:, :, 0:ncols], scr_v, r=[("wscr", gid)], w=[("wsl", s_)])
            wstate["issued"] += 1

    def w_next(kind):
        i = wstate["used"]
        assert wlist[i][0] == kind, (wlist[i][0], kind)
        w_pump()
        assert wstate["issued"] > i
        wstate["used"] += 1
        return wsl[i % NSLOT], ("wsl", i % NSLOT)

    def w_release(n=1):
        wstate["released"] += n
        w_pump()

    P.dma("c0", pp[:], pp_d, w=["pp"])
    for l in range(2):
        P.dma(("cg", l), gate1[:, l, :], adabg_d[l:l + 1, :].partition_broadcast(128), w=[("gate1", l)])
        P.dma(("cl", l), lng[:, l, :], lng_d[l:l + 1, :].partition_broadcast(128), w=[("lng", l)])
        P.dma(("cb", l), lnb[:, l, :], lnb_d[l:l + 1, :].partition_broadcast(128), w=[("lnb", l)])
    P.op("pool", lambda E: E.iota(ident[:], [[1, 128]], base=0, channel_multiplier=-1,
                                    allow_small_or_imprecise_dtypes=True), w=["ident"])
    P.op("pool", lambda E: E.tensor_single_scalar(ident[:], ident[:], 0.0, ALU.is_equal), r=["ident"], w=["ident"])
    P.ms("pool", onesf[:], 1.0, ["onesf"])
    P.ms("pool", onesb[:], 1.0, ["onesb"])
    P.ms("pool", halo[:], 0.0, [("halo", c) for c in range(8)])
    P.op("pool", lambda E: E.iota(rcw[:, 0, :], [[1, 16]], base=1, channel_multiplier=0,
                                    allow_small_or_imprecise_dtypes=True), w=["rcw"])
    for g in (3, 2, 1):
        P.cp("pool", rcw[:, g, :], rcw[:, 0, :], ["rcw"], ["rcw"])
    for g, win in enumerate(POOL_WINDOWS):
        P.ts("dve", rcw[:, g, :], rcw[:, g, :], float(win), None, ALU.min, None, ["rcw"], ["rcw"])
    P.op("dve", lambda E: E.reciprocal(rcw[:], rcw[:]), ["rcw"], ["rcw"])
    P.act(cond[:], pp[:, PP_C:PP_C + 8], AF.Silu, ["pp"], ["cond"])

    for kc in range(8):
        P.cp("dve", scr[kc][:, 0:128], cond[:, kc:kc + 1].to_broadcast([128, 128]), ["cond"], [SC(kc)])
    stg = [xa[:].rearrange("p j (a n) -> p (j a) n", a=2), xb[:].rearrange("p j (a n) -> p (j a) n", a=2)]
    STG = [[("xa", j) for j in range(4)], [("xb", j) for j in range(4)]]
    sti = 0
    for l in range(2):
        bmod = nbank()
        for blk in range(4):
            s_ = sti % 2
            sti += 1
            P.dma(("stg", s_), stg[s_], kview(adaw_d[l], blk * 512, 512), w=STG[s_])
            for fc in range(4):
                col = blk * 4 + fc
                for kc in range(8):
                    P.mm(ps[bmod][:, col:col + 1], stg[s_][:, kc, fc * 128:(fc + 1) * 128], cond[:, kc:kc + 1],
                         kc == 0, kc == 7, STG[s_] + ["cond"], [PS(bmod)])
        P.tt("dve", modT[:, l, :], ps[bmod][:, 0:16], pp[:, PP_ADAB + 16 * l:PP_ADAB + 16 * l + 16], ALU.add,
             [PS(bmod), "pp"], [("modT", l)])
        P.ts("dve", modT[:, l, 8:16], modT[:, l, 8:16], 1.0, None, ALU.add, None, [("modT", l)], [("modT", l)])
        for hf in range(2):
            s_ = sti % 2
            sti += 1
            bg = nbank()
            P.dma(("stg", s_), stg[s_], kview(adaw_d[l], 2048 + hf * 512, 512), w=STG[s_])
            for kc in range(8):
                P.mm(ps[bg][:], scr[kc][:, 0:128], stg[s_][:, kc, :], kc == 0, kc == 7, STG[s_] + [SC(kc)], [PS(bg)])
            P.stt("dve", gate1[:, l, hf * 512:(hf + 1) * 512], ps[bg][:], 1.0, gate1[:, l, hf * 512:(hf + 1) * 512],
                  ALU.add, ALU.add, [PS(bg), ("gate1", l)], [("gate1", l)])

    wstg = stg[0]
    P.dma(("stg", 0), wstg[:, 0:4, 0:128], ws_d.rearrange("h t s -> t h s"), w=STG[0])
    wsf = stg[1]
    rows = scr[0][0:1, 0:256], scr[1][0:1, 0:512]
    P.dma("c1", rows[0], rows_d[0:1, 0:256], w=[SC(0)])
    P.dma("c2", rows[1], rows_d[0:1, 256:768], w=[SC(1)])
    rsrow = scr[2][0:1, 0:512]
    brs = nbank()
    for h in range(4):
        b_ = nbank()
        P.tr(ps[b_][:, 0:128], wstg[:, h, 0:128], ident[:], STG[0] + ["ident"], [PS(b_)])
        P.cp("dve", wsf[:, h, 0:128], ps[b_][:, 0:128], [PS(b_)], STG[1])
        P.ms("pool", wsf[64:128, h, 0:64], 0.0, STG[1])
        P.cp("act", WsT[:, h, :], wsf[:, h, 0:128], STG[1], ["WsT"])
        P.mm(ps[brs][0:1, h * 128:(h + 1) * 128], onesf[:, 0:1], wsf[:, h, 0:128], True, True,
             STG[1] + ["onesf"], [PS(brs)])
    P.cp("dve", rsrow, ps[brs][0:1, :], [PS(brs)], [SC(2)])
    for h in range(4):
        for dc in range(2):
            b_ = nbank()
            P.mm(ps[b_][:, 0:128], rows[0][0:1, dc * 128:(dc + 1) * 128], rsrow[0:1, h * 128:(h + 1) * 128], True, False,
                 [SC(0), SC(2)], [PS(b_)])
            P.mm(ps[b_][:, 0:128], onesf[0:1, :], rows[1][0:1, h * 128:(h + 1) * 128], False, True,
                 [SC(1), "onesf"], [PS(b_)])
            P.cp("dve", Cmat[:, h, dc, :], ps[b_][:, 0:128], [PS(b_)], ["Cmat"])
    P.dma("cw0", wuv[:], wuv_d.rearrange("r (h d) -> r h d", h=16), w=["wuv"], eng="pool")
    P.dma("cw1", poolw[:], poolw_d.rearrange("g (dc p) e -> p g dc e", p=128), w=["poolw"], eng="pool")
    P.dma(("stg", 1), stg[1][:, 0:4, :], wuk_d.rearrange("r (a n) -> r a n", a=4), w=STG[1])
    for h in range(16):
        b_ = nbank()
        P.tr(ps[b_][:, 0:128], stg[1][:, h // 4, (h % 4) * 128:(h % 4 + 1) * 128], ident[:], STG[1] + ["ident"], [PS(b_)])
        P.cp("dve" if h % 2 else "act", wukT[:, h, :], ps[b_][:, 0:128], [PS(b_)], ["wukT"])

    def make_hT(src, l, src_res):
        order = [(state["bank"] + 1 + i) % 8 for i in range(8)]
        state["bank"] = order[-1]
        for c in range(8):
            bk = order[c]
            for j in range(3):
                P.tr(ps[bk][:, j * 128:(j + 1) * 128], src[:, j, c * 128:(c + 1) * 128], ident[:],
                     [(src_res, j), "ident"], [PS(bk)])
        for c in range(8):
            bk = order[c]
            P.tr(ps[bk][:, 384:512], src[:, 3, c * 128:(c + 1) * 128], ident[:], [(src_res, 3), "ident"], [PS(bk)])
            if c % 2 == 0:
                P.act(hT[:, c, :], ps[bk][:], AF.Identity, [PS(bk), ("modT", l)], [("hT", c)],
                      bias=modT[:, l, c:c + 1], scale=modT[:, l, 8 + c:9 + c])
            else:
                P.ts("dve", hT[:, c, :], ps[bk][:], modT[:, l, 8 + c:9 + c], modT[:, l, c:c + 1], ALU.mult, ALU.add,
                     [PS(bk), ("modT", l)], [("hT", c)])

    HT_ALL = [("hT", c) for c in range(8)]
    MIX_ALL = [("mixT", c) for c in range(16)]

    def out_proj_and_norm(kind, l, res_src, res_name, t):
        for j in range(4):
            P.act(xb[:, j, :], res_src[:, j, :], AF.Identity, [(res_name, j)], [("xb", j)], scale=ALPHA)
        wg = [w_next(kind) for _ in range(4)]
        for j in range(4):
            for hf in range(2):
                b_ = nbank()
                for kc in range(16):
                    wt, rr = wg[2 * hf + (kc // 8)]
                    P.mm(ps[b_][:], mixT[:, kc, j * 128:(j + 1) * 128], wt[:, kc % 8, :], kc == 0, kc == 15,
                         [("mixT", kc), rr], [PS(b_)])
                P.tt("dve", xb[:, j, hf * 512:(hf + 1) * 512], ps[b_][:], xb[:, j, hf * 512:(hf + 1) * 512], ALU.add,
                     [PS(b_), ("xb", j)], [("xb", j)])
            c0 = 64 + 16 * j
            P.op("dve", lambda E, j=j, c0=c0: E.bn_stats(small[:, c0:c0 + 6], xb[:, j, 0:512]), [("xb", j)], [("st", j)])
            P.op("dve", lambda E, j=j, c0=c0: E.bn_stats(small[:, c0 + 6:c0 + 12], xb[:, j, 512:1024]), [("xb", j)], [("st", j)])
            P.op("dve", lambda E, c0=c0: E.bn_aggr(small[:, c0 + 12:c0 + 14], small[:, c0:c0 + 12]), [("st", j)], [("mv", j)])
            P.act(small[:, c0 + 14:c0 + 15], small[:, c0 + 13:c0 + 14], AF.Ln, [("mv", j)], [("rstd", j)], bias=LN_EPS)
            P.act(small[:, c0 + 14:c0 + 15], small[:, c0 + 14:c0 + 15], AF.Exp, [("rstd", j)], [("rstd", j)], scale=-0.5)
            P.stt("dve", small[:, c0 + 15:c0 + 16], small[:, c0 + 12:c0 + 13], -1.0, small[:, c0 + 14:c0 + 15], ALU.mult, ALU.mult,
                  [("mv", j), ("rstd", j)], [("nb", j)])
            P.act(xb[:, j, :], xb[:, j, :], AF.Identity, [("xb", j), ("rstd", j), ("nb", j)], [("xb", j)],
                  bias=small[:, c0 + 15:c0 + 16], scale=small[:, c0 + 14:c0 + 15])
            P.tt("dve", xb[:, j, :], xb[:, j, :], lng[:, l, :], ALU.mult, [("xb", j), ("lng", l)], [("xb", j)])
            P.tt("dve", xb[:, j, :], xb[:, j, :], lnb[:, l, :], ALU.add, [("xb", j), ("lnb", l)], [("xb", j)])
        w_release(4)

    def layer0(t):
        make_hT(xa, 0, "xa")
        for hf in range(2):
            wt, wr = w_next("ewin")
            for j in range(4):
                b_ = nbank()
                for kc in range(8):
                    P.mm(ps[b_][:], hT[:, kc, j * 128:(j + 1) * 128], wt[:, kc, :], kc == 0, kc == 7,
                         [("hT", kc), wr], [PS(b_)])
                st = small[:, 16:28].rearrange("p (a b) -> p a b", a=2)
                mv = small[:, 28:32].rearrange("p (a b) -> p a b", a=2)
                for i in range(2):
                    P.op("dve", lambda E, i=i, b_=b_: E.bn_stats(st[:, i, :], ps[b_][:, i * 256:(i + 1) * 256]),
                         [PS(b_)], ["vst"])
                    P.op("dve", lambda E, i=i: E.bn_aggr(mv[:, i, :], st[:, i, :]), ["vst"], ["vmv"])
                rs_ = small[:, 32:34]
                nb_ = small[:, 34:36]
                P.act(rs_, mv[:, :, 1], AF.Ln, ["vmv"], ["vrs"], bias=LN_EPS)
                P.act(rs_, rs_, AF.Exp, ["vrs"], ["vrs"], scale=-0.5)
                P.stt("dve", nb_, mv[:, :, 0], -1.0, rs_, ALU.mult, ALU.mult, ["vmv", "vrs"], ["vnb"])
                for i in range(2):
                    h = hf * 2 + i
                    P.act(vn[:, j, h * 256:(h + 1) * 256], ps[b_][:, i * 256:(i + 1) * 256], AF.Identity,
                          [PS(b_), "vrs", "vnb"], [("vn", j, h)], bias=small[:, 34 + i:35 + i], scale=small[:, 32 + i:33 + i])
            w_release()
        for hp in range(2):
            wu, ru = w_next("ewin")
            wz, rz = w_next("ewin")
            for hh in range(2):
                h = hp * 2 + hh
                for dc in range(2):
                    cl = hh * 2 + dc
                    bu, bz, bs_ = nbank(), nbank(), nbank()
                    for kc in range(8):
                        P.mm(ps[bz][:], wz[:, kc, cl * 128:(cl + 1) * 128], hT[:, kc, :], kc == 0, kc == 7,
                             [("hT", kc), rz], [PS(bz)])
                    for j in range(4):
                        P.mm(ps[bs_][:, j * 128:(j + 1) * 128], vn[:, j, h * 256 + dc * 128:h * 256 + (dc + 1) * 128],
                             WsT[:, h, :], True, True, [("vn", j, h), "WsT"], [PS(bs_)])
                    for kc in range(8):
                        P.mm(ps[bu][:], wu[:, kc, cl * 128:(cl + 1) * 128], hT[:, kc, :], kc == 0, kc == 7,
                             [("hT", kc), ru], [PS(bu)])
                    s1, s2 = nscr(), nscr()
                    P.act(scr[s1][:, 0:512], ps[bz][:], AF.Silu, [PS(bz)], [SC(s1)])
                    P.stt("dve", scr[s2][:, 0:512].rearrange("p (j t) -> p j t", j=4),
                          ps[bs_][:].rearrange("p (j t) -> p j t", j=4), pp[:, PP_GNG + dc:PP_GNG + dc + 1],
                          Cmat[:, h, dc, :].unsqueeze(1).to_broadcast([128, 4, 128]), ALU.mult, ALU.add,
                          [PS(bs_), "pp", "Cmat"], [SC(s2)])
                    P.tt("pool", scr[s2][:, 0:512], scr[s2][:, 0:512], scr[s1][:, 0:512], ALU.mult,
                         [SC(s1), SC(s2)], [SC(s2)])
                    P.tt("dve", mixT[:, 2 * h + dc, :], ps[bu][:], scr[s2][:, 0:512], ALU.mult,
                         [PS(bu), SC(s2)], [("mixT", 2 * h + dc)])
            w_release(2)
        pw = {}

        def pool_x(g):
            wx, rx = pw["x%d" % (g // 2)]
            gg = g % 2
            win = POOL_WINDOWS[g]
            nstep = int(np.log2(win))
            for dc in range(2):
                cl = gg * 2 + dc
                ch = 2 * g + dc
                bx = nbank()
                for kc in range(8):
                    P.mm(ps[bx][:], wx[:, kc, cl * 128:(cl + 1) * 128], hT[:, kc, :], kc == 0, kc == 7,
                         [("hT", kc), rx], [PS(bx)])
                sx = nscr()
                P.cp("pool", scr[sx][:, 0:16], halo[:, ch, :], [("halo", ch)], [SC(sx)])
                P.cp("act", scr[sx][:, 16:528], ps[bx][:], [PS(bx)], [SC(sx)])
                P.cp("pool", halo[:, ch, :], scr[sx][:, 512:528], [SC(sx)], [("halo", ch)])
                cur = sx
                for k in range(nstep):
                    sh = 1 << k
                    lo = (1 << (k + 1)) - 1
                    nx = nscr()
                    P.tt("pool", scr[nx][:, lo:528], scr[cur][:, lo:528], scr[cur][:, lo - sh:528 - sh], ALU.add,
                         [SC(cur)], [SC(nx)])
                    cur = nx
                P.stt("dve", pl[:, g % 2, dc, :], scr[cur][:, 16:528], 1.0 / win, scr[sx][:, 16:528], ALU.mult, ALU.subtract,
                      [SC(cur), SC(sx)], [("pl", g % 2, dc)])
                if t == 0:
                    s3 = nscr()
                    P.tt("dve", scr[s3][:, 0:16], scr[cur][:, 16:32], rcw[:, g, :], ALU.mult, [SC(cur), "rcw"], [SC(s3)])
                    P.tt("dve", pl[:, g % 2, dc, 0:16], scr[s3][:, 0:16], scr[sx][:, 16:32], ALU.subtract,
                         [SC(s3), SC(sx)], [("pl", g % 2, dc)])

        def pool_zy(g):
            wz, rz = pw["z%d" % (g // 2)]
            gg = g % 2
            for ec in range(2):
                cl = gg * 2 + ec
                ch = 2 * g + ec
                by, bz = nbank(), nbank()
                for kc in range(8):
                    P.mm(ps[bz][:], wz[:, kc, cl * 128:(cl + 1) * 128], hT[:, kc, :], kc == 0, kc == 7,
                         [("hT", kc), rz], [PS(bz)])
                P.act(szb[:, ec, :], ps[bz][:], AF.Silu, [PS(bz)], [("szb", ec)])
                for dc in range(2):
                    P.mm(ps[by][:], poolw[:, g, dc, ec * 128:(ec + 1) * 128], pl[:, g % 2, dc, :], dc == 0, dc == 1,
                         [("pl", g % 2, dc), "poolw"], [PS(by)])
                s2 = nscr()
                P.ts("dve", scr[s2][:, 0:512], ps[by][:], pp[:, PP_PB + ch:PP_PB + ch + 1], pp[:, PP_PS + ch:PP_PS + ch + 1],
                     ALU.add, ALU.mult, [PS(by), "pp"], [SC(s2)])
                P.tt("pool", mixT[:, 8 + ch, :], scr[s2][:, 0:512], szb[:, ec, :], ALU.mult,
                     [("szb", ec), SC(s2)], [("mixT", 8 + ch)])

        pw["x0"] = w_next("ewin")
        pool_x(0)
        pool_x(1)
        w_release()
        pw["z0"] = w_next("ewin")
        pool_zy(0)
        pw["x1"] = w_next("ewin")
        pool_x(2)
        pool_zy(1)
        w_release()
        pool_x(3)
        w_release()
        pw["z1"] = w_next("ewin")
        pool_zy(2)
        pool_zy(3)
        w_release()
        out_proj_and_norm("ewout", 0, xa, "xa", t)
        if debug_x1:
            P.dma("x1o", x1_d[t * T:(t + 1) * T, :].rearrange("(j p) d -> p j d", p=128), xb[:],
                  r=[("xb", j) for j in range(4)])

    def rope_apply(bank, dst, scale):
        s1, s2 = nscr(), nscr()
        P.stt("dve", scr[s1][0:64, 0:512], ps[bank][0:64, :], scale, cs[0:64, :], ALU.mult, ALU.mult,
              [PS(bank), "cs"], [SC(s1)])
        P.stt("dve", scr[s2][0:64, 0:512], ps[bank][64:128, :], scale, cs[64:128, :], ALU.mult, ALU.mult,
              [PS(bank), "cs"], [SC(s2)])
        return s1, s2

    def load_x(t):
        P.dma("xin", xa[:], x_d[t * T:(t + 1) * T, :].rearrange("(j p) d -> p j d", p=128),
              w=[("xa", j) for j in range(4)])

    def layer1(t):
        if t + 1 < NT:
            load_x(t + 1)
        make_hT(xb, 1, "xb")

        def load_wq(g):
            wqi = g % 2
            for cc in range(2):
                P.dma(("wq", wqi, cc), wq[wqi][:, cc, :, 0:192],
                      wuq_d[cc * 128:(cc + 1) * 128, g * 768:(g + 1) * 768].rearrange("p (a n) -> p a n", a=4),
                      w=WQ[wqi], eng="pool")
            P.ts("pool", wq[wqi][:, :, :, 192:224], wq[wqi][:, :, :, 160:192], -1.0, None, ALU.mult, None,
                 WQ[wqi], WQ[wqi])
            P.cp("pool", wq[wqi][:, :, :, 224:256], wq[wqi][:, :, :, 128:160], WQ[wqi], WQ[wqi])

        load_wq(0)
        p_ = "posi"
        posi_ap = posi[:]
        P.dma("pos", posi_ap, pos_d[0:1, t * T:(t + 1) * T].partition_broadcast(128), w=["posi"])
        a_, k_, m_ = nscr(), nscr(), nscr()
        ang, kf, mk = scr[a_][:, 0:512], scr[k_][:, 0:512], scr[m_][:, 0:512]
        ki = posi_ap
        P.cp("dve", ang, posi_ap, ["posi"], [SC(a_)])
        P.ts("dve", ang, ang, pp[:, PP_INV:PP_INV + 1], None, ALU.mult, None, [SC(a_), "pp"], [SC(a_)])

        def wrap():
            P.ts("dve", mk, ang, PI, -TWO_PI, ALU.is_gt, ALU.mult, [SC(a_)], [SC(m_)])
            P.tt("dve", ang, ang, mk, ALU.add, [SC(a_), SC(m_)], [SC(a_)])
            P.ts("dve", mk, ang, -PI, TWO_PI, ALU.is_lt, ALU.mult, [SC(a_)], [SC(m_)])
            P.tt("dve", ang, ang, mk, ALU.add, [SC(a_), SC(m_)], [SC(a_)])

        P.ts("dve", kf, ang, 1.0 / TWO_PI, None, ALU.mult, None, [SC(a_)], [SC(k_)])
        P.cp("dve", ki, kf, [SC(k_), SC(a_)], ["posi"])
        P.cp("dve", kf, ki, ["posi"], [SC(k_)])
        P.stt("dve", ang, kf, -C1, ang, ALU.mult, ALU.add, [SC(k_), SC(a_)], [SC(a_)])
        P.stt("dve", ang, kf, -C2, ang, ALU.mult, ALU.add, [SC(k_), SC(a_)], [SC(a_)])
        wrap()
        P.ts("dve", ang, ang, pp[:, PP_PH:PP_PH + 1], None, ALU.add, None, [SC(a_), "pp"], [SC(a_)])
        wrap()
        P.act(cs[:], ang, AF.Sin, [SC(a_)], ["cs"])

        MB = 7
        HOOKS = 15

        def q_nope(h):
            wqi, hq = (h // 4) % 2, h % 4
            if hq == 0 and h + 4 < 16:
                load_wq(h // 4 + 1)
            for cc in range(2):
                P.mm(ps[MB][:], wq[wqi][:, cc, hq, 0:128], qcn[:, cc, :], cc == 0, cc == 1,
                     WQ[wqi] + [("qcn", cc)], [PS(MB)])
            P.cp("act", qn[:], ps[MB][:], [PS(MB)], ["qn"])

        def q_rope(h):
            wqi, hq, qi = (h // 4) % 2, h % 4, h % 2
            for cc in range(2):
                P.mm(ps[MB][:], wq[wqi][:, cc, hq, 128:256], qcn[:, cc, :], cc == 0, cc == 1,
                     WQ[wqi] + [("qcn", cc)], [PS(MB)])
            s1, s2 = rope_apply(MB, None, ATTN_SCALE)
            P.tt("dve", qrope[qi][0:64, :], scr[s1][0:64, 0:512], scr[s2][0:64, 0:512], ALU.add, [SC(s1), SC(s2)], [("qrope", qi)])
            P.tt("pool", qrope[qi][64:128, :], scr[s1][0:64, 0:512], scr[s2][0:64, 0:512], ALU.add, [SC(s1), SC(s2)], [("qrope", qi)])

        def q_lat(h):
            qi = h % 2
            P.mm(ps[MB][:], wukT[:, h, :], qn[:], True, True, ["wukT", "qn"], [PS(MB)])
            P.act(qlat[qi][:], ps[MB][:], AF.Identity, [PS(MB)], [("qlat", qi)], scale=ATTN_SCALE)

        def o_norm(h):
            qi = h % 2
            bo, bm = 3 + qi, 5 + qi
            rs_ = nscr()
            P.op("dve", lambda E, rs_=rs_, bm=bm: E.reciprocal(scr[rs_][:, 0:512], ps[bm][:]), [PS(bm)], [SC(rs_)])
            P.tt("dve", olat[qi][:], ps[bo][:], scr[rs_][:, 0:512], ALU.mult, [PS(bo), SC(rs_)], [("olat", qi)])

        def o_proj(h):
            qi = h % 2
            P.mm(ps[MB][:], wuv[:, h, :], olat[qi][:], True, True, ["wuv", ("olat", qi)], [PS(MB)])
            P.tt("dve", mixT[:, h, :], ps[MB][:], mixT[:, h, :], ALU.mult, [PS(MB), ("mixT", h)], [("mixT", h)])

        def zgroup(g):
            wz, rz = w_next("owinZ")
            for cl in range(4):
                h = g * 4 + cl
                b_ = 6 + (h % 2)
                for kc in range(8):
                    P.mm(ps[b_][:], wz[:, kc, cl * 128:(cl + 1) * 128], hT[:, kc, :], kc == 0, kc == 7,
                         [("hT", kc), rz], [PS(b_)])
                P.act(mixT[:, h, :], ps[b_][:], AF.Silu, [PS(b_)], [("mixT", h)])
            w_release()

        wa, ra = w_next("owinA")
        P.ts("pool", wa[:, :, 448:480], wa[:, :, 416:448], -1.0, None, ALU.mult, None, [ra], [ra])
        P.cp("pool", wa[:, :, 480:512], wa[:, :, 384:416], [ra], [ra])
        bq = [0, 1]
        bkv, bkr = 2, 3
        for cc in range(2):
            for kc in range(8):
                P.mm(ps[bq[cc]][:], wa[:, kc, cc * 128:(cc + 1) * 128], hT[:, kc, :], kc == 0, kc == 7,
                     [("hT", kc), ra], [PS(bq[cc])])
        for kc in range(8):
            P.mm(ps[bkv][:], wa[:, kc, 256:384], hT[:, kc, :], kc == 0, kc == 7, [("hT", kc), ra], [PS(bkv)])
        for kc in range(8):
            P.mm(ps[bkr][:], wa[:, kc, 384:512], hT[:, kc, :], kc == 0, kc == 7, [("hT", kc), ra], [PS(bkr)])
        w_release()
        sq = [nscr(), nscr(), nscr()]
        for cc in range(2):
            P.act(scr[sq[cc]][:, 0:512], ps[bq[cc]][:], AF.Square, [PS(bq[cc])], [SC(sq[cc])])
        P.act(scr[sq[2]][:, 0:512], ps[bkv][:], AF.Square, [PS(bkv)], [SC(sq[2])])
        zgroup(0)
        bsq, bsk = 4, 5
        for cc in range(2):
            P.mm(ps[bsq][:], onesf[:], scr[sq[cc]][:, 0:512], cc == 0, cc == 1, [SC(sq[cc]), "onesf"], [PS(bsq)])
        P.mm(ps[bsk][:], onesf[:], scr[sq[2]][:, 0:512], True, True, [SC(sq[2]), "onesf"], [PS(bsk)])
        rq, rk = nscr(), nscr()
        P.act(scr[rq][:, 0:512], ps[bsq][:], AF.Ln, [PS(bsq)], [SC(rq)], bias=LN_EPS, scale=1.0 / 256)
        P.act(scr[rq][:, 0:512], scr[rq][:, 0:512], AF.Exp, [SC(rq)], [SC(rq)], scale=-0.5)
        P.act(scr[rk][:, 0:512], ps[bsk][:], AF.Ln, [PS(bsk)], [SC(rk)], bias=LN_EPS, scale=1.0 / 128)
        P.act(scr[rk][:, 0:512], scr[rk][:, 0:512], AF.Exp, [SC(rk)], [SC(rk)], scale=-0.5)
        for cc in range(2):
            P.stt("dve", qcn[:, cc, :], ps[bq[cc]][:], pp[:, PP_QNG + cc:PP_QNG + cc + 1], scr[rq][:, 0:512],
                  ALU.mult, ALU.mult, [PS(bq[cc]), "pp", SC(rq)], [("qcn", cc)])
        zgroup(1)
        kvf = nscr()
        P.stt("dve", scr[kvf][:, 0:512], ps[bkv][:], pp[:, PP_KVG:PP_KVG + 1], scr[rk][:, 0:512],
              ALU.mult, ALU.mult, [PS(bkv), "pp", SC(rk)], [SC(kvf)])
        P.cp("pool", kvT[:, t * T:(t + 1) * T], scr[kvf][:, 0:512], [SC(kvf)], [("kvT", t)])
        bt = 4
        for j in range(4):
            P.tr(ps[bt][:, j * 128:(j + 1) * 128], scr[kvf][:, j * 128:(j + 1) * 128], ident[:], [SC(kvf), "ident"], [PS(bt)])
        P.cp("act", kvtok[:, 4 * t:4 * t + 4, :], ps[bt][:].rearrange("p (j r) -> p j r", j=4), [PS(bt)], [("kvtok", t)])
        s1, s2 = rope_apply(bkr, None, 1.0)
        P.tt("dve", krT[0:64, t * T:(t + 1) * T], scr[s1][0:64, 0:512], scr[s2][0:64, 0:512], ALU.add,
             [SC(s1), SC(s2)], [("krT", t)])
        P.tt("pool", krT[64:128, t * T:(t + 1) * T], scr[s1][0:64, 0:512], scr[s2][0:64, 0:512], ALU.add,
             [SC(s1), SC(s2)], [("krT", t)])
        zgroup(2)
        q_nope(0)
        q_rope(0)
        q_lat(0)
        zgroup(3)
        for h in range(16):
            qi = h % 2
            bo, bm = 3 + qi, 5 + qi
            kbs = [(4 * t + kk, kk * 128, True) for kk in range(4)] + [(kb, 0, False) for kb in range(4 * t)]
            nk = len(kbs)
            hooks = {}
            pre = []
            if h >= 1:
                (hooks.__setitem__(min(nk - 1, 5), lambda h=h: o_proj(h - 1)) if HOOKS & 8 else pre.append(lambda h=h: o_proj(h - 1)))
            if h + 1 < 16:
                (hooks.__setitem__(0, lambda h=h: q_nope(h + 1)) if HOOKS & 1 else pre.append(lambda h=h: q_nope(h + 1)))
                (hooks.__setitem__(1, lambda h=h: q_rope(h + 1)) if HOOKS & 2 else pre.append(lambda h=h: q_rope(h + 1)))
                (hooks.__setitem__(2 if nk < 8 else 3, lambda h=h: q_lat(h + 1)) if HOOKS & 4 else pre.append(lambda h=h: q_lat(h + 1)))
            for f_ in pre:
                f_()
            pts = {}
            sbk = {}
            npair = nk // 2
            for p_ in range(npair + 1):
                if p_ < npair:
                    blk = [2 * p_, 2 * p_ + 1]
                    for n_ in blk:
                        kb, lo, diag = kbs[n_]
                        sbk[n_] = state["sb"] = (state["sb"] + 1) % 3
                        bs_ = sbk[n_]
                        pts[n_] = state["pt"] = (state["pt"] + 1) % NPT
                        P.mm(ps[bs_][:, lo:512], kvT[:, kb * 128:(kb + 1) * 128], qlat[qi][:, lo:512], True, False,
                             [("kvT", kb // 4), ("qlat", qi)], [PS(bs_)])
                    for i_, n_ in enumerate(blk):
                        kb, lo, diag = kbs[n_]
                        bs_ = sbk[n_]
                        r0 = 64 * i_
                        P.mm(ps[bs_][:, lo:512], krT[r0:r0 + 64, kb * 128:(kb + 1) * 128], qrope[qi][r0:r0 + 64, lo:512],
                             False, True, [("krT", kb // 4), ("qrope", qi)], [PS(bs_)])
                    for n_ in blk:
                        kb, lo, diag = kbs[n_]
                        bs_ = sbk[n_]
                        pi_ = pts[n_]
                        P.act(pT[pi_][:, lo:512], ps[bs_][:, lo:512], AF.Exp, [PS(bs_)], [("pT", pi_)])
                        if diag:
                            P.ms("pool", pT[pi_][64:128, lo:lo + 64], 0.0, [("pT", pi_)])
                if p_ >= 1:
                    for m_ in (2 * p_ - 2, 2 * p_ - 1):
                        kb, lo, diag = kbs[m_]
                        pi_ = pts[m_]
                        tk = kb // 4
                        P.mm(ps[bo][:, lo:512], kvtok[:, kb, :], pT[pi_][:, lo:512], m_ == 0, m_ == nk - 1,
                             [("kvtok", tk), ("pT", pi_)], [PS(bo)])
                        P.mm(ps[bm][:, lo:512], onesb[:], pT[pi_][:, lo:512], m_ == 0, m_ == nk - 1,
                             ["onesb", ("pT", pi_)], [PS(bm)])
                        if m_ in hooks:
                            hooks[m_]()
                        if m_ + 0.5 in hooks:
                            hooks[m_ + 0.5]()
            o_norm(h)
        o_proj(15)
        out_proj_and_norm("owout", 1, xb, "xb", t)
        P.dma("xout", out_d[t * T:(t + 1) * T, :].rearrange("(j p) d -> p j d", p=128), xb[:],
              r=[("xb", j) for j in range(4)])

    load_x(0)
    for t in range(NT):
        layer0(t)
        layer1(t)
    stats = P.emit()
    return nc, stats


def _host_inputs(inp, b, NT=8):
    S = NT * T
    f = lambda a: np.ascontiguousarray(a, dtype=np.float32)
    fm = lambda v, n: np.asarray(v, np.float32).reshape(n, 128).T
    pp = np.zeros((128, PP_N), np.float32)
    pp[:, PP_C:PP_C + 8] = fm(inp["c"][b], 8)
    for l in range(2):
        pp[:, PP_ADAB + 16 * l:PP_ADAB + 16 * l + 16] = fm(inp["ada_b"][l, :2048], 16)
    pp[:, PP_GNG:PP_GNG + 2] = fm(inp["gmlp_norm_g"][0], 2)
    pp[:, PP_GNB:PP_GNB + 2] = fm(inp["gmlp_norm_b"][0], 2)
    pp[:, PP_PB:PP_PB + 8] = fm(inp["pool_b"][0], 8)
    pp[:, PP_PS:PP_PS + 8] = fm(inp["pool_scale"][0], 8)
    pp[:, PP_QNG:PP_QNG + 2] = fm(inp["mla_q_norm_g"][0], 2)
    pp[:, PP_KVG:PP_KVG + 1] = fm(inp["mla_kv_norm_g"][0], 1)
    inv = (1.0 / (np.float32(10000.0) ** (np.arange(0, 64, 2, dtype=np.float32) / np.float32(64)))).astype(np.float32)
    pp[:, PP_INV] = np.tile(inv, 4)
    pp[:, PP_PH] = np.where(np.arange(128) < 64, np.float32(np.pi / 2), np.float32(0.0))
    rows = np.concatenate([np.asarray(inp["gmlp_norm_b"][0], np.float32).reshape(-1),
                           np.asarray(inp["gmlp_bs"][0], np.float32).reshape(-1)])[None, :]
    return {
        "x": f(inp["x"][b, :S]),
        "pos": np.ascontiguousarray(inp["positions"][b:b + 1, :S], dtype=np.int32),
        "pp": pp, "rows": f(rows),
        "ada_w": f(inp["ada_w"]), "ada_bg": f(inp["ada_b"][:, 2048:]),
        "ln_g": f(inp["ln_g"]), "ln_b": f(inp["ln_b"]),
        "e_w_in": f(inp["e_w_in"][0]), "gmlp_ws": f(inp["gmlp_ws"][0]), "pool_w": f(inp["pool_w"][0]),
        "e_w_out": f(inp["e_w_out"][0]), "o_w_in": f(inp["o_w_in"][0]),
        "w_uq": f(np.asarray(inp["mla_w_uq"][0]).reshape(256, 16 * 192)),
        "w_uk": f(np.asarray(inp["mla_w_uk"][0]).reshape(128, 16 * 128)),
        "w_uv": f(np.asarray(inp["mla_w_uv"][0]).reshape(128, 16 * 128)),
        "o_w_out": f(inp["o_w_out"][0]),
    }


_CACHE = {}


def kernel(**inputs):
    inp = {k: np.asarray(v) for k, v in inputs.items()}
    if "nc" not in _CACHE:
        _CACHE["nc"] = build_program(8)[0]
    nc = _CACHE["nc"]
    in_maps = [_host_inputs(inp, b) for b in range(NCORES)]
    res = run_bass_kernel_spmd(nc, in_maps, core_ids=list(range(NCORES)))
    return np.stack([np.asarray(r["out"], dtype=np.float32) for r in res.results], axis=0)
```
